# Optimizing a Trainium2 kernel written in Bass

```python
import jax, jax.numpy as jnp
from jax import lax
import numpy as np

D_MODEL = 2048
BATCH = 4
SEQ = 4096
DEPTH = 2

CHUNK = 64
QBLK = 128
N_A_LAYERS = DEPTH // 2
N_B_LAYERS = DEPTH - N_A_LAYERS
PLE_DIM = 256
A_HEADS = 16
Q_LORA = 512
KV_LORA = 512
QK_NOPE = 128
QK_ROPE = 64
V_HEAD = 128
QK_HEAD = QK_NOPE + QK_ROPE
ROPE_BASE = 10000.0
B_HEADS = 16
B_HEAD_DIM = D_MODEL // B_HEADS
N_PREV_CHUNKS = 8
REL_CLIP = 256
D_FF = ((8 * D_MODEL + 3 * 256 - 1) // (3 * 256)) * 256
EPS = 1e-6
NEG_INF = -1e30

kernel_name = 'yoco_mla_chunkband_hybrid'


def rmsnorm(x, g):
    xf = x.astype(jnp.float32)
    y = xf * lax.rsqrt(jnp.mean(xf * xf, axis=-1, keepdims=True) + EPS) * g.astype(jnp.float32)
    return y.astype(x.dtype)


def rope_tables(positions):
    inv_freq = ROPE_BASE ** (-jnp.arange(0, QK_ROPE, 2, dtype=jnp.float32) / QK_ROPE)
    ang = positions.astype(jnp.float32)[..., None] * inv_freq
    return jnp.cos(ang)[:, :, None, :], jnp.sin(ang)[:, :, None, :]


def apply_rope(x, cos, sin):
    xf = x.astype(jnp.float32)
    x1, x2 = xf[..., : QK_ROPE // 2], xf[..., QK_ROPE // 2:]
    return jnp.concatenate([x1 * cos - x2 * sin, x2 * cos + x1 * sin], axis=-1).astype(x.dtype)


def block_causal_attention(q, k, v):
    B, S, H, Dq = q.shape
    Dv = v.shape[-1]
    scale = 1.0 / np.sqrt(Dq)
    key_chunk = jnp.arange(S) // CHUNK

    def one_block(b):
        start = b * QBLK
        qb = lax.dynamic_slice_in_dim(q, start, QBLK, axis=1)
        q_chunk = (start + jnp.arange(QBLK)) // CHUNK
        s = jnp.einsum('bqhd,bkhd->bhqk', qb, k).astype(jnp.float32) * scale
        s = jnp.where(key_chunk[None, :] <= q_chunk[:, None], s, NEG_INF)
        pr = jax.nn.softmax(s, axis=-1).astype(v.dtype)
        return jnp.einsum('bhqk,bkhd->bqhd', pr, v)

    out = lax.map(one_block, jnp.arange(S // QBLK))
    return out.transpose(1, 0, 2, 3, 4).reshape(B, S, H, Dv)


def mla_mixer(h, positions, w_dq, g_q, w_uq, w_dkv, g_kv, w_ukv, g_qn, g_kn, w_o):
    B, S, _ = h.shape
    cq = rmsnorm(h @ w_dq, g_q)
    q = (cq @ w_uq).reshape(B, S, A_HEADS, QK_HEAD)
    ckv = h @ w_dkv
    c_kv = rmsnorm(ckv[..., :KV_LORA], g_kv)
    k_pe = ckv[..., KV_LORA:][:, :, None, :]
    kv = (c_kv @ w_ukv).reshape(B, S, A_HEADS, QK_NOPE + V_HEAD)
    k_nope, v = kv[..., :QK_NOPE], kv[..., QK_NOPE:]
    k = jnp.concatenate([k_nope, jnp.broadcast_to(k_pe, (B, S, A_HEADS, QK_ROPE))], axis=-1)
    q = rmsnorm(q, g_qn)
    k = rmsnorm(k, g_kn)
    cos, sin = rope_tables(positions)
    q = jnp.concatenate([q[..., :QK_NOPE], apply_rope(q[..., QK_NOPE:], cos, sin)], axis=-1)
    k = jnp.concatenate([k[..., :QK_NOPE], apply_rope(k[..., QK_NOPE:], cos, sin)], axis=-1)
    o = block_causal_attention(q, k, v)
    return o.reshape(B, S, A_HEADS * V_HEAD) @ w_o


def chunk_band_attention(q, k, v, positions, rel_table):
    B, S, H, Dh = q.shape
    band = (N_PREV_CHUNKS + 1) * CHUNK
    pad = N_PREV_CHUNKS * CHUNK
    scale = 1.0 / np.sqrt(Dh)
    kp = jnp.pad(k, ((0, 0), (pad, 0), (0, 0), (0, 0)))
    vp = jnp.pad(v, ((0, 0), (pad, 0), (0, 0), (0, 0)))
    pos_p = jnp.pad(positions, ((0, 0), (pad, 0)))
    valid_p = jnp.arange(S + pad) >= pad

    def one_chunk(c):
        start = c * CHUNK
        qc = lax.dynamic_slice_in_dim(q, start, CHUNK, axis=1)
        kb = lax.dynamic_slice_in_dim(kp, start, band, axis=1)
        vb = lax.dynamic_slice_in_dim(vp, start, band, axis=1)
        qpos = lax.dynamic_slice_in_dim(positions, start, CHUNK, axis=1)
        kpos = lax.dynamic_slice_in_dim(pos_p, start, band, axis=1)
        vld = lax.dynamic_slice_in_dim(valid_p, start, band, axis=0)
        rel = jnp.clip(qpos[:, :, None] - kpos[:, None, :], -REL_CLIP, REL_CLIP) + REL_CLIP
        bias = rel_table[:, rel].transpose(1, 0, 2, 3)
        s = jnp.einsum('bqhd,bkhd->bhqk', qc, kb).astype(jnp.float32) * scale + bias.astype(jnp.float32)
        s = jnp.where(vld[None, None, None, :], s, NEG_INF)
        pr = jax.nn.softmax(s, axis=-1).astype(v.dtype)
        return jnp.einsum('bhqk,bkhd->bqhd', pr, vb)

    out = lax.map(one_chunk, jnp.arange(S // CHUNK))
    return out.transpose(1, 0, 2, 3, 4).reshape(B, S, H, Dh)


def swiglu(h, w_gate, w_up, w_down):
    return (jax.nn.silu(h @ w_gate) * (h @ w_up)) @ w_down


def setup_inputs(seed: int = 0) -> dict:
    key = jax.random.key(seed)
    ks = iter(jax.random.split(key, 40))

    def w(shape, fan_in):
        return jax.random.normal(next(ks), shape, jnp.float32) * (fan_in ** -0.5)

    def gain(shape):
        return 1.0 + 0.1 * jax.random.normal(next(ks), shape, jnp.float32)

    NA, NB, L = N_A_LAYERS, N_B_LAYERS, DEPTH
    x = jax.random.normal(next(ks), (BATCH, SEQ, D_MODEL), jnp.float32)
    p = jax.random.normal(next(ks), (DEPTH, BATCH, SEQ, PLE_DIM), jnp.float32)
    offsets = jax.random.randint(next(ks), (BATCH, 1), 0, 1024, dtype=jnp.int32)
    positions = offsets + jnp.arange(SEQ, dtype=jnp.int32)[None, :]
    return {
        'x': x, 'p': p, 'positions': positions,
        'a_norm': gain((NA, D_MODEL)),
        'a_w_dq': w((NA, D_MODEL, Q_LORA), D_MODEL),
        'a_g_q': gain((NA, Q_LORA)),
        'a_w_uq': w((NA, Q_LORA, A_HEADS * QK_HEAD), Q_LORA),
        'a_w_dkv': w((NA, D_MODEL, KV_LORA + QK_ROPE), D_MODEL),
        'a_g_kv': gain((NA, KV_LORA)),
        'a_w_ukv': w((NA, KV_LORA, A_HEADS * (QK_NOPE + V_HEAD)), KV_LORA),
        'a_g_qn': gain((NA, QK_HEAD)),
        'a_g_kn': gain((NA, QK_HEAD)),
        'a_w_o': w((NA, A_HEADS * V_HEAD, D_MODEL), A_HEADS * V_HEAD),
        's_norm': gain((D_MODEL,)),
        's_w_k': w((D_MODEL, B_HEADS * B_HEAD_DIM), D_MODEL),
        's_w_v': w((D_MODEL, B_HEADS * B_HEAD_DIM), D_MODEL),
        's_g_kn': gain((B_HEAD_DIM,)),
        'b_norm': gain((NB, D_MODEL)),
        'b_w_q': w((NB, D_MODEL, B_HEADS * B_HEAD_DIM), D_MODEL),
        'b_g_qn': gain((NB, B_HEAD_DIM)),
        'b_rel_bias': 0.5 * jax.random.normal(next(ks), (NB, B_HEADS, 2 * REL_CLIP + 1), jnp.float32),
        'b_w_o': w((NB, B_HEADS * B_HEAD_DIM, D_MODEL), B_HEADS * B_HEAD_DIM),
        'f_norm': gain((L, D_MODEL)),
        'f_w_gate': w((L, D_MODEL, D_FF), D_MODEL),
        'f_w_up': w((L, D_MODEL, D_FF), D_MODEL),
        'f_w_down': w((L, D_FF, D_MODEL), D_FF),
        'e_norm': gain((L, D_MODEL)),
        'e_w_gate': w((L, D_MODEL, D_MODEL), D_MODEL),
        'e_w_proj': w((L, PLE_DIM, D_MODEL), PLE_DIM),
    }


def reference(x, p, positions,
              a_norm, a_w_dq, a_g_q, a_w_uq, a_w_dkv, a_g_kv, a_w_ukv, a_g_qn, a_g_kn, a_w_o,
              s_norm, s_w_k, s_w_v, s_g_kn,
              b_norm, b_w_q, b_g_qn, b_rel_bias, b_w_o,
              f_norm, f_w_gate, f_w_up, f_w_down,
              e_norm, e_w_gate, e_w_proj):
    B, S, _ = x.shape
    shared_k = None
    shared_v = None
    for i in range(DEPTH):
        if i < N_A_LAYERS:
            j = i
            h = rmsnorm(x, a_norm[j])
            x = x + mla_mixer(h, positions, a_w_dq[j], a_g_q[j], a_w_uq[j], a_w_dkv[j], a_g_kv[j],
                              a_w_ukv[j], a_g_qn[j], a_g_kn[j], a_w_o[j])
        else:
            j = i - N_A_LAYERS
            h = rmsnorm(x, b_norm[j])
            q = rmsnorm((h @ b_w_q[j]).reshape(B, S, B_HEADS, B_HEAD_DIM), b_g_qn[j])
            o = chunk_band_attention(q, shared_k, shared_v, positions, b_rel_bias[j])
            x = x + o.reshape(B, S, B_HEADS * B_HEAD_DIM) @ b_w_o[j]
        x = x + swiglu(rmsnorm(x, f_norm[i]), f_w_gate[i], f_w_up[i], f_w_down[i])
        gate = jax.nn.sigmoid(rmsnorm(x, e_norm[i]) @ e_w_gate[i])
        x = x + gate * (p[i] @ e_w_proj[i])
        if i == N_A_LAYERS - 1:
            hs = rmsnorm(x, s_norm)
            shared_k = rmsnorm((hs @ s_w_k).reshape(B, S, B_HEADS, B_HEAD_DIM), s_g_kn)
            shared_v = (hs @ s_w_v).reshape(B, S, B_HEADS, B_HEAD_DIM)
    return x
```

```python
import contextlib
import os
import numpy as np
import ml_dtypes
import concourse.bass as bass
import concourse.mybir as mybir
from concourse.bass_utils import run_bass_kernel_spmd

F32 = mybir.dt.float32
BF16 = mybir.dt.bfloat16
I32 = mybir.dt.int32
AF = mybir.ActivationFunctionType
ALU = mybir.AluOpType

D = 2048
SEQ = 4096
NB = 4
T = 2048
G = 1024
DFF = 5632
NFF = DFF // 128
EPS = 1e-6
NEG = -30000.0
POS2SUB = {0: [5, 0, 1, 6, 7, 2, 3, 4], 1: [1, 2, 3, 4, 5, 0, 6, 7]}
NL0 = 5
T1 = 4096
T2 = NL0 * 512
L0_GROUPS = [(0, 1), (2, 3), (4,)]
L1_GROUPS = [(1, 2), (3, 4)]
L0_PAST = {}
for _i in range(NL0):
    _u = set()
    for _r in (0, 1):
        _u |= {p for p in range(8) if POS2SUB[_r][p] < POS2SUB[_r][_i]}
    L0_PAST[_i] = sorted(_u)
L1_PREV = {}
for _i in range(1, NL0):
    _u = set()
    for _r in (0, 1):
        _ps = POS2SUB[_r][_i] - 1
        if _ps >= 0:
            _u.add(POS2SUB[_r].index(_ps))
    assert all(p < NL0 for p in _u)
    L1_PREV[_i] = sorted(_u)
FUSED = os.environ.get('KUNFUSED') is None
STOP = int(os.environ.get('KSTOP', '99'))

GV_SPEC = [("a_norm", 16), ("a_g_q", 4), ("a_g_kv", 4), ("f_norm0", 16), ("e_norm0", 16), ("s_norm", 16),
           ("b_norm", 16), ("f_norm1", 16), ("e_norm1", 16), ("gqn_n", 1), ("gkn_n", 1), ("s_gkn", 1), ("b_gqn", 1),
           ("gqr", 1), ("gqsw", 1), ("gkr", 1), ("gksw", 1), ("invf", 1), ("sgn", 1)]
GVI = {}
_c = 0
for _n, _k in GV_SPEC:
    GVI[_n] = (_c, _k)
    _c += _k
NGV = _c
MK0 = {}
_c = 0
for _s in range(NL0):
    for _i in range(len(L0_PAST[_s])):
        MK0[(_s, _i)] = _c
        _c += 1
MK1 = {}
for _s in range(1, NL0):
    for _i in range(len(L1_PREV[_s])):
        MK1[(_s, _i)] = _c
        _c += 1
NMK = _c


class R:
    __slots__ = ("name", "w", "rs", "excl")

    def __init__(self, name="", excl=False):
        self.name = name
        self.w = None
        self.rs = []
        self.excl = excl


class Prog:
    NDMASEM = 12

    def __init__(self, nc):
        self.nc = nc
        self.engs = {"pe": nc.tensor, "act": nc.scalar, "dve": nc.vector, "pool": nc.gpsimd, "sp": nc.sync}
        self.ops = []
        self.floor = 0
        self.esem = {k: nc.alloc_semaphore(name="es_" + k) for k in self.engs}
        self.ecount = {k: 0 for k in self.engs}
        self.dsem = {k: [nc.alloc_semaphore(name="ds_%s_%d" % (k, j)) for j in range(self.NDMASEM)] for k in ("sp", "pool")}
        self.dcount = {k: 0 for k in self.dsem}
        self.tok = []
        self.waited = {k: {} for k in self.engs}
        self.nwaits = 0

    def op(self, eng, fn, reads=(), writes=(), dma=False):
        i = len(self.ops)
        deps = set()
        fl = self.floor
        if any(r.excl for r in reads):
            writes = list(writes) + [r for r in reads if r.excl]
            reads = [r for r in reads if not r.excl]
        for r in reads:
            if r.w is not None and r.w >= fl:
                deps.add(r.w)
        for r in writes:
            if r.w is not None and r.w >= fl:
                deps.add(r.w)
            for x in r.rs:
                if x >= fl:
                    deps.add(x)
        for r in reads:
            r.rs.append(i)
        for r in writes:
            r.w = i
            r.rs = []
        self.ops.append((eng, fn, deps, dma))
        return i

    def dma(self, q, out, in_, reads=(), writes=()):
        return self.op(q, lambda e: e.dma_start(out=out, in_=in_), reads, writes, dma=True)

    def flush(self):
        ops = self.ops
        start = len(self.tok)
        n = len(ops)
        if start == n:
            return
        needed = {}
        last = {}
        for i in range(start, n):
            eng, fn, deps, dma = ops[i]
            if not dma:
                last[eng] = i
            for d in deps:
                p = ops[d]
                if (not p[3]) and (not dma) and p[0] == "pe" and eng == "pe":
                    continue
                needed[d] = True
        for ek, i in last.items():
            needed[i] = True
        for i in range(start, n):
            ek, fn, deps, dma = ops[i]
            e = self.engs[ek]
            for d in sorted(deps):
                p = ops[d]
                if (not p[3]) and (not dma) and p[0] == "pe" and ek == "pe":
                    continue
                key, sem, val = self.tok[d]
                if self.waited[ek].get(key, 0) >= val:
                    continue
                e.wait_ge(sem, val)
                self.nwaits += 1
                self.waited[ek][key] = val
            if dma:
                k = self.dcount[ek]
                self.dcount[ek] += 1
                j = k % self.NDMASEM
                sem = self.dsem[ek][j]
                key = ("d", ek, j)
                prev = 16 * (k // self.NDMASEM)
                if prev > 0 and self.waited[ek].get(key, 0) < prev:
                    e.wait_ge(sem, prev)
                    self.waited[ek][key] = prev
                ins = fn(e)
                ins.then_inc(sem, 16)
                self.tok.append((key, sem, prev + 16))
            else:
                ins = fn(e)
                if needed.get(i, False):
                    self.ecount[ek] += 1
                    ins.then_inc(self.esem[ek], 1)
                    self.tok.append((("e", ek), self.esem[ek], self.ecount[ek]))
                else:
                    self.tok.append(None)
        for i in range(start, n):
            if self.tok[i] is None:
                self.tok[i] = self.tok[last[ops[i][0]]]

    def wait_all_dma(self):
        for ek in self.dsem:
            e = self.engs[ek]
            k = self.dcount[ek]
            for j in range(self.NDMASEM):
                cnt = (k - j + self.NDMASEM - 1) // self.NDMASEM if k > j else 0
                key = ("d", ek, j)
                if cnt > 0 and self.waited[ek].get(key, 0) < 16 * cnt:
                    e.wait_ge(self.dsem[ek][j], 16 * cnt)
                    self.waited[ek][key] = 16 * cnt

    def barrier(self):
        self.flush()
        self.wait_all_dma()
        self.nc.all_engine_barrier()
        self.floor = len(self.ops)


class Ctx:
    def __init__(self, nc):
        self.nc = nc
        self.P = Prog(nc)
        self.banks = []
        for i in range(8):
            t = nc.alloc_psum_tensor("ps%d" % i, [128, 512], F32)
            self.banks.append((t, R("ps%d" % i, excl=True)))
        self.pair_i = 0
        self.stacks = []
        self.uid = 0
        self.ns = 2

    @property
    def gw(self):
        return 512 * self.ns

    @contextlib.contextmanager
    def scope(self):
        st = contextlib.ExitStack()
        self.stacks.append(st)
        try:
            yield
            self.P.barrier()
        finally:
            self.stacks.pop()
            st.close()

    def alloc(self, name, shape, dtype):
        self.uid += 1
        return self.stacks[-1].enter_context(self.nc.sbuf_tensor("%s_%d" % (name, self.uid), shape, dtype))

    def bankpair(self):
        i = self.pair_i
        self.pair_i = (i + 1) % 4
        return [self.banks[2 * i], self.banks[2 * i + 1]][:self.ns]

    def init_wring(self, nw=4, nslab=2):
        self.wring = [(self.alloc("wr", [128, 2048], BF16), R("wr%d" % i)) for i in range(nw)]
        self.wi = 0
        self.slabs = [(self.alloc("ws", [128, 8192], BF16), R("ws%d" % i)) for i in range(nslab)]
        self.si = 0

    def load_w(self, src, kc, m):
        t, r = self.wring[self.wi]
        self.wi = (self.wi + 1) % len(self.wring)
        v = t[:, 0:kc * m].rearrange("p (k m) -> p k m", k=kc)
        self.P.dma("pool", v, src, writes=[r])
        return v, r

    def load_slab(self, src, a, b):
        t, r = self.slabs[self.si]
        self.si = (self.si + 1) % len(self.slabs)
        v = t[:, 0:a * b].rearrange("p (a b) -> p a b", a=a)
        self.P.dma("pool", v, src, writes=[r])
        return v, r


def mm(cx, out, lhsT, rhs, start, stop, reads, wr):
    cx.P.op("pe", lambda e: e.matmul(out, lhsT=lhsT, rhs=rhs, start=start, stop=stop), reads=reads, writes=[wr])


def act(cx, out, in_, func, reads, writes, scale=None, bias=None):
    kw = {}
    if scale is not None:
        kw["scale"] = scale
    if bias is not None:
        kw["bias"] = bias
    cx.P.op("act", lambda e: e.activation(out=out, in_=in_, func=func, **kw), reads=reads, writes=writes)


def tt(cx, out, in0, in1, op, reads, writes, eng="dve"):
    cx.P.op(eng, lambda e: e.tensor_tensor(out=out, in0=in0, in1=in1, op=op), reads=reads, writes=writes)


def ts(cx, out, in0, s1, s2, op0, op1, reads, writes, eng="dve"):
    if op1 is None:
        cx.P.op(eng, lambda e: e.tensor_scalar(out=out, in0=in0, scalar1=s1, scalar2=None, op0=op0), reads=reads, writes=writes)
    else:
        cx.P.op(eng, lambda e: e.tensor_scalar(out=out, in0=in0, scalar1=s1, scalar2=s2, op0=op0, op1=op1), reads=reads, writes=writes)


def stt(cx, out, in0, scalar, in1, op0, op1, reads, writes):
    cx.P.op("dve", lambda e: e.scalar_tensor_tensor(out=out, in0=in0, scalar=scalar, in1=in1, op0=op0, op1=op1),
            reads=reads, writes=writes)


def cpy(cx, eng, out, in_, reads, writes):
    if eng == "act":
        act(cx, out, in_, AF.Copy, reads, writes)
    else:
        cx.P.op(eng, lambda e: e.tensor_copy(out=out, in_=in_), reads=reads, writes=writes)


def gcol(cx, name, i=0, p=128):
    c0, _ = GVI[name]
    return cx.gv[0:p, c0 + i:c0 + i + 1]


def rstd_from_banks(cx, banks, dim):
    for s, (bt, br) in enumerate(banks):
        act(cx, cx.lnt[:, s * 512:(s + 1) * 512], bt[:, :], AF.Ln, [br, cx.r_const], [cx.r_lnt], scale=1.0 / dim,
            bias=cx.epsc[:, 0:1])
    act(cx, cx.rstd[:, 0:cx.gw], cx.lnt[:, 0:cx.gw], AF.Exp, [cx.r_lnt], [cx.r_rstd], scale=-0.5)


def next_sq(cx):
    sq, rsq = cx.sqring[cx.sqi]
    cx.sqi = (cx.sqi + 1) % len(cx.sqring)
    return sq, rsq


def norm_fm(cx, srcs, gname, outs, dim):
    banks = cx.bankpair()
    n = len(srcs)
    for i, (s, rs) in enumerate(srcs):
        sq, rsq = next_sq(cx)
        act(cx, sq[:, 0:cx.gw], s[:, 0:cx.gw], AF.Square, [rs], [rsq])
        for sb, (bt, br) in enumerate(banks):
            mm(cx, bt[:, :], cx.ones[:, :], sq[:, sb * 512:(sb + 1) * 512], i == 0, i == n - 1, [rsq, cx.r_const], br)
    rstd_from_banks(cx, banks, dim)
    for i, ((s, rs), (o, ro)) in enumerate(zip(srcs, outs)):
        stt(cx, o[:, 0:cx.gw], s[:, 0:cx.gw], gcol(cx, gname, i), cx.rstd[:, 0:cx.gw], ALU.mult, ALU.mult,
            [rs, cx.r_rstd, cx.r_const], [ro])


def head_norm(cx, ba, extra, dim, gname, out, r_out_list):
    sq, rsq = next_sq(cx)
    bb = cx.bankpair()
    for s, (bt, br) in enumerate(ba):
        act(cx, sq[:, s * 512:(s + 1) * 512], bt[:, :], AF.Square, [br], [rsq])
    for s, (bt, br) in enumerate(bb):
        sl = slice(s * 512, (s + 1) * 512)
        mm(cx, bt[:, :], cx.ones[:, :], sq[:, sl], True, extra is None, [rsq, cx.r_const], br)
        if extra is not None:
            mm(cx, bt[:, :], cx.ones[0:64, :], extra[0][:, sl], False, True, [extra[1], cx.r_const], br)
    rstd_from_banks(cx, bb, dim)
    for s, (bt, br) in enumerate(ba):
        sl = slice(s * 512, (s + 1) * 512)
        stt(cx, out[:, sl], bt[:, :], gcol(cx, gname), cx.rstd[:, sl], ALU.mult, ALU.mult, [br, cx.r_rstd, cx.r_const],
            r_out_list)


def linear_fm(cx, wtile, kc_n, ins, n_oc, epilogue, m=128):
    for oc in range(n_oc):
        wt, rw = cx.load_w(wtile(oc), kc_n, m)
        banks = cx.bankpair()
        for kc in range(kc_n):
            a, ra = ins[kc]
            for s, (bt, br) in enumerate(banks):
                mm(cx, bt[0:m, :], wt[:, kc, :], a[:, s * 512:(s + 1) * 512], kc == 0, kc == kc_n - 1, [rw, ra], br)
        epilogue(oc, banks)


def setup_consts(cx, io, rope, nt=0):
    P = cx.P
    cx.r_const = R("const")
    cx.ones = cx.alloc("ones", [128, 128], BF16)
    cx.gv = cx.alloc("gv", [128, NGV], F32)
    cx.mk = cx.alloc("mk", [128, NMK], F32)
    cx.epsc = cx.alloc("epsc", [128, 2], F32)
    P.op("dve", lambda e: e.memset(cx.ones[:, :], 1.0), writes=[cx.r_const])
    P.op("dve", lambda e: e.memset(cx.epsc[:, 0:1], EPS), writes=[cx.r_const])
    P.op("dve", lambda e: e.memset(cx.epsc[:, 1:2], 0.0), writes=[cx.r_const])
    cx.zeroc = cx.epsc[:, 1:2]
    P.dma("sp", cx.gv[:, :], io["gv"], writes=[cx.r_const])
    P.dma("sp", cx.mk[:, :], io["mk"], writes=[cx.r_const])
    cx.sqring = [(cx.alloc("sq", [128, 1024], BF16), R("sq%d" % i)) for i in range(2)]
    cx.sqi = 0
    cx.rstd = cx.alloc("rstd", [128, 1024], F32)
    cx.r_rstd = R("rstd")
    cx.lnt = cx.alloc("lnt", [128, 1024], F32)
    cx.r_lnt = R("lnt")
    if rope is not None:
        cx.ropeC = cx.alloc("ropeC", [64, nt], F32)
        cx.ropeS = cx.alloc("ropeS", [64, nt], F32)
        cx.r_rope = R("rope")
        build_rope(cx, io, "g%sr" % rope, "g%ssw" % rope, nt)


def build_rope(cx, io, gr, gsw, nt):
    P = cx.P
    PI = float(np.pi)
    rr = cx.r_rope
    with cx.scope():
        posi = cx.alloc("posi", [64, nt], I32)
        ang = cx.alloc("ang", [64, nt], F32)
        kf = cx.alloc("kf", [64, nt], F32)
        ki = cx.alloc("ki", [64, nt], I32)
        r = cx.alloc("r", [64, nt], F32)
        m = cx.alloc("m", [64, nt], F32)
        P.dma("sp", posi[:, :], io["pos"][0:nt].partition_broadcast(64), writes=[rr])
        P.op("dve", lambda e: e.tensor_copy(out=ang[:, :], in_=posi[:, :]), reads=[rr], writes=[rr])
        ts(cx, ang[:, :], ang[:, :], gcol(cx, "invf", 0, 64), None, ALU.mult, None, [rr, cx.r_const], [rr])
        ts(cx, kf[:, :], ang[:, :], 1.0 / (2 * PI), None, ALU.mult, None, [rr], [rr])
        P.op("dve", lambda e: e.tensor_copy(out=ki[:, :], in_=kf[:, :]), reads=[rr], writes=[rr])
        P.op("dve", lambda e: e.tensor_copy(out=kf[:, :], in_=ki[:, :]), reads=[rr], writes=[rr])
        C1 = 6.28125
        C2 = float(2 * np.pi - 6.28125)
        stt(cx, r[:, :], kf[:, :], -C1, ang[:, :], ALU.mult, ALU.add, [rr], [rr])
        stt(cx, r[:, :], kf[:, :], -C2, r[:, :], ALU.mult, ALU.add, [rr], [rr])

        def wrap(x):
            ts(cx, m[:, :], x, PI, -2 * PI, ALU.is_gt, ALU.mult, [rr], [rr])
            tt(cx, x, x, m[:, :], ALU.add, [rr], [rr])
            ts(cx, m[:, :], x, -PI, 2 * PI, ALU.is_lt, ALU.mult, [rr], [rr])
            tt(cx, x, x, m[:, :], ALU.add, [rr], [rr])
            ts(cx, x, x, PI, -PI, ALU.min, ALU.max, [rr], [rr])

        wrap(r[:, :])
        act(cx, kf[:, :], r[:, :], AF.Sin, [rr], [rr])
        ts(cx, cx.ropeS[:, :], kf[:, :], gcol(cx, gsw, 0, 64), gcol(cx, "sgn", 0, 64), ALU.mult, ALU.mult,
           [rr, cx.r_const], [rr])
        ts(cx, r[:, :], r[:, :], PI / 2, None, ALU.add, None, [rr], [rr])
        wrap(r[:, :])
        act(cx, kf[:, :], r[:, :], AF.Sin, [rr], [rr])
        ts(cx, cx.ropeC[:, :], kf[:, :], gcol(cx, gr, 0, 64), None, ALU.mult, None, [rr, cx.r_const], [rr])


def alloc_stream(cx):
    cx.xT = cx.alloc("xT", [128, 16, G], F32)
    cx.rx = [R("x%d" % c) for c in range(16)]
    cx.hT = cx.alloc("hT", [128, 16, G], BF16)
    cx.rh = [R("h%d" % c) for c in range(16)]


def xchunks(cx):
    return [(cx.xT[:, c, :], cx.rx[c]) for c in range(16)]


def hchunks(cx):
    return [(cx.hT[:, c, :], cx.rh[c]) for c in range(16)]


def residual_epilogue(cx):
    def ep(oc, banks):
        for s, (bt, br) in enumerate(banks):
            xs = cx.xT[:, oc, s * 512:(s + 1) * 512]
            tt(cx, xs, xs, bt[:, :], ALU.add, [cx.rx[oc], br], [cx.rx[oc]])
    return ep


def slab_f32(cx, i):
    t, r = cx.slabs[i]
    return t[:, :].bitcast(F32).rearrange("p (a b) -> p a b", a=4), r


def phase1(cx, io):
    P = cx.P
    with cx.scope():
        setup_consts(cx, io, "k", T1)
        Ck, Sk = cx.ropeC, cx.ropeS
        cx.ns = 2
        cx.init_wring(nw=2)
        alloc_stream(cx)
        ckvf, r_ckvf = slab_f32(cx, 1)
        ckvT = cx.alloc("ckvT", [128, 4, G], BF16)
        r_ckvT = [R("ckvT%d" % i) for i in range(4)]
        krf = cx.alloc("krf", [64, G], F32)
        r_krf = R("krf")
        sqpe = cx.alloc("sqpe", [64, G], BF16)
        r_sqpe = R("sqpe")
        vt = [(cx.alloc("vt", [128, 512], BF16), R("vt%d" % i)) for i in range(2)]
        kno = [(cx.alloc("kno", [128, G], BF16), R("kno%d" % i)) for i in range(2)]
        kro = [(cx.alloc("kro", [64, G], BF16), R("kro%d" % i)) for i in range(2)]
        t1 = cx.lnt[0:64, :]
        r_t1 = cx.r_lnt
        r_out = R("p1out")
        if STOP == 0:
            return
        for g in range(T1 // G):
            tok = slice(g * G, (g + 1) * G)
            P.dma("sp", cx.xT[:, :, :], io["xT"][:, :, tok], writes=cx.rx)
            if STOP == 1:
                return
            norm_fm(cx, xchunks(cx), "a_norm", hchunks(cx), D)
            if STOP == 2:
                return

            def ep_ckv(oc, banks):
                for s, (bt, br) in enumerate(banks):
                    cpy(cx, "act" if s == 0 else "dve", ckvf[:, oc, s * 512:(s + 1) * 512], bt[:, :], [br], [r_ckvf])
            linear_fm(cx, lambda oc: io["WDKV"][oc], 16, hchunks(cx), 4, ep_ckv)
            if STOP == 3:
                return
            wt, rw = cx.load_w(io["WDKV"][4], 16, 128)
            bp = cx.bankpair()
            bs = cx.bankpair()
            for half, banks in ((0, bp), (1, bs)):
                for kc in range(16):
                    for s, (bt, br) in enumerate(banks):
                        mm(cx, bt[0:64, :], wt[:, kc, half * 64:(half + 1) * 64], cx.hT[:, kc, s * 512:(s + 1) * 512],
                           kc == 0, kc == 15, [rw, cx.rh[kc]], br)
            for s in range(2):
                sl = slice(s * 512, (s + 1) * 512)
                gsl = slice(g * G + s * 512, g * G + (s + 1) * 512)
                act(cx, sqpe[:, sl], bp[s][0][0:64, :], AF.Square, [bp[s][1]], [r_sqpe])
                tt(cx, krf[:, sl], bp[s][0][0:64, :], Ck[:, gsl], ALU.mult, [bp[s][1], cx.r_rope], [r_krf])
                tt(cx, t1[:, sl], bs[s][0][0:64, :], Sk[:, gsl], ALU.mult, [bs[s][1], cx.r_rope], [r_t1])
                tt(cx, krf[:, sl], krf[:, sl], t1[:, sl], ALU.add, [r_krf, r_t1], [r_krf])
            if STOP == 4:
                return
            norm_fm(cx, [(ckvf[:, i, :], r_ckvf) for i in range(4)], "a_g_kv",
                    [(ckvT[:, i, :], r_ckvT[i]) for i in range(4)], 512)
            if STOP == 5:
                return
            wv, r_wv = cx.load_slab(io["WV"], 4, 2048)
            assert r_wv is cx.slabs[0][1]
            cx.si = 0
            for tti in range(8):
                for cg in range(4):
                    bt, br = cx.banks[(tti * 4 + cg) % 8]
                    for kc in range(4):
                        mm(cx, bt[:, :], ckvT[:, kc, tti * 128:(tti + 1) * 128], wv[:, kc, cg * 512:(cg + 1) * 512],
                           kc == 0, kc == 3, [r_ckvT[kc], r_wv], br)
                    vtt, rvt = vt[cg % 2]
                    cpy(cx, "act" if cg % 2 == 0 else "dve", vtt[:, :], bt[:, :], [br], [rvt])
                    dst = io["VV"][cg * 4:(cg + 1) * 4, :, g * 8 + tti, :].rearrange("h p d -> p h d")
                    P.dma("sp", dst, vtt[:, :].rearrange("p (h d) -> p h d", h=4), reads=[rvt], writes=[r_out])
            if STOP == 6:
                return
            for h in range(16):
                wt, rw = cx.load_w(io["WUKVN"][h], 4, 128)
                ba = cx.bankpair()
                for kc in range(4):
                    for s, (bt, br) in enumerate(ba):
                        mm(cx, bt[:, :], wt[:, kc, :], ckvT[:, kc, s * 512:(s + 1) * 512], kc == 0, kc == 3, [rw, r_ckvT[kc]], br)
                ko, rko = kno[h % 2]
                kr_, rkr = kro[h % 2]
                head_norm(cx, ba, (sqpe, r_sqpe), 192, "gkn_n", ko, [rko])
                tt(cx, kr_[:, :], krf[:, :], cx.rstd[0:64, :], ALU.mult, [r_krf, cx.r_rstd], [rkr])
                P.dma("sp", io["KN"][h, :, tok], ko[:, :], reads=[rko], writes=[r_out])
                P.dma("sp", io["KR"][h, :, tok], kr_[:, :], reads=[rkr], writes=[r_out])


def ffn_block(cx, io, layer, fnorm):
    norm_fm(cx, xchunks(cx), fnorm, hchunks(cx), D)
    WG, WU, WD = io["WG%d" % layer], io["WU%d" % layer], io["WD%d" % layer]
    ns = cx.ns
    gb = [cx.banks[0], cx.banks[1]][:ns]
    ub = [cx.banks[2], cx.banks[3]][:ns]
    db = [[cx.banks[4], cx.banks[5]][:ns], [cx.banks[6], cx.banks[7]][:ns]]
    di = 0
    for sb in range(NFF // 4):
        aT, r_aT = cx.aT[sb % 2]
        for c in range(4):
            ff = sb * 4 + c
            wg, rwg = cx.load_w(WG[ff], 16, 128)
            wu, rwu = cx.load_w(WU[ff], 16, 128)
            sg, rsg = cx.sg[ff % 2]
            for kc in range(16):
                for s, (bt, br) in enumerate(gb):
                    mm(cx, bt[:, :], wg[:, kc, :], cx.hT[:, kc, s * 512:(s + 1) * 512], kc == 0, kc == 15, [rwg, cx.rh[kc]], br)
            for s, (bt, br) in enumerate(gb):
                act(cx, sg[:, s * 512:(s + 1) * 512], bt[:, :], AF.Silu, [br], [rsg])
            for kc in range(16):
                for s, (bt, br) in enumerate(ub):
                    mm(cx, bt[:, :], wu[:, kc, :], cx.hT[:, kc, s * 512:(s + 1) * 512], kc == 0, kc == 15, [rwu, cx.rh[kc]], br)
            for s, (bt, br) in enumerate(ub):
                sl = slice(s * 512, (s + 1) * 512)
                tt(cx, aT[:, c, sl], sg[:, sl], bt[:, :], ALU.mult, [rsg, br], [r_aT[c]])
        wd, rwd = cx.load_slab(WD[sb], 4, 2048)
        for oc in range(16):
            banks = db[di]
            di = 1 - di
            for c in range(4):
                for s, (bt, br) in enumerate(banks):
                    mm(cx, bt[:, :], wd[:, c, oc * 128:(oc + 1) * 128], aT[:, c, s * 512:(s + 1) * 512], c == 0, c == 3,
                       [rwd, r_aT[c]], br)
            for s, (bt, br) in enumerate(banks):
                xs = cx.xT[:, oc, s * 512:(s + 1) * 512]
                tt(cx, xs, xs, bt[:, :], ALU.add, [cx.rx[oc], br], [cx.rx[oc]])


def egate_block(cx, io, layer, enorm, tok):
    P = cx.P
    norm_fm(cx, xchunks(cx), enorm, hchunks(cx), D)
    EG, EP = io["EG%d" % layer], io["EP%d" % layer]
    P.dma("pool", cx.pT[:, :, 0:cx.gw], io["pT%d" % layer][:, :, tok], writes=[cx.r_pT])
    for oc in range(16):
        wg, rwg = cx.load_w(EG[oc], 16, 128)
        wp, rwp = cx.load_w(EP[oc], 2, 128)
        ba = cx.bankpair()
        bb = cx.bankpair()
        for kc in range(16):
            for s, (bt, br) in enumerate(ba):
                mm(cx, bt[:, :], wg[:, kc, :], cx.hT[:, kc, s * 512:(s + 1) * 512], kc == 0, kc == 15, [rwg, cx.rh[kc]], br)
        for kc in range(2):
            for s, (bt, br) in enumerate(bb):
                mm(cx, bt[:, :], wp[:, kc, :], cx.pT[:, kc, s * 512:(s + 1) * 512], kc == 0, kc == 1, [rwp, cx.r_pT], br)
        sg, rsg = cx.sg[oc % 2]
        for s in range(cx.ns):
            sl = slice(s * 512, (s + 1) * 512)
            act(cx, sg[:, sl], ba[s][0][:, :], AF.Sigmoid, [ba[s][1]], [rsg])
            tt(cx, sg[:, sl], sg[:, sl], bb[s][0][:, :], ALU.mult, [rsg, bb[s][1]], [rsg])
            xs = cx.xT[:, oc, sl]
            tt(cx, xs, xs, sg[:, sl], ALU.add, [cx.rx[oc], rsg], [cx.rx[oc]])


def alloc_ffn(cx):
    cx.aT = []
    for i in range(2):
        cx.aT.append((cx.alloc("aT", [128, 4, G], BF16), [R("aT%d_%d" % (i, c)) for c in range(4)]))
    cx.sg = [(cx.alloc("sg", [128, G], F32), R("sg%d" % i)) for i in range(2)]
    cx.pT = cx.alloc("pT", [128, 2, G], BF16)
    cx.r_pT = R("pT")


def attention(cx, h, sl, qn, rqn, qr, rqr, tiles, scale, bias_fn=None, st_banks=(0, 1), o_banks=(2, 3)):
    P = cx.P
    ob, orr = cx.banks[o_banks[0]]
    sb_, srr = cx.banks[o_banks[1]]
    n = len(tiles)
    nb = len(st_banks)
    la = nb - 1

    def qk(i):
        t = tiles[i]
        c0, c1 = t["c0"], t["c1"]
        stb, rst = cx.banks[st_banks[i % nb]]
        if qr is not None:
            mm(cx, stb[:, c0:c1], t["K"], qn[:, c0:c1], True, False, t["rK"] + [rqn], rst)
            mm(cx, stb[:, c0:c1], t["KR"], qr[:, c0:c1], False, True, t["rK"] + [rqr], rst)
        else:
            mm(cx, stb[:, c0:c1], t["K"], qn[:, c0:c1], True, True, t["rK"] + [rqn], rst)

    def rest(i):
        t = tiles[i]
        c0, c1 = t["c0"], t["c1"]
        stb, rst = cx.banks[st_banks[i % nb]]
        pt, rpt = cx.pring[cx.pi]
        cx.pi = (cx.pi + 1) % len(cx.pring)
        mcol = cx.mk[:, t["mask"]:t["mask"] + 1] if t["mask"] is not None else cx.zeroc
        if bias_fn is None:
            act(cx, pt[:, c0:c1], stb[:, c0:c1], AF.Exp, [rst, cx.r_const], [rpt], scale=scale, bias=mcol)
        else:
            tmp, rtmp = cx.tmpring[cx.ti]
            cx.ti = (cx.ti + 1) % len(cx.tmpring)
            b_ap, b_r = bias_fn(t)
            stt(cx, tmp[:, c0:c1], stb[:, c0:c1], scale, b_ap, ALU.mult, ALU.add, [rst, b_r], [rtmp])
            act(cx, pt[:, c0:c1], tmp[:, c0:c1], AF.Exp, [rtmp, cx.r_const], [rpt], scale=1.0, bias=mcol)
        if t["zero"] is not None:
            p0, p1, z0, z1 = t["zero"]
            cx.P.op("pool", lambda e, o=pt[p0:p1, z0:z1]: e.memset(o, 0.0), writes=[rpt])
        mm(cx, ob[:, c0:c1], t["V"], pt[:, c0:c1], i == 0, i == n - 1, t["rK"] + [rpt], orr)
        mm(cx, sb_[:, c0:c1], cx.ones[:, :], pt[:, c0:c1], i == 0, i == n - 1, [rpt, cx.r_const], srr)

    for i in range(min(la, n)):
        qk(i)
    for i in range(n):
        if i + la < n:
            qk(i + la)
        rest(i)
    act(cx, cx.rinv[:, :], sb_[:, :], AF.Ln, [srr], [cx.r_rinv])
    act(cx, cx.rinv[:, :], cx.rinv[:, :], AF.Exp, [cx.r_rinv], [cx.r_rinv], scale=-1.0)
    tt(cx, cx.hT[:, h, sl * 512:(sl + 1) * 512], ob[:, :], cx.rinv[:, :], ALU.mult, [orr, cx.r_rinv], [cx.rh[h]])


def alloc_attn(cx):
    cx.pring = [(cx.alloc("pt", [128, 512], BF16), R("pt%d" % i)) for i in range(4)]
    cx.pi = 0
    cx.rinv = cx.alloc("rinv", [128, 512], F32)
    cx.r_rinv = R("rinv")


def phase2(cx, io):
    P = cx.P
    with cx.scope():
        setup_consts(cx, io, "q", T2)
        Cq, Sq = cx.ropeC, cx.ropeS
        cx.ns = 2
        cx.init_wring()
        alloc_stream(cx)
        r_out = R("p2out")
        sc0 = float(1.0 / np.sqrt(192.0))
        t0s, r0s = cx.slabs[0]
        t1s, r1s = cx.slabs[1]
        KA = t0s[:, 0:4096]
        VA = t0s[:, 4096:8192].rearrange("p (t d) -> p t d", t=32)
        KRA = t1s[0:64, 0:4096]
        for grp in L0_GROUPS:
            cx.ns = len(grp)
            gw = cx.gw
            tok = slice(grp[0] * 512, grp[0] * 512 + gw)
            P.dma("sp", cx.xT[:, :, 0:gw], io["xT"][:, :, tok], writes=cx.rx)
            with cx.scope():
                alloc_attn(cx)
                cqf, r_cqf = slab_f32(cx, 1)
                cqT = cx.alloc("cqT", [128, 4, G], BF16)
                r_cqT = [R("cqT%d" % i) for i in range(4)]
                qn = [(cx.alloc("qn", [128, 512], BF16), R("qn%d" % i)) for i in range(2)]
                qr = [(cx.alloc("qr", [64, 512], BF16), R("qr%d" % i)) for i in range(2)]
                sqn = cx.alloc("sqn", [128, 512], BF16)
                sqr = cx.alloc("sqr", [64, 512], BF16)
                r_sqq = R("sqq")
                rq = cx.rstd[:, 0:512]
                lq = cx.lnt[:, 0:512]
                r_rq, r_lq = cx.r_rstd, cx.r_lnt
                t1 = cx.alloc("t1", [64, 512], F32)
                t2 = cx.alloc("t2", [64, 512], F32)
                r_t1 = R("t1")
                r_t2 = R("t2")
                norm_fm(cx, xchunks(cx), "a_norm", hchunks(cx), D)

                def ep_cq(oc, banks):
                    for s_, (bt, br) in enumerate(banks):
                        cpy(cx, "act" if s_ == 0 else "dve", cqf[:, oc, s_ * 512:(s_ + 1) * 512], bt[:, :], [br], [r_cqf])
                linear_fm(cx, lambda oc: io["WDQ"][oc], 16, hchunks(cx), 4, ep_cq)
                norm_fm(cx, [(cqf[:, i, :], r_cqf) for i in range(4)], "a_g_q",
                        [(cqT[:, i, :], r_cqT[i]) for i in range(4)], 512)
                items = [(h, sl) for h in range(16) for sl in range(cx.ns)]
                wqs = {}

                def qpath(k):
                    h, sl = items[k]
                    if h not in wqs:
                        wqs[h] = cx.load_w(io["WUQ"][h], 4, 256)
                    wq, rwq = wqs[h]
                    slot = grp[sl]
                    csl = slice(sl * 512, (sl + 1) * 512)
                    gsl = slice(slot * 512, (slot + 1) * 512)
                    bA, rA = cx.banks[4]
                    bB, rB = cx.banks[5]
                    bC, rC = cx.banks[6]
                    bD, rD = cx.banks[7]
                    for kc in range(4):
                        mm(cx, bA[:, :], wq[:, kc, 0:128], cqT[:, kc, csl], kc == 0, kc == 3, [rwq, r_cqT[kc]], rA)
                    for kc in range(4):
                        mm(cx, bB[0:64, :], wq[:, kc, 128:192], cqT[:, kc, csl], kc == 0, kc == 3, [rwq, r_cqT[kc]], rB)
                    for kc in range(4):
                        mm(cx, bC[0:64, :], wq[:, kc, 192:256], cqT[:, kc, csl], kc == 0, kc == 3, [rwq, r_cqT[kc]], rC)
                    act(cx, sqn[:, :], bA[:, :], AF.Square, [rA], [r_sqq])
                    act(cx, sqr[:, :], bB[0:64, :], AF.Square, [rB], [r_sqq])
                    mm(cx, bD[:, :], cx.ones[:, :], sqn[:, :], True, False, [r_sqq, cx.r_const], rD)
                    mm(cx, bD[:, :], cx.ones[0:64, :], sqr[:, :], False, True, [r_sqq, cx.r_const], rD)
                    act(cx, lq, bD[:, :], AF.Ln, [rD, cx.r_const], [r_lq], scale=1.0 / 192, bias=cx.epsc[:, 0:1])
                    act(cx, rq, lq, AF.Exp, [r_lq], [r_rq], scale=-0.5)
                    qnt, rqn = qn[k % 2]
                    qrt, rqr = qr[k % 2]
                    stt(cx, qnt[:, :], bA[:, :], gcol(cx, "gqn_n"), rq, ALU.mult, ALU.mult, [rA, r_rq, cx.r_const], [rqn])
                    tt(cx, t1[:, :], bB[0:64, :], Cq[:, gsl], ALU.mult, [rB, cx.r_rope], [r_t1])
                    tt(cx, t2[:, :], bC[0:64, :], Sq[:, gsl], ALU.mult, [rC, cx.r_rope], [r_t2])
                    tt(cx, t1[:, :], t1[:, :], t2[:, :], ALU.add, [r_t1, r_t2], [r_t1])
                    tt(cx, qrt[:, :], t1[:, :], rq[0:64, :], ALU.mult, [r_t1, r_rq], [rqr])

                qpath(0)
                for k, (h, sl) in enumerate(items):
                    if sl == 0:
                        P.dma("sp", KA, io["KN"][h], writes=[r0s])
                        P.dma("sp", VA, io["VV"][h], writes=[r0s])
                        P.dma("sp", KRA, io["KR"][h], writes=[r1s])
                    if k + 1 < len(items):
                        qpath(k + 1)
                    slot = grp[sl]
                    qnt, rqn = qn[k % 2]
                    qrt, rqr = qr[k % 2]
                    tiles = []
                    for t in range(4):
                        ks = slice(slot * 512 + t * 128, slot * 512 + (t + 1) * 128)
                        tiles.append(dict(K=KA[:, ks], KR=KRA[:, ks], V=VA[:, slot * 4 + t, :], rK=[r0s, r1s],
                                          c0=t * 128, c1=512, mask=None, zero=(64, 128, t * 128, t * 128 + 64)))
                    for i, pp in enumerate(L0_PAST[slot]):
                        for t in range(4):
                            ks = slice(pp * 512 + t * 128, pp * 512 + (t + 1) * 128)
                            tiles.append(dict(K=KA[:, ks], KR=KRA[:, ks], V=VA[:, pp * 4 + t, :], rK=[r0s, r1s],
                                              c0=0, c1=512, mask=MK0[(slot, i)], zero=None))
                    attention(cx, h, sl, qnt, rqn, qrt, rqr, tiles, sc0)
            with cx.scope():
                alloc_ffn(cx)
                linear_fm(cx, lambda oc: io["WO0"][oc], 16, hchunks(cx), 16, residual_epilogue(cx))
                ffn_block(cx, io, 0, "f_norm0")
                egate_block(cx, io, 0, "e_norm0", tok)
                P.dma("sp", io["xres"][:, :, tok], cx.xT[:, :, 0:gw], reads=cx.rx, writes=[r_out])
            with cx.scope():
                sko = [(cx.alloc("sko", [128, G], BF16), R("sko%d" % i)) for i in range(2)]
                svt = [(cx.alloc("svt", [128, 512], BF16), R("svt%d" % i)) for i in range(2)]
                norm_fm(cx, xchunks(cx), "s_norm", hchunks(cx), D)
                for h in range(16):
                    wt, rw = cx.load_w(io["SWK"][h], 16, 128)
                    ba = cx.bankpair()
                    for kc in range(16):
                        for s_, (bt, br) in enumerate(ba):
                            mm(cx, bt[:, :], wt[:, kc, :], cx.hT[:, kc, s_ * 512:(s_ + 1) * 512], kc == 0, kc == 15,
                               [rw, cx.rh[kc]], br)
                    ko, rko = sko[h % 2]
                    head_norm(cx, ba, None, 128, "s_gkn", ko, [rko])
                    P.dma("sp", io["SK"][h, :, tok], ko[:, 0:gw], reads=[rko], writes=[r_out])
                for cg in range(4):
                    wsl, rws = cx.load_slab(io["SWV"][cg], 16, 512)
                    for tti in range(4 * cx.ns):
                        bt, br = cx.banks[(cg * 8 + tti) % 8]
                        for kc in range(16):
                            mm(cx, bt[:, :], cx.hT[:, kc, tti * 128:(tti + 1) * 128], wsl[:, kc, :], kc == 0, kc == 15,
                               [cx.rh[kc], rws], br)
                        v_, rv_ = svt[tti % 2]
                        cpy(cx, "act" if tti % 2 == 0 else "dve", v_[:, :], bt[:, :], [br], [rv_])
                        dst = io["SV"][cg * 4:(cg + 1) * 4, :, grp[0] * 4 + tti, :].rearrange("h p d -> p h d")
                        P.dma("sp", dst, v_[:, :].rearrange("p (h d) -> p h d", h=4), reads=[rv_], writes=[r_out])
        cx.ns = 2


def phase3(cx, io):
    P = cx.P
    with cx.scope():
        setup_consts(cx, io, None)
        cx.ns = 2
        cx.init_wring()
        alloc_stream(cx)
        r_out = R("p3out")
        r_ext = R("ext")
        ext = io["ext"]
        tab = io["relb"]
        Z = io["Z"]
        with cx.scope():
            tab_sb = cx.alloc("tab_sb", [16, 513], F32)
            ext_sb = cx.alloc("ext_sb", [16, 1535], F32)
            P.dma("sp", tab_sb[:, :], tab, writes=[r_ext])
            P.op("dve", lambda e: e.memset(ext_sb[:, :], 0.0), writes=[r_ext])
            ts(cx, ext_sb[:, 0:255], ext_sb[:, 0:255], tab_sb[:, 0:1], None, ALU.add, None, [r_ext], [r_ext])
            ts(cx, ext_sb[:, 768:1535], ext_sb[:, 768:1535], tab_sb[:, 512:513], None, ALU.add, None, [r_ext], [r_ext])
            P.op("dve", lambda e: e.tensor_copy(out=ext_sb[:, 255:768], in_=tab_sb[:, :]), reads=[r_ext], writes=[r_ext])
            P.dma("sp", ext, ext_sb[:, :], reads=[r_ext], writes=[r_ext])
            srcb = bass.AP(tensor=ext.tensor, offset=0, ap=[[1535, 16], [0, 128], [1, 1535]])
            P.dma("sp", Z[:, :, 0:1535], srcb, reads=[r_ext], writes=[r_ext])
        sc1 = float(1.0 / np.sqrt(128.0))
        t0s, r0s = cx.slabs[0]
        SKA = t0s[:, 0:T2]
        SVA = t0s[:, 4096:4096 + T2].rearrange("p (t d) -> p t d", t=4 * NL0)
        for gi, grp in enumerate(L1_GROUPS):
            tok = slice(grp[0] * 512, grp[0] * 512 + G)
            otok = slice(gi * G, (gi + 1) * G)
            P.dma("sp", cx.xT[:, :, :], io["xres"][:, :, tok], writes=cx.rx)
            with cx.scope():
                alloc_attn(cx)
                cx.tmpring = [(cx.alloc("tmp", [128, 512], F32), R("tmp%d" % i)) for i in range(2)]
                cx.ti = 0
                qT = cx.alloc("qT", [128, 16, G], BF16)
                r_qT = [R("qT%d" % i) for i in range(16)]
                TB = [(cx.alloc("TB", [128, 1024], F32), R("TB%d" % i)) for i in range(2)]
                norm_fm(cx, xchunks(cx), "b_norm", hchunks(cx), D)
                for h in range(16):
                    wt, rw = cx.load_w(io["BWQ"][h], 16, 128)
                    ba = cx.bankpair()
                    for kc in range(16):
                        for s_, (bt, br) in enumerate(ba):
                            mm(cx, bt[:, :], wt[:, kc, :], cx.hT[:, kc, s_ * 512:(s_ + 1) * 512], kc == 0, kc == 15,
                               [rw, cx.rh[kc]], br)
                    head_norm(cx, ba, None, 128, "b_gqn", qT[:, h, :], [r_qT[h]])
                for h in range(16):
                    P.dma("sp", SKA, io["SK"][h], writes=[r0s])
                    P.dma("sp", SVA, io["SV"][h], writes=[r0s])
                    tb, rtb = TB[h % 2]
                    src = bass.AP(tensor=Z.tensor, offset=h * 128 * 1536 + 127, ap=[[1535, 128], [1, 1024]])
                    P.dma("sp", tb[:, :], src, reads=[r_ext], writes=[rtb])
                    for sl in range(2):
                        slot = grp[sl]
                        csl = slice(sl * 512, (sl + 1) * 512)
                        tiles = []
                        for t in range(4):
                            ks = slice(slot * 512 + t * 128, slot * 512 + (t + 1) * 128)
                            tiles.append(dict(K=SKA[:, ks], V=SVA[:, slot * 4 + t, :], rK=[r0s], rel=t,
                                              c0=t * 128, c1=512, mask=None, zero=(64, 128, t * 128, t * 128 + 64)))
                        for i, pp in enumerate(L1_PREV[slot]):
                            for t in range(4):
                                ks = slice(pp * 512 + t * 128, pp * 512 + (t + 1) * 128)
                                tiles.append(dict(K=SKA[:, ks], V=SVA[:, pp * 4 + t, :], rK=[r0s], rel=t - 4,
                                                  c0=0, c1=(t + 1) * 128, mask=MK1[(slot, i)],
                                                  zero=(0, 64, t * 128 + 64, t * 128 + 128)))

                        def bias_fn(t, tb=tb, rtb=rtb):
                            j0 = 384 - t["rel"] * 128
                            return tb[:, j0 + t["c0"]:j0 + t["c1"]], rtb
                        attention(cx, h, sl, qT[:, h, csl], r_qT[h], None, None, tiles, sc1, bias_fn,
                                  st_banks=(0, 1, 4, 5), o_banks=((2, 3) if (2 * h + sl) % 2 == 0 else (6, 7)))
            with cx.scope():
                alloc_ffn(cx)
                linear_fm(cx, lambda oc: io["WO1"][oc], 16, hchunks(cx), 16, residual_epilogue(cx))
                ffn_block(cx, io, 1, "f_norm1")
                egate_block(cx, io, 1, "e_norm1", tok)
                P.dma("sp", io["outT"][:, :, otok], cx.xT[:, :, :], reads=cx.rx, writes=[r_out])


def tiled(w, kc, oc, m=128):
    return np.ascontiguousarray(w.reshape(kc, 128, oc, m).transpose(2, 1, 0, 3))


DRAM_SPECS = {
    "xT": ([128, 16, T1], F32), "pos": ([T1], I32), "gv": ([128, NGV], F32), "mk": ([128, NMK], F32),
    "WDKV": ([5, 128, 16, 128], F32), "WUKVN": ([16, 128, 4, 128], F32), "WV": ([128, 4, 2048], F32),
    "KN": ([16, 128, T1], BF16), "KR": ([16, 64, T1], BF16), "VV": ([16, 128, 32, 128], BF16),
    "WDQ": ([4, 128, 16, 128], F32), "WUQ": ([16, 128, 4, 256], F32), "WO0": ([16, 128, 16, 128], F32),
    "WG0": ([NFF, 128, 16, 128], F32), "WU0": ([NFF, 128, 16, 128], F32), "WD0": ([NFF // 4, 128, 4, 2048], F32),
    "EG0": ([16, 128, 16, 128], F32), "EP0": ([16, 128, 2, 128], F32), "pT0": ([128, 2, T2], F32),
    "SWK": ([16, 128, 16, 128], F32), "SWV": ([4, 128, 16, 512], F32),
    "xres": ([128, 16, T2], F32), "SK": ([16, 128, T2], BF16), "SV": ([16, 128, 4 * NL0, 128], BF16),
    "BWQ": ([16, 128, 16, 128], F32), "WO1": ([16, 128, 16, 128], F32),
    "WG1": ([NFF, 128, 16, 128], F32), "WU1": ([NFF, 128, 16, 128], F32), "WD1": ([NFF // 4, 128, 4, 2048], F32),
    "EG1": ([16, 128, 16, 128], F32), "EP1": ([16, 128, 2, 128], F32), "pT1": ([128, 2, T2], F32),
    "relb": ([16, 513], F32), "ext": ([16, 1535], F32), "Z": ([16, 128, 1536], F32), "outT": ([128, 16, T], F32),
}
PHASE_IN = {
    1: ["xT", "pos", "gv", "mk", "WDKV", "WUKVN", "WV"],
    2: ["xT", "pos", "gv", "mk", "KN", "KR", "VV", "WDQ", "WUQ", "WO0", "WG0", "WU0", "WD0",
        "EG0", "EP0", "pT0", "SWK", "SWV"],
    3: ["xres", "gv", "mk", "SK", "SV", "BWQ", "WO1", "WG1", "WU1", "WD1", "EG1", "EP1", "pT1", "relb"],
}
PHASE_OUT = {1: ["KN", "KR", "VV"], 2: ["xres", "SK", "SV"], 3: ["outT"]}
PHASE_INT = {1: [], 2: [], 3: ["ext", "Z"]}


def build_program(phases):
    fused = len(phases) > 1
    nc = bass.Bass("TRN2", target_bir_lowering=False)
    io = {}
    internal = set()
    if fused:
        for ph in phases:
            internal.update(PHASE_OUT[ph])
        internal.discard("outT")
    for ph in phases:
        for n in PHASE_IN[ph] + PHASE_OUT[ph] + PHASE_INT[ph]:
            if n in io:
                continue
            shape, dt = DRAM_SPECS[n]
            if n in internal or n in PHASE_INT[ph]:
                kind = "Internal"
            elif n in PHASE_OUT[ph]:
                kind = "ExternalOutput"
            else:
                kind = "ExternalInput"
            io[n] = nc.dram_tensor(n, shape, dt, kind=kind).ap()
    cx = Ctx(nc)
    for ph in phases:
        {1: phase1, 2: phase2, 3: phase3}[ph](cx, io)
    cx.P.flush()
    cx.P.wait_all_dma()
    return nc


_PROGS = {}
LAST_RES = None


def get_prog(phases):
    key = tuple(phases)
    if key not in _PROGS:
        _PROGS[key] = build_program(phases)
    return _PROGS[key]


def core_tokens(r, npos=8):
    return np.concatenate([np.arange(s * 512, (s + 1) * 512) for s in POS2SUB[r][:npos]])


def fm(a):
    t, f = a.shape
    return np.ascontiguousarray(a.T.reshape(f // 128, 128, t).transpose(1, 0, 2))


def prep_weights(inp):
    f = np.float32
    W = {}
    a_w_dkv = inp["a_w_dkv"][0]
    pe = a_w_dkv[:, 512:576]
    pe_sw = np.concatenate([pe[:, 32:64], pe[:, 0:32]], axis=1)
    W["WDKV"] = tiled(np.concatenate([a_w_dkv[:, :512], pe, pe_sw], axis=1), 16, 5)
    ukv = inp["a_w_ukv"][0].reshape(512, 16, 256)
    W["WUKVN"] = tiled(np.ascontiguousarray(ukv[:, :, :128]).reshape(512, 2048), 4, 16)
    W["WV"] = np.ascontiguousarray(ukv[:, :, 128:].reshape(4, 128, 2048).transpose(1, 0, 2))
    W["WDQ"] = tiled(inp["a_w_dq"][0], 16, 4)
    uq = inp["a_w_uq"][0].reshape(512, 16, 192)
    uq2 = np.concatenate([uq[:, :, :128], uq[:, :, 128:192], uq[:, :, 160:192], uq[:, :, 128:160]], axis=2)
    W["WUQ"] = tiled(np.ascontiguousarray(uq2).reshape(512, 16 * 256), 4, 16, 256)
    W["WO0"] = tiled(inp["a_w_o"][0], 16, 16)
    W["WO1"] = tiled(inp["b_w_o"][0], 16, 16)
    W["BWQ"] = tiled(inp["b_w_q"][0], 16, 16)
    W["SWK"] = tiled(inp["s_w_k"], 16, 16)
    W["SWV"] = np.ascontiguousarray(inp["s_w_v"].reshape(16, 128, 4, 512).transpose(2, 1, 0, 3))
    for l in range(2):
        W["WG%d" % l] = tiled(inp["f_w_gate"][l], 16, NFF)
        W["WU%d" % l] = tiled(inp["f_w_up"][l], 16, NFF)
        W["WD%d" % l] = np.ascontiguousarray(inp["f_w_down"][l].reshape(NFF // 4, 4, 128, 2048).transpose(0, 2, 1, 3))
        W["EG%d" % l] = tiled(inp["e_w_gate"][l], 16, 16)
        W["EP%d" % l] = tiled(inp["e_w_proj"][l], 2, 16)
    W["relb"] = np.ascontiguousarray(inp["b_rel_bias"][0])
    gv = np.zeros((128, NGV), f)

    def put(name, vec):
        c0, k = GVI[name]
        v = np.asarray(vec, f)
        if v.size == 128 * k:
            gv[:, c0:c0 + k] = v.reshape(k, 128).T
        else:
            gv[:v.size, c0] = v
    put("a_norm", inp["a_norm"][0]); put("a_g_q", inp["a_g_q"][0]); put("a_g_kv", inp["a_g_kv"][0])
    put("f_norm0", inp["f_norm"][0]); put("e_norm0", inp["e_norm"][0]); put("s_norm", inp["s_norm"])
    put("b_norm", inp["b_norm"][0]); put("f_norm1", inp["f_norm"][1]); put("e_norm1", inp["e_norm"][1])
    gq, gk = inp["a_g_qn"][0], inp["a_g_kn"][0]
    put("gqn_n", gq[:128]); put("gkn_n", gk[:128]); put("s_gkn", inp["s_g_kn"]); put("b_gqn", inp["b_g_qn"][0])
    sw = (np.arange(64) + 32) % 64
    put("gqr", gq[128:192]); put("gqsw", gq[128:192][sw]); put("gkr", gk[128:192]); put("gksw", gk[128:192][sw])
    invf = (np.float32(10000.0) ** (-np.arange(0, 64, 2, dtype=f) / f(64))).astype(f)
    put("invf", invf[np.arange(64) % 32])
    put("sgn", np.where(np.arange(64) < 32, -1.0, 1.0))
    W["gv"] = gv
    return W


def core_masks(r):
    mk = np.zeros((128, NMK), np.float32)
    p2s = POS2SUB[r]
    for s in range(NL0):
        for i, pp in enumerate(L0_PAST[s]):
            if not (p2s[pp] < p2s[s]):
                mk[:, MK0[(s, i)]] = NEG
    for s in range(1, NL0):
        for i, pp in enumerate(L1_PREV[s]):
            if not (p2s[pp] == p2s[s] - 1):
                mk[:, MK1[(s, i)]] = NEG
    return mk


def run(phases, in_maps):
    nc = get_prog(phases)
    names = set()
    produced = set()
    for ph in phases:
        names.update(PHASE_IN[ph])
        produced.update(PHASE_OUT[ph])
    maps = [{k: v for k, v in m.items() if k in names and k not in produced} for m in in_maps]
    res = run_bass_kernel_spmd(nc, maps, core_ids=list(range(8)))
    global LAST_RES
    LAST_RES = res
    return res.results


def make_maps(inp):
    W = prep_weights(inp)
    x, p, positions = inp["x"], inp["p"], inp["positions"]
    maps = []
    for c in range(8):
        b, r = c // 2, c % 2
        tk = core_tokens(r)
        m = dict(W)
        m["xT"] = fm(x[b][tk])
        m["pos"] = np.ascontiguousarray(positions[b][tk]).astype(np.int32)
        m["pT0"] = fm(p[0, b][tk[:T2]])
        m["pT1"] = fm(p[1, b][tk[:T2]])
        m["mk"] = core_masks(r)
        maps.append(m)
    return maps


def kernel(**inp):
    inp = {k: np.asarray(v) for k, v in inp.items()}
    maps = make_maps(inp)
    if FUSED:
        r3 = run([1, 2, 3], maps)
    else:
        r1 = run([1], maps)
        for c in range(8):
            maps[c].update({n: r1[c][n] for n in PHASE_OUT[1]})
        r2 = run([2], maps)
        for c in range(8):
            maps[c].update({n: r2[c][n] for n in PHASE_OUT[2]})
        r3 = run([3], maps)
    out = np.zeros((NB, SEQ, D), np.float32)
    for c in range(8):
        b, r = c // 2, c % 2
        o = np.asarray(r3[c]["outT"])
        out[b][core_tokens(r, NL0)[512:]] = o.transpose(2, 1, 0).reshape(T, D)
    return out
```

```python
import contextlib
import os
import numpy as np
import ml_dtypes
import concourse.bass as bass
import concourse.mybir as mybir
from concourse.bass_utils import run_bass_kernel_spmd

F32 = mybir.dt.float32
BF16 = mybir.dt.bfloat16
I32 = mybir.dt.int32
AF = mybir.ActivationFunctionType
ALU = mybir.AluOpType

D = 2048
SEQ = 4096
NB = 4
T = 2048
G = 1024
DFF = 5632
NFF = DFF // 128
EPS = 1e-6
NEG = -30000.0
POS2SUB = {0: [5, 0, 1, 6, 7, 2, 3, 4], 1: [1, 2, 3, 4, 5, 0, 6, 7]}
NL0 = 5
T1 = 4096
T2 = NL0 * 512
L0_GROUPS = [(0, 1), (2, 3), (4,)]
L1_GROUPS = [(1, 2), (3, 4)]
L0_PAST = {}
for _i in range(NL0):
    _u = set()
    for _r in (0, 1):
        _u |= {p for p in range(8) if POS2SUB[_r][p] < POS2SUB[_r][_i]}
    L0_PAST[_i] = sorted(_u)
L1_PREV = {}
for _i in range(1, NL0):
    _u = set()
    for _r in (0, 1):
        _ps = POS2SUB[_r][_i] - 1
        if _ps >= 0:
            _u.add(POS2SUB[_r].index(_ps))
    assert all(p < NL0 for p in _u)
    L1_PREV[_i] = sorted(_u)
FUSED = os.environ.get('KUNFUSED') is None
STOP = int(os.environ.get('KSTOP', '99'))

GV_SPEC = [("a_norm", 16), ("a_g_q", 4), ("a_g_kv", 4), ("f_norm0", 16), ("e_norm0", 16), ("s_norm", 16),
           ("b_norm", 16), ("f_norm1", 16), ("e_norm1", 16), ("gqn_n", 1), ("gkn_n", 1), ("s_gkn", 1), ("b_gqn", 1),
           ("gqr", 1), ("gqsw", 1), ("gkr", 1), ("gksw", 1), ("invf", 1), ("sgn", 1)]
GVI = {}
_c = 0
for _n, _k in GV_SPEC:
    GVI[_n] = (_c, _k)
    _c += _k
NGV = _c
MK0 = {}
_c = 0
for _s in range(NL0):
    for _i in range(len(L0_PAST[_s])):
        MK0[(_s, _i)] = _c
        _c += 1
MK1 = {}
for _s in range(1, NL0):
    for _i in range(len(L1_PREV[_s])):
        MK1[(_s, _i)] = _c
        _c += 1
NMK = _c


class R:
    __slots__ = ("name", "w", "rs", "excl")

    def __init__(self, name="", excl=False):
        self.name = name
        self.w = None
        self.rs = []
        self.excl = excl


class Prog:
    NDMASEM = 12

    def __init__(self, nc):
        self.nc = nc
        self.engs = {"pe": nc.tensor, "act": nc.scalar, "dve": nc.vector, "pool": nc.gpsimd, "sp": nc.sync}
        self.ops = []
        self.floor = 0
        self.esem = {k: nc.alloc_semaphore(name="es_" + k) for k in self.engs}
        self.ecount = {k: 0 for k in self.engs}
        self.dsem = {k: [nc.alloc_semaphore(name="ds_%s_%d" % (k, j)) for j in range(self.NDMASEM)] for k in ("sp", "pool")}
        self.dcount = {k: 0 for k in self.dsem}
        self.tok = []
        self.waited = {k: {} for k in self.engs}
        self.nwaits = 0

    def op(self, eng, fn, reads=(), writes=(), dma=False):
        i = len(self.ops)
        deps = set()
        fl = self.floor
        if any(r.excl for r in reads):
            writes = list(writes) + [r for r in reads if r.excl]
            reads = [r for r in reads if not r.excl]
        for r in reads:
            if r.w is not None and r.w >= fl:
                deps.add(r.w)
        for r in writes:
            if r.w is not None and r.w >= fl:
                deps.add(r.w)
            for x in r.rs:
                if x >= fl:
                    deps.add(x)
        for r in reads:
            r.rs.append(i)
        for r in writes:
            r.w = i
            r.rs = []
        self.ops.append((eng, fn, deps, dma))
        return i

    def dma(self, q, out, in_, reads=(), writes=()):
        return self.op(q, lambda e: e.dma_start(out=out, in_=in_), reads, writes, dma=True)

    def flush(self):
        ops = self.ops
        start = len(self.tok)
        n = len(ops)
        if start == n:
            return
        needed = {}
        last = {}
        for i in range(start, n):
            eng, fn, deps, dma = ops[i]
            if not dma:
                last[eng] = i
            for d in deps:
                p = ops[d]
                if (not p[3]) and (not dma) and p[0] == "pe" and eng == "pe":
                    continue
                needed[d] = True
        for ek, i in last.items():
            needed[i] = True
        for i in range(start, n):
            ek, fn, deps, dma = ops[i]
            e = self.engs[ek]
            for d in sorted(deps):
                p = ops[d]
                if (not p[3]) and (not dma) and p[0] == "pe" and ek == "pe":
                    continue
                key, sem, val = self.tok[d]
                if self.waited[ek].get(key, 0) >= val:
                    continue
                e.wait_ge(sem, val)
                self.nwaits += 1
                self.waited[ek][key] = val
            if dma:
                k = self.dcount[ek]
                self.dcount[ek] += 1
                j = k % self.NDMASEM
                sem = self.dsem[ek][j]
                key = ("d", ek, j)
                prev = 16 * (k // self.NDMASEM)
                if prev > 0 and self.waited[ek].get(key, 0) < prev:
                    e.wait_ge(sem, prev)
                    self.waited[ek][key] = prev
                ins = fn(e)
                ins.then_inc(sem, 16)
                self.tok.append((key, sem, prev + 16))
            else:
                ins = fn(e)
                if needed.get(i, False):
                    self.ecount[ek] += 1
                    ins.then_inc(self.esem[ek], 1)
                    self.tok.append((("e", ek), self.esem[ek], self.ecount[ek]))
                else:
                    self.tok.append(None)
        for i in range(start, n):
            if self.tok[i] is None:
                self.tok[i] = self.tok[last[ops[i][0]]]

    def wait_all_dma(self):
        for ek in self.dsem:
            e = self.engs[ek]
            k = self.dcount[ek]
            for j in range(self.NDMASEM):
                cnt = (k - j + self.NDMASEM - 1) // self.NDMASEM if k > j else 0
                key = ("d", ek, j)
                if cnt > 0 and self.waited[ek].get(key, 0) < 16 * cnt:
                    e.wait_ge(self.dsem[ek][j], 16 * cnt)
                    self.waited[ek][key] = 16 * cnt

    def barrier(self):
        self.flush()
        self.wait_all_dma()
        self.nc.all_engine_barrier()
        self.floor = len(self.ops)


class Ctx:
    def __init__(self, nc):
        self.nc = nc
        self.P = Prog(nc)
        self.banks = []
        for i in range(8):
            t = nc.alloc_psum_tensor("ps%d" % i, [128, 512], F32)
            self.banks.append((t, R("ps%d" % i, excl=True)))
        self.pair_i = 0
        self.stacks = []
        self.uid = 0
        self.ns = 2

    @property
    def gw(self):
        return 512 * self.ns

    @contextlib.contextmanager
    def scope(self):
        st = contextlib.ExitStack()
        self.stacks.append(st)
        try:
            yield
            self.P.barrier()
        finally:
            self.stacks.pop()
            st.close()

    def alloc(self, name, shape, dtype):
        self.uid += 1
        return self.stacks[-1].enter_context(self.nc.sbuf_tensor("%s_%d" % (name, self.uid), shape, dtype))

    def bankpair(self):
        i = self.pair_i
        self.pair_i = (i + 1) % 4
        return [self.banks[2 * i], self.banks[2 * i + 1]][:self.ns]

    def init_wring(self, nw=4, nslab=2):
        self.wring = [(self.alloc("wr", [128, 2048], BF16), R("wr%d" % i)) for i in range(nw)]
        self.wi = 0
        self.slabs = [(self.alloc("ws", [128, 8192], BF16), R("ws%d" % i)) for i in range(nslab)]
        self.si = 0

    def load_w(self, src, kc, m):
        t, r = self.wring[self.wi]
        self.wi = (self.wi + 1) % len(self.wring)
        v = t[:, 0:kc * m].rearrange("p (k m) -> p k m", k=kc)
        self.P.dma("pool", v, src, writes=[r])
        return v, r

    def load_slab(self, src, a, b):
        t, r = self.slabs[self.si]
        self.si = (self.si + 1) % len(self.slabs)
        v = t[:, 0:a * b].rearrange("p (a b) -> p a b", a=a)
        self.P.dma("pool", v, src, writes=[r])
        return v, r


def mm(cx, out, lhsT, rhs, start, stop, reads, wr):
    cx.P.op("pe", lambda e: e.matmul(out, lhsT=lhsT, rhs=rhs, start=start, stop=stop), reads=reads, writes=[wr])


def act(cx, out, in_, func, reads, writes, scale=None, bias=None):
    kw = {}
    if scale is not None:
        kw["scale"] = scale
    if bias is not None:
        kw["bias"] = bias
    cx.P.op("act", lambda e: e.activation(out=out, in_=in_, func=func, **kw), reads=reads, writes=writes)


def tt(cx, out, in0, in1, op, reads, writes, eng="dve"):
    cx.P.op(eng, lambda e: e.tensor_tensor(out=out, in0=in0, in1=in1, op=op), reads=reads, writes=writes)


def ts(cx, out, in0, s1, s2, op0, op1, reads, writes, eng="dve"):
    if op1 is None:
        cx.P.op(eng, lambda e: e.tensor_scalar(out=out, in0=in0, scalar1=s1, scalar2=None, op0=op0), reads=reads, writes=writes)
    else:
        cx.P.op(eng, lambda e: e.tensor_scalar(out=out, in0=in0, scalar1=s1, scalar2=s2, op0=op0, op1=op1), reads=reads, writes=writes)


def stt(cx, out, in0, scalar, in1, op0, op1, reads, writes):
    cx.P.op("dve", lambda e: e.scalar_tensor_tensor(out=out, in0=in0, scalar=scalar, in1=in1, op0=op0, op1=op1),
            reads=reads, writes=writes)


def cpy(cx, eng, out, in_, reads, writes):
    if eng == "act":
        act(cx, out, in_, AF.Copy, reads, writes)
    else:
        cx.P.op(eng, lambda e: e.tensor_copy(out=out, in_=in_), reads=reads, writes=writes)


def gcol(cx, name, i=0, p=128):
    c0, _ = GVI[name]
    return cx.gv[0:p, c0 + i:c0 + i + 1]


def rstd_from_banks(cx, banks, dim):
    for s, (bt, br) in enumerate(banks):
        act(cx, cx.lnt[:, s * 512:(s + 1) * 512], bt[:, :], AF.Ln, [br, cx.r_const], [cx.r_lnt], scale=1.0 / dim,
            bias=cx.epsc[:, 0:1])
    act(cx, cx.rstd[:, 0:cx.gw], cx.lnt[:, 0:cx.gw], AF.Exp, [cx.r_lnt], [cx.r_rstd], scale=-0.5)


def next_sq(cx):
    sq, rsq = cx.sqring[cx.sqi]
    cx.sqi = (cx.sqi + 1) % len(cx.sqring)
    return sq, rsq


def norm_fm(cx, srcs, gname, outs, dim):
    banks = cx.bankpair()
    n = len(srcs)
    for i, (s, rs) in enumerate(srcs):
        sq, rsq = next_sq(cx)
        act(cx, sq[:, 0:cx.gw], s[:, 0:cx.gw], AF.Square, [rs], [rsq])
        for sb, (bt, br) in enumerate(banks):
            mm(cx, bt[:, :], cx.ones[:, :], sq[:, sb * 512:(sb + 1) * 512], i == 0, i == n - 1, [rsq, cx.r_const], br)
    rstd_from_banks(cx, banks, dim)
    for i, ((s, rs), (o, ro)) in enumerate(zip(srcs, outs)):
        stt(cx, o[:, 0:cx.gw], s[:, 0:cx.gw], gcol(cx, gname, i), cx.rstd[:, 0:cx.gw], ALU.mult, ALU.mult,
            [rs, cx.r_rstd, cx.r_const], [ro])


def head_norm(cx, ba, extra, dim, gname, out, r_out_list):
    sq, rsq = next_sq(cx)
    bb = cx.bankpair()
    for s, (bt, br) in enumerate(ba):
        act(cx, sq[:, s * 512:(s + 1) * 512], bt[:, :], AF.Square, [br], [rsq])
    for s, (bt, br) in enumerate(bb):
        sl = slice(s * 512, (s + 1) * 512)
        mm(cx, bt[:, :], cx.ones[:, :], sq[:, sl], True, extra is None, [rsq, cx.r_const], br)
        if extra is not None:
            mm(cx, bt[:, :], cx.ones[0:64, :], extra[0][:, sl], False, True, [extra[1], cx.r_const], br)
    rstd_from_banks(cx, bb, dim)
    for s, (bt, br) in enumerate(ba):
        sl = slice(s * 512, (s + 1) * 512)
        stt(cx, out[:, sl], bt[:, :], gcol(cx, gname), cx.rstd[:, sl], ALU.mult, ALU.mult, [br, cx.r_rstd, cx.r_const],
            r_out_list)


def linear_fm(cx, wtile, kc_n, ins, n_oc, epilogue, m=128):
    for oc in range(n_oc):
        wt, rw = cx.load_w(wtile(oc), kc_n, m)
        banks = cx.bankpair()
        for kc in range(kc_n):
            a, ra = ins[kc]
            for s, (bt, br) in enumerate(banks):
                mm(cx, bt[0:m, :], wt[:, kc, :], a[:, s * 512:(s + 1) * 512], kc == 0, kc == kc_n - 1, [rw, ra], br)
        epilogue(oc, banks)


def setup_consts(cx, io, rope, nt=0):
    P = cx.P
    cx.r_const = R("const")
    cx.ones = cx.alloc("ones", [128, 128], BF16)
    cx.gv = cx.alloc("gv", [128, NGV], F32)
    cx.mk = cx.alloc("mk", [128, NMK], F32)
    cx.epsc = cx.alloc("epsc", [128, 2], F32)
    P.op("dve", lambda e: e.memset(cx.ones[:, :], 1.0), writes=[cx.r_const])
    P.op("dve", lambda e: e.memset(cx.epsc[:, 0:1], EPS), writes=[cx.r_const])
    P.op("dve", lambda e: e.memset(cx.epsc[:, 1:2], 0.0), writes=[cx.r_const])
    cx.zeroc = cx.epsc[:, 1:2]
    P.dma("sp", cx.gv[:, :], io["gv"], writes=[cx.r_const])
    P.dma("sp", cx.mk[:, :], io["mk"], writes=[cx.r_const])
    cx.sqring = [(cx.alloc("sq", [128, 1024], BF16), R("sq%d" % i)) for i in range(2)]
    cx.sqi = 0
    cx.rstd = cx.alloc("rstd", [128, 1024], F32)
    cx.r_rstd = R("rstd")
    cx.lnt = cx.alloc("lnt", [128, 1024], F32)
    cx.r_lnt = R("lnt")
    if rope is not None:
        cx.ropeC = cx.alloc("ropeC", [64, nt], F32)
        cx.ropeS = cx.alloc("ropeS", [64, nt], F32)
        cx.r_rope = R("rope")
        build_rope(cx, io, "g%sr" % rope, "g%ssw" % rope, nt)


def build_rope(cx, io, gr, gsw, nt):
    P = cx.P
    PI = float(np.pi)
    rr = cx.r_rope
    with cx.scope():
        posi = cx.alloc("posi", [64, nt], I32)
        ang = cx.alloc("ang", [64, nt], F32)
        kf = cx.alloc("kf", [64, nt], F32)
        ki = cx.alloc("ki", [64, nt], I32)
        r = cx.alloc("r", [64, nt], F32)
        m = cx.alloc("m", [64, nt], F32)
        P.dma("sp", posi[:, :], io["pos"][0:nt].partition_broadcast(64), writes=[rr])
        P.op("dve", lambda e: e.tensor_copy(out=ang[:, :], in_=posi[:, :]), reads=[rr], writes=[rr])
        ts(cx, ang[:, :], ang[:, :], gcol(cx, "invf", 0, 64), None, ALU.mult, None, [rr, cx.r_const], [rr])
        ts(cx, kf[:, :], ang[:, :], 1.0 / (2 * PI), None, ALU.mult, None, [rr], [rr])
        P.op("dve", lambda e: e.tensor_copy(out=ki[:, :], in_=kf[:, :]), reads=[rr], writes=[rr])
        P.op("dve", lambda e: e.tensor_copy(out=kf[:, :], in_=ki[:, :]), reads=[rr], writes=[rr])
        C1 = 6.28125
        C2 = float(2 * np.pi - 6.28125)
        stt(cx, r[:, :], kf[:, :], -C1, ang[:, :], ALU.mult, ALU.add, [rr], [rr])
        stt(cx, r[:, :], kf[:, :], -C2, r[:, :], ALU.mult, ALU.add, [rr], [rr])

        def wrap(x):
            ts(cx, m[:, :], x, PI, -2 * PI, ALU.is_gt, ALU.mult, [rr], [rr])
            tt(cx, x, x, m[:, :], ALU.add, [rr], [rr])
            ts(cx, m[:, :], x, -PI, 2 * PI, ALU.is_lt, ALU.mult, [rr], [rr])
            tt(cx, x, x, m[:, :], ALU.add, [rr], [rr])
            ts(cx, x, x, PI, -PI, ALU.min, ALU.max, [rr], [rr])

        wrap(r[:, :])
        act(cx, kf[:, :], r[:, :], AF.Sin, [rr], [rr])
        ts(cx, cx.ropeS[:, :], kf[:, :], gcol(cx, gsw, 0, 64), gcol(cx, "sgn", 0, 64), ALU.mult, ALU.mult,
           [rr, cx.r_const], [rr])
        ts(cx, r[:, :], r[:, :], PI / 2, None, ALU.add, None, [rr], [rr])
        wrap(r[:, :])
        act(cx, kf[:, :], r[:, :], AF.Sin, [rr], [rr])
        ts(cx, cx.ropeC[:, :], kf[:, :], gcol(cx, gr, 0, 64), None, ALU.mult, None, [rr, cx.r_const], [rr])


def alloc_stream(cx):
    cx.xT = cx.alloc("xT", [128, 16, G], F32)
    cx.rx = [R("x%d" % c) for c in range(16)]
    cx.hT = cx.alloc("hT", [128, 16, G], BF16)
    cx.rh = [R("h%d" % c) for c in range(16)]


def xchunks(cx):
    return [(cx.xT[:, c, :], cx.rx[c]) for c in range(16)]


def hchunks(cx):
    return [(cx.hT[:, c, :], cx.rh[c]) for c in range(16)]


def residual_epilogue(cx):
    def ep(oc, banks):
        for s, (bt, br) in enumerate(banks):
            xs = cx.xT[:, oc, s * 512:(s + 1) * 512]
            tt(cx, xs, xs, bt[:, :], ALU.add, [cx.rx[oc], br], [cx.rx[oc]])
    return ep


def slab_f32(cx, i):
    t, r = cx.slabs[i]
    return t[:, :].bitcast(F32).rearrange("p (a b) -> p a b", a=4), r


def phase1(cx, io):
    P = cx.P
    with cx.scope():
        setup_consts(cx, io, "k", T1)
        Ck, Sk = cx.ropeC, cx.ropeS
        cx.ns = 2
        cx.init_wring(nw=2)
        alloc_stream(cx)
        ckvf, r_ckvf = slab_f32(cx, 1)
        ckvT = cx.alloc("ckvT", [128, 4, G], BF16)
        r_ckvT = [R("ckvT%d" % i) for i in range(4)]
        krf = cx.alloc("krf", [64, G], F32)
        r_krf = R("krf")
        sqpe = cx.alloc("sqpe", [64, G], BF16)
        r_sqpe = R("sqpe")
        vt = [(cx.alloc("vt", [128, 512], BF16), R("vt%d" % i)) for i in range(2)]
        kno = [(cx.alloc("kno", [128, G], BF16), R("kno%d" % i)) for i in range(2)]
        kro = [(cx.alloc("kro", [64, G], BF16), R("kro%d" % i)) for i in range(2)]
        t1 = cx.lnt[0:64, :]
        r_t1 = cx.r_lnt
        r_out = R("p1out")
        if STOP == 0:
            return
        for g in range(T1 // G):
            tok = slice(g * G, (g + 1) * G)
            P.dma("sp", cx.xT[:, :, :], io["xT"][:, :, tok], writes=cx.rx)
            if STOP == 1:
                return
            norm_fm(cx, xchunks(cx), "a_norm", hchunks(cx), D)
            if STOP == 2:
                return

            def ep_ckv(oc, banks):
                for s, (bt, br) in enumerate(banks):
                    cpy(cx, "act" if s == 0 else "dve", ckvf[:, oc, s * 512:(s + 1) * 512], bt[:, :], [br], [r_ckvf])
            linear_fm(cx, lambda oc: io["WDKV"][oc], 16, hchunks(cx), 4, ep_ckv)
            if STOP == 3:
                return
            wt, rw = cx.load_w(io["WDKV"][4], 16, 128)
            bp = cx.bankpair()
            bs = cx.bankpair()
            for half, banks in ((0, bp), (1, bs)):
                for kc in range(16):
                    for s, (bt, br) in enumerate(banks):
                        mm(cx, bt[0:64, :], wt[:, kc, half * 64:(half + 1) * 64], cx.hT[:, kc, s * 512:(s + 1) * 512],
                           kc == 0, kc == 15, [rw, cx.rh[kc]], br)
            for s in range(2):
                sl = slice(s * 512, (s + 1) * 512)
                gsl = slice(g * G + s * 512, g * G + (s + 1) * 512)
                act(cx, sqpe[:, sl], bp[s][0][0:64, :], AF.Square, [bp[s][1]], [r_sqpe])
                tt(cx, krf[:, sl], bp[s][0][0:64, :], Ck[:, gsl], ALU.mult, [bp[s][1], cx.r_rope], [r_krf])
                tt(cx, t1[:, sl], bs[s][0][0:64, :], Sk[:, gsl], ALU.mult, [bs[s][1], cx.r_rope], [r_t1])
                tt(cx, krf[:, sl], krf[:, sl], t1[:, sl], ALU.add, [r_krf, r_t1], [r_krf])
            if STOP == 4:
                return
            norm_fm(cx, [(ckvf[:, i, :], r_ckvf) for i in range(4)], "a_g_kv",
                    [(ckvT[:, i, :], r_ckvT[i]) for i in range(4)], 512)
            if STOP == 5:
                return
            wv, r_wv = cx.load_slab(io["WV"], 4, 2048)
            assert r_wv is cx.slabs[0][1]
            cx.si = 0
            for tti in range(8):
                for cg in range(4):
                    bt, br = cx.banks[(tti * 4 + cg) % 8]
                    for kc in range(4):
                        mm(cx, bt[:, :], ckvT[:, kc, tti * 128:(tti + 1) * 128], wv[:, kc, cg * 512:(cg + 1) * 512],
                           kc == 0, kc == 3, [r_ckvT[kc], r_wv], br)
                    vtt, rvt = vt[cg % 2]
                    cpy(cx, "act" if cg % 2 == 0 else "dve", vtt[:, :], bt[:, :], [br], [rvt])
                    dst = io["VV"][cg * 4:(cg + 1) * 4, :, g * 8 + tti, :].rearrange("h p d -> p h d")
                    P.dma("sp", dst, vtt[:, :].rearrange("p (h d) -> p h d", h=4), reads=[rvt], writes=[r_out])
            if STOP == 6:
                return
            def kmm(h):
                wt, rw = cx.load_w(io["WUKVN"][h], 4, 128)
                ba = cx.bankpair()
                for kc in range(4):
                    for s, (bt, br) in enumerate(ba):
                        mm(cx, bt[:, :], wt[:, kc, :], ckvT[:, kc, s * 512:(s + 1) * 512], kc == 0, kc == 3, [rw, r_ckvT[kc]], br)
                return ba
            ba_next = kmm(0)
            for h in range(16):
                ba = ba_next
                if h + 1 < 16:
                    ba_next = kmm(h + 1)
                ko, rko = kno[h % 2]
                kr_, rkr = kro[h % 2]
                head_norm(cx, ba, (sqpe, r_sqpe), 192, "gkn_n", ko, [rko])
                tt(cx, kr_[:, :], krf[:, :], cx.rstd[0:64, :], ALU.mult, [r_krf, cx.r_rstd], [rkr])
                P.dma("sp", io["KN"][h, :, tok], ko[:, :], reads=[rko], writes=[r_out])
                P.dma("sp", io["KR"][h, :, tok], kr_[:, :], reads=[rkr], writes=[r_out])


def ffn_block(cx, io, layer, fnorm):
    norm_fm(cx, xchunks(cx), fnorm, hchunks(cx), D)
    WG, WU, WD = io["WG%d" % layer], io["WU%d" % layer], io["WD%d" % layer]
    ns = cx.ns
    gb = [cx.banks[0], cx.banks[1]][:ns]
    ub = [cx.banks[2], cx.banks[3]][:ns]
    db = [[cx.banks[4], cx.banks[5]][:ns], [cx.banks[6], cx.banks[7]][:ns]]
    di = 0
    for sb in range(NFF // 4):
        aT, r_aT = cx.aT[sb % 2]
        for c in range(4):
            ff = sb * 4 + c
            wg, rwg = cx.load_w(WG[ff], 16, 128)
            wu, rwu = cx.load_w(WU[ff], 16, 128)
            sg, rsg = cx.sg[ff % 2]
            for kc in range(16):
                for s, (bt, br) in enumerate(gb):
                    mm(cx, bt[:, :], wg[:, kc, :], cx.hT[:, kc, s * 512:(s + 1) * 512], kc == 0, kc == 15, [rwg, cx.rh[kc]], br)
            for s, (bt, br) in enumerate(gb):
                act(cx, sg[:, s * 512:(s + 1) * 512], bt[:, :], AF.Silu, [br], [rsg])
            for kc in range(16):
                for s, (bt, br) in enumerate(ub):
                    mm(cx, bt[:, :], wu[:, kc, :], cx.hT[:, kc, s * 512:(s + 1) * 512], kc == 0, kc == 15, [rwu, cx.rh[kc]], br)
            for s, (bt, br) in enumerate(ub):
                sl = slice(s * 512, (s + 1) * 512)
                tt(cx, aT[:, c, sl], sg[:, sl], bt[:, :], ALU.mult, [rsg, br], [r_aT[c]])
        wd, rwd = cx.load_slab(WD[sb], 4, 2048)
        for oc in range(16):
            banks = db[di]
            di = 1 - di
            for c in range(4):
                for s, (bt, br) in enumerate(banks):
                    mm(cx, bt[:, :], wd[:, c, oc * 128:(oc + 1) * 128], aT[:, c, s * 512:(s + 1) * 512], c == 0, c == 3,
                       [rwd, r_aT[c]], br)
            for s, (bt, br) in enumerate(banks):
                xs = cx.xT[:, oc, s * 512:(s + 1) * 512]
                tt(cx, xs, xs, bt[:, :], ALU.add, [cx.rx[oc], br], [cx.rx[oc]])


def egate_block(cx, io, layer, enorm, tok):
    P = cx.P
    norm_fm(cx, xchunks(cx), enorm, hchunks(cx), D)
    EG, EP = io["EG%d" % layer], io["EP%d" % layer]
    P.dma("pool", cx.pT[:, :, 0:cx.gw], io["pT%d" % layer][:, :, tok], writes=[cx.r_pT])
    for oc in range(16):
        wg, rwg = cx.load_w(EG[oc], 16, 128)
        wp, rwp = cx.load_w(EP[oc], 2, 128)
        ba = cx.bankpair()
        bb = cx.bankpair()
        for kc in range(16):
            for s, (bt, br) in enumerate(ba):
                mm(cx, bt[:, :], wg[:, kc, :], cx.hT[:, kc, s * 512:(s + 1) * 512], kc == 0, kc == 15, [rwg, cx.rh[kc]], br)
        for kc in range(2):
            for s, (bt, br) in enumerate(bb):
                mm(cx, bt[:, :], wp[:, kc, :], cx.pT[:, kc, s * 512:(s + 1) * 512], kc == 0, kc == 1, [rwp, cx.r_pT], br)
        sg, rsg = cx.sg[oc % 2]
        for s in range(cx.ns):
            sl = slice(s * 512, (s + 1) * 512)
            act(cx, sg[:, sl], ba[s][0][:, :], AF.Sigmoid, [ba[s][1]], [rsg])
            tt(cx, sg[:, sl], sg[:, sl], bb[s][0][:, :], ALU.mult, [rsg, bb[s][1]], [rsg])
            xs = cx.xT[:, oc, sl]
            tt(cx, xs, xs, sg[:, sl], ALU.add, [cx.rx[oc], rsg], [cx.rx[oc]])


def alloc_ffn(cx):
    cx.aT = []
    for i in range(2):
        cx.aT.append((cx.alloc("aT", [128, 4, G], BF16), [R("aT%d_%d" % (i, c)) for c in range(4)]))
    cx.sg = [(cx.alloc("sg", [128, G], F32), R("sg%d" % i)) for i in range(2)]
    cx.pT = cx.alloc("pT", [128, 2, G], BF16)
    cx.r_pT = R("pT")


def attention(cx, h, sl, qn, rqn, qr, rqr, tiles, scale, bias_fn=None, st_banks=(0, 1), o_banks=(2, 3)):
    P = cx.P
    ob, orr = cx.banks[o_banks[0]]
    sb_, srr = cx.banks[o_banks[1]]
    n = len(tiles)
    nb = len(st_banks)
    la = nb - 1

    def qk(i):
        t = tiles[i]
        c0, c1 = t["c0"], t["c1"]
        stb, rst = cx.banks[st_banks[i % nb]]
        if qr is not None:
            mm(cx, stb[:, c0:c1], t["K"], qn[:, c0:c1], True, False, t["rK"] + [rqn], rst)
            mm(cx, stb[:, c0:c1], t["KR"], qr[:, c0:c1], False, True, t["rK"] + [rqr], rst)
        else:
            mm(cx, stb[:, c0:c1], t["K"], qn[:, c0:c1], True, True, t["rK"] + [rqn], rst)

    def rest(i):
        t = tiles[i]
        c0, c1 = t["c0"], t["c1"]
        stb, rst = cx.banks[st_banks[i % nb]]
        pt, rpt = cx.pring[cx.pi]
        cx.pi = (cx.pi + 1) % len(cx.pring)
        mcol = cx.mk[:, t["mask"]:t["mask"] + 1] if t["mask"] is not None else cx.zeroc
        if bias_fn is None:
            act(cx, pt[:, c0:c1], stb[:, c0:c1], AF.Exp, [rst, cx.r_const], [rpt], scale=scale, bias=mcol)
        else:
            tmp, rtmp = cx.tmpring[cx.ti]
            cx.ti = (cx.ti + 1) % len(cx.tmpring)
            b_ap, b_r = bias_fn(t)
            stt(cx, tmp[:, c0:c1], stb[:, c0:c1], scale, b_ap, ALU.mult, ALU.add, [rst, b_r], [rtmp])
            act(cx, pt[:, c0:c1], tmp[:, c0:c1], AF.Exp, [rtmp, cx.r_const], [rpt], scale=1.0, bias=mcol)
        if t["zero"] is not None:
            p0, p1, z0, z1 = t["zero"]
            cx.P.op("pool", lambda e, o=pt[p0:p1, z0:z1]: e.memset(o, 0.0), writes=[rpt])
        mm(cx, ob[:, c0:c1], t["V"], pt[:, c0:c1], i == 0, i == n - 1, t["rK"] + [rpt], orr)
        mm(cx, sb_[:, c0:c1], cx.ones[:, :], pt[:, c0:c1], i == 0, i == n - 1, [rpt, cx.r_const], srr)

    for i in range(min(la, n)):
        qk(i)
    for i in range(n):
        if i + la < n:
            qk(i + la)
        rest(i)
    act(cx, cx.rinv[:, :], sb_[:, :], AF.Ln, [srr], [cx.r_rinv])
    act(cx, cx.rinv[:, :], cx.rinv[:, :], AF.Exp, [cx.r_rinv], [cx.r_rinv], scale=-1.0)
    tt(cx, cx.hT[:, h, sl * 512:(sl + 1) * 512], ob[:, :], cx.rinv[:, :], ALU.mult, [orr, cx.r_rinv], [cx.rh[h]])


def alloc_attn(cx, npt=4):
    cx.pring = [(cx.alloc("pt", [128, 512], BF16), R("pt%d" % i)) for i in range(npt)]
    cx.pi = 0
    cx.rinv = cx.alloc("rinv", [128, 512], F32)
    cx.r_rinv = R("rinv")


def phase2(cx, io):
    P = cx.P
    with cx.scope():
        setup_consts(cx, io, "q", T2)
        Cq, Sq = cx.ropeC, cx.ropeS
        cx.ns = 2
        cx.init_wring(nw=3)
        alloc_stream(cx)
        r_out = R("p2out")
        sc0 = float(1.0 / np.sqrt(192.0))
        t0s, r0s = cx.slabs[0]
        t1s, r1s = cx.slabs[1]
        r_kr = [R("kr0"), R("kr1")]
        for grp in L0_GROUPS:
            cx.ns = len(grp)
            gw = cx.gw
            tok = slice(grp[0] * 512, grp[0] * 512 + gw)
            P.dma("sp", cx.xT[:, :, 0:gw], io["xT"][:, :, tok], writes=cx.rx)
            with cx.scope():
                alloc_attn(cx, 2)
                cqf, r_cqf = slab_f32(cx, 1)
                cqT = cx.alloc("cqT", [128, 4, G], BF16)
                r_cqT = [R("cqT%d" % i) for i in range(4)]
                qn = [(cx.alloc("qn", [128, 512], BF16), R("qn%d" % i)) for i in range(2)]
                qr = [(cx.alloc("qr", [64, 512], BF16), R("qr%d" % i)) for i in range(2)]
                sqn = cx.alloc("sqn", [128, 512], BF16)
                sqr = cx.alloc("sqr", [64, 512], BF16)
                r_sqq = R("sqq")
                rq = cx.rstd[:, 0:512]
                lq = cx.lnt[:, 0:512]
                r_rq, r_lq = cx.r_rstd, cx.r_lnt
                t1 = cx.lnt[0:64, 512:1024]
                t2 = cx.rstd[0:64, 512:1024]
                r_t1 = R("t1")
                r_t2 = R("t2")
                kvB = cx.alloc("kvB", [128, 8192], BF16)
                r_kvB = R("kvB")
                KVB = [(t0s, r0s), (kvB, r_kvB)]
                norm_fm(cx, xchunks(cx), "a_norm", hchunks(cx), D)

                def ep_cq(oc, banks):
                    for s_, (bt, br) in enumerate(banks):
                        cpy(cx, "act" if s_ == 0 else "dve", cqf[:, oc, s_ * 512:(s_ + 1) * 512], bt[:, :], [br], [r_cqf])
                linear_fm(cx, lambda oc: io["WDQ"][oc], 16, hchunks(cx), 4, ep_cq)
                norm_fm(cx, [(cqf[:, i, :], r_cqf) for i in range(4)], "a_g_q",
                        [(cqT[:, i, :], r_cqT[i]) for i in range(4)], 512)
                items = [(h, sl) for h in range(16) for sl in range(cx.ns)]
                wqs = {}

                def qpath(k):
                    h, sl = items[k]
                    if h not in wqs:
                        wqs[h] = cx.load_w(io["WUQ"][h], 4, 256)
                    wq, rwq = wqs[h]
                    slot = grp[sl]
                    csl = slice(sl * 512, (sl + 1) * 512)
                    gsl = slice(slot * 512, (slot + 1) * 512)
                    bA, rA = cx.banks[4]
                    bB, rB = cx.banks[5]
                    bC, rC = cx.banks[6]
                    bD, rD = cx.banks[7]
                    for kc in range(4):
                        mm(cx, bA[:, :], wq[:, kc, 0:128], cqT[:, kc, csl], kc == 0, kc == 3, [rwq, r_cqT[kc]], rA)
                    for kc in range(4):
                        mm(cx, bB[0:64, :], wq[:, kc, 128:192], cqT[:, kc, csl], kc == 0, kc == 3, [rwq, r_cqT[kc]], rB)
                    for kc in range(4):
                        mm(cx, bC[0:64, :], wq[:, kc, 192:256], cqT[:, kc, csl], kc == 0, kc == 3, [rwq, r_cqT[kc]], rC)
                    act(cx, sqn[:, :], bA[:, :], AF.Square, [rA], [r_sqq])
                    act(cx, sqr[:, :], bB[0:64, :], AF.Square, [rB], [r_sqq])
                    mm(cx, bD[:, :], cx.ones[:, :], sqn[:, :], True, False, [r_sqq, cx.r_const], rD)
                    mm(cx, bD[:, :], cx.ones[0:64, :], sqr[:, :], False, True, [r_sqq, cx.r_const], rD)
                    act(cx, lq, bD[:, :], AF.Ln, [rD, cx.r_const], [r_lq], scale=1.0 / 192, bias=cx.epsc[:, 0:1])
                    act(cx, rq, lq, AF.Exp, [r_lq], [r_rq], scale=-0.5)
                    qnt, rqn = qn[k % 2]
                    qrt, rqr = qr[k % 2]
                    stt(cx, qnt[:, :], bA[:, :], gcol(cx, "gqn_n"), rq, ALU.mult, ALU.mult, [rA, r_rq, cx.r_const], [rqn])
                    tt(cx, t1[:, :], bB[0:64, :], Cq[:, gsl], ALU.mult, [rB, cx.r_rope], [r_t1])
                    tt(cx, t2[:, :], bC[0:64, :], Sq[:, gsl], ALU.mult, [rC, cx.r_rope], [r_t2])
                    tt(cx, t1[:, :], t1[:, :], t2[:, :], ALU.add, [r_t1, r_t2], [r_t1])
                    tt(cx, qrt[:, :], t1[:, :], rq[0:64, :], ALU.mult, [r_t1, r_rq], [rqr])

                qpath(0)
                for k, (h, sl) in enumerate(items):
                    kvt, r_kv = KVB[h % 2]
                    KA = kvt[:, 0:4096]
                    VA = kvt[:, 4096:8192].rearrange("p (t d) -> p t d", t=32)
                    KRA = t1s[0:64, (h % 2) * 4096:(h % 2 + 1) * 4096]
                    r_kra = r_kr[h % 2]
                    if sl == 0:
                        P.dma("sp", KA, io["KN"][h], writes=[r_kv])
                        P.dma("sp", KRA, io["KR"][h], reads=r_cqT, writes=[r_kra])
                        P.dma("sp", VA, io["VV"][h], writes=[r_kv])
                    if k + 1 < len(items):
                        qpath(k + 1)
                    slot = grp[sl]
                    qnt, rqn = qn[k % 2]
                    qrt, rqr = qr[k % 2]
                    tiles = []
                    for t in range(4):
                        ks = slice(slot * 512 + t * 128, slot * 512 + (t + 1) * 128)
                        tiles.append(dict(K=KA[:, ks], KR=KRA[:, ks], V=VA[:, slot * 4 + t, :], rK=[r_kv, r_kra],
                                          c0=t * 128, c1=512, mask=None, zero=(64, 128, t * 128, t * 128 + 64)))
                    for i, pp in enumerate(L0_PAST[slot]):
                        for t in range(4):
                            ks = slice(pp * 512 + t * 128, pp * 512 + (t + 1) * 128)
                            tiles.append(dict(K=KA[:, ks], KR=KRA[:, ks], V=VA[:, pp * 4 + t, :], rK=[r_kv, r_kra],
                                              c0=0, c1=512, mask=MK0[(slot, i)], zero=None))
                    attention(cx, h, sl, qnt, rqn, qrt, rqr, tiles, sc0)
            with cx.scope():
                alloc_ffn(cx)
                linear_fm(cx, lambda oc: io["WO0"][oc], 16, hchunks(cx), 16, residual_epilogue(cx))
                ffn_block(cx, io, 0, "f_norm0")
                egate_block(cx, io, 0, "e_norm0", tok)
                P.dma("sp", io["xres"][:, :, tok], cx.xT[:, :, 0:gw], reads=cx.rx, writes=[r_out])
            with cx.scope():
                sko = [(cx.alloc("sko", [128, G], BF16), R("sko%d" % i)) for i in range(2)]
                svt = [(cx.alloc("svt", [128, 512], BF16), R("svt%d" % i)) for i in range(2)]
                norm_fm(cx, xchunks(cx), "s_norm", hchunks(cx), D)
                for h in range(16):
                    wt, rw = cx.load_w(io["SWK"][h], 16, 128)
                    ba = cx.bankpair()
                    for kc in range(16):
                        for s_, (bt, br) in enumerate(ba):
                            mm(cx, bt[:, :], wt[:, kc, :], cx.hT[:, kc, s_ * 512:(s_ + 1) * 512], kc == 0, kc == 15,
                               [rw, cx.rh[kc]], br)
                    ko, rko = sko[h % 2]
                    head_norm(cx, ba, None, 128, "s_gkn", ko, [rko])
                    P.dma("sp", io["SK"][h, :, tok], ko[:, 0:gw], reads=[rko], writes=[r_out])
                for cg in range(4):
                    wsl, rws = cx.load_slab(io["SWV"][cg], 16, 512)
                    for tti in range(4 * cx.ns):
                        bt, br = cx.banks[(cg * 8 + tti) % 8]
                        for kc in range(16):
                            mm(cx, bt[:, :], cx.hT[:, kc, tti * 128:(tti + 1) * 128], wsl[:, kc, :], kc == 0, kc == 15,
                               [cx.rh[kc], rws], br)
                        v_, rv_ = svt[tti % 2]
                        cpy(cx, "act" if tti % 2 == 0 else "dve", v_[:, :], bt[:, :], [br], [rv_])
                        dst = io["SV"][cg * 4:(cg + 1) * 4, :, grp[0] * 4 + tti, :].rearrange("h p d -> p h d")
                        P.dma("sp", dst, v_[:, :].rearrange("p (h d) -> p h d", h=4), reads=[rv_], writes=[r_out])
        cx.ns = 2


def phase3(cx, io):
    P = cx.P
    with cx.scope():
        setup_consts(cx, io, None)
        cx.ns = 2
        cx.init_wring()
        alloc_stream(cx)
        r_out = R("p3out")
        r_ext = R("ext")
        ext = io["ext"]
        tab = io["relb"]
        Z = io["Z"]
        with cx.scope():
            tab_sb = cx.alloc("tab_sb", [16, 513], F32)
            ext_sb = cx.alloc("ext_sb", [16, 1535], F32)
            P.dma("sp", tab_sb[:, :], tab, writes=[r_ext])
            P.op("dve", lambda e: e.memset(ext_sb[:, :], 0.0), writes=[r_ext])
            ts(cx, ext_sb[:, 0:255], ext_sb[:, 0:255], tab_sb[:, 0:1], None, ALU.add, None, [r_ext], [r_ext])
            ts(cx, ext_sb[:, 768:1535], ext_sb[:, 768:1535], tab_sb[:, 512:513], None, ALU.add, None, [r_ext], [r_ext])
            P.op("dve", lambda e: e.tensor_copy(out=ext_sb[:, 255:768], in_=tab_sb[:, :]), reads=[r_ext], writes=[r_ext])
            P.dma("sp", ext, ext_sb[:, :], reads=[r_ext], writes=[r_ext])
            srcb = bass.AP(tensor=ext.tensor, offset=0, ap=[[1535, 16], [0, 128], [1, 1535]])
            P.dma("sp", Z[:, :, 0:1535], srcb, reads=[r_ext], writes=[r_ext])
        sc1 = float(1.0 / np.sqrt(128.0))
        for gi, grp in enumerate(L1_GROUPS):
            tok = slice(grp[0] * 512, grp[0] * 512 + G)
            otok = slice(gi * G, (gi + 1) * G)
            P.dma("sp", cx.xT[:, :, :], io["xres"][:, :, tok], writes=cx.rx)
            with cx.scope():
                alloc_attn(cx)
                cx.tmpring = [(cx.alloc("tmp", [128, 512], F32), R("tmp%d" % i)) for i in range(2)]
                cx.ti = 0
                qT = cx.alloc("qT", [128, 16, G], BF16)
                r_qT = [R("qT%d" % i) for i in range(16)]
                TB = [(cx.alloc("TB", [128, 1024], F32), R("TB%d" % i)) for i in range(2)]
                norm_fm(cx, xchunks(cx), "b_norm", hchunks(cx), D)
                for h in range(16):
                    wt, rw = cx.load_w(io["BWQ"][h], 16, 128)
                    ba = cx.bankpair()
                    for kc in range(16):
                        for s_, (bt, br) in enumerate(ba):
                            mm(cx, bt[:, :], wt[:, kc, :], cx.hT[:, kc, s_ * 512:(s_ + 1) * 512], kc == 0, kc == 15,
                               [rw, cx.rh[kc]], br)
                    head_norm(cx, ba, None, 128, "b_gqn", qT[:, h, :], [r_qT[h]])
                for h in range(16):
                    t0s, r0s = cx.slabs[h % 2]
                    SKA = t0s[:, 0:T2]
                    SVA = t0s[:, 4096:4096 + T2].rearrange("p (t d) -> p t d", t=4 * NL0)
                    P.dma("sp", SKA, io["SK"][h], writes=[r0s])
                    P.dma("sp", SVA, io["SV"][h], writes=[r0s])
                    tb, rtb = TB[h % 2]
                    src = bass.AP(tensor=Z.tensor, offset=h * 128 * 1536 + 127, ap=[[1535, 128], [1, 1024]])
                    P.dma("sp", tb[:, :], src, reads=[r_ext], writes=[rtb])
                    for sl in range(2):
                        slot = grp[sl]
                        csl = slice(sl * 512, (sl + 1) * 512)
                        tiles = []
                        for t in range(4):
                            ks = slice(slot * 512 + t * 128, slot * 512 + (t + 1) * 128)
                            tiles.append(dict(K=SKA[:, ks], V=SVA[:, slot * 4 + t, :], rK=[r0s], rel=t,
                                              c0=t * 128, c1=512, mask=None, zero=(64, 128, t * 128, t * 128 + 64)))
                        for i, pp in enumerate(L1_PREV[slot]):
                            for t in range(4):
                                ks = slice(pp * 512 + t * 128, pp * 512 + (t + 1) * 128)
                                tiles.append(dict(K=SKA[:, ks], V=SVA[:, pp * 4 + t, :], rK=[r0s], rel=t - 4,
                                                  c0=0, c1=(t + 1) * 128, mask=MK1[(slot, i)],
                                                  zero=(0, 64, t * 128 + 64, t * 128 + 128)))

                        def bias_fn(t, tb=tb, rtb=rtb):
                            j0 = 384 - t["rel"] * 128
                            return tb[:, j0 + t["c0"]:j0 + t["c1"]], rtb
                        attention(cx, h, sl, qT[:, h, csl], r_qT[h], None, None, tiles, sc1, bias_fn,
                                  st_banks=(0, 1, 4, 5), o_banks=((2, 3) if (2 * h + sl) % 2 == 0 else (6, 7)))
            with cx.scope():
                alloc_ffn(cx)
                linear_fm(cx, lambda oc: io["WO1"][oc], 16, hchunks(cx), 16, residual_epilogue(cx))
                ffn_block(cx, io, 1, "f_norm1")
                egate_block(cx, io, 1, "e_norm1", tok)
                P.dma("sp", io["outT"][:, :, otok], cx.xT[:, :, :], reads=cx.rx, writes=[r_out])


def tiled(w, kc, oc, m=128):
    return np.ascontiguousarray(w.reshape(kc, 128, oc, m).transpose(2, 1, 0, 3))


DRAM_SPECS = {
    "xT": ([128, 16, T1], F32), "pos": ([T1], I32), "gv": ([128, NGV], F32), "mk": ([128, NMK], F32),
    "WDKV": ([5, 128, 16, 128], F32), "WUKVN": ([16, 128, 4, 128], F32), "WV": ([128, 4, 2048], F32),
    "KN": ([16, 128, T1], BF16), "KR": ([16, 64, T1], BF16), "VV": ([16, 128, 32, 128], BF16),
    "WDQ": ([4, 128, 16, 128], F32), "WUQ": ([16, 128, 4, 256], F32), "WO0": ([16, 128, 16, 128], F32),
    "WG0": ([NFF, 128, 16, 128], F32), "WU0": ([NFF, 128, 16, 128], F32), "WD0": ([NFF // 4, 128, 4, 2048], F32),
    "EG0": ([16, 128, 16, 128], F32), "EP0": ([16, 128, 2, 128], F32), "pT0": ([128, 2, T2], F32),
    "SWK": ([16, 128, 16, 128], F32), "SWV": ([4, 128, 16, 512], F32),
    "xres": ([128, 16, T2], F32), "SK": ([16, 128, T2], BF16), "SV": ([16, 128, 4 * NL0, 128], BF16),
    "BWQ": ([16, 128, 16, 128], F32), "WO1": ([16, 128, 16, 128], F32),
    "WG1": ([NFF, 128, 16, 128], F32), "WU1": ([NFF, 128, 16, 128], F32), "WD1": ([NFF // 4, 128, 4, 2048], F32),
    "EG1": ([16, 128, 16, 128], F32), "EP1": ([16, 128, 2, 128], F32), "pT1": ([128, 2, T2], F32),
    "relb": ([16, 513], F32), "ext": ([16, 1535], F32), "Z": ([16, 128, 1536], F32), "outT": ([128, 16, T], F32),
}
PHASE_IN = {
    1: ["xT", "pos", "gv", "mk", "WDKV", "WUKVN", "WV"],
    2: ["xT", "pos", "gv", "mk", "KN", "KR", "VV", "WDQ", "WUQ", "WO0", "WG0", "WU0", "WD0",
        "EG0", "EP0", "pT0", "SWK", "SWV"],
    3: ["xres", "gv", "mk", "SK", "SV", "BWQ", "WO1", "WG1", "WU1", "WD1", "EG1", "EP1", "pT1", "relb"],
}
PHASE_OUT = {1: ["KN", "KR", "VV"], 2: ["xres", "SK", "SV"], 3: ["outT"]}
PHASE_INT = {1: [], 2: [], 3: ["ext", "Z"]}


def build_program(phases):
    fused = len(phases) > 1
    nc = bass.Bass("TRN2", target_bir_lowering=False)
    io = {}
    internal = set()
    if fused:
        for ph in phases:
            internal.update(PHASE_OUT[ph])
        internal.discard("outT")
    for ph in phases:
        for n in PHASE_IN[ph] + PHASE_OUT[ph] + PHASE_INT[ph]:
            if n in io:
                continue
            shape, dt = DRAM_SPECS[n]
            if n in internal or n in PHASE_INT[ph]:
                kind = "Internal"
            elif n in PHASE_OUT[ph]:
                kind = "ExternalOutput"
            else:
                kind = "ExternalInput"
            io[n] = nc.dram_tensor(n, shape, dt, kind=kind).ap()
    cx = Ctx(nc)
    for ph in phases:
        {1: phase1, 2: phase2, 3: phase3}[ph](cx, io)
    cx.P.flush()
    cx.P.wait_all_dma()
    return nc


_PROGS = {}
LAST_RES = None


def get_prog(phases):
    key = tuple(phases)
    if key not in _PROGS:
        _PROGS[key] = build_program(phases)
    return _PROGS[key]


def core_tokens(r, npos=8):
    return np.concatenate([np.arange(s * 512, (s + 1) * 512) for s in POS2SUB[r][:npos]])


def fm(a):
    t, f = a.shape
    return np.ascontiguousarray(a.T.reshape(f // 128, 128, t).transpose(1, 0, 2))


def prep_weights(inp):
    f = np.float32
    W = {}
    a_w_dkv = inp["a_w_dkv"][0]
    pe = a_w_dkv[:, 512:576]
    pe_sw = np.concatenate([pe[:, 32:64], pe[:, 0:32]], axis=1)
    W["WDKV"] = tiled(np.concatenate([a_w_dkv[:, :512], pe, pe_sw], axis=1), 16, 5)
    ukv = inp["a_w_ukv"][0].reshape(512, 16, 256)
    W["WUKVN"] = tiled(np.ascontiguousarray(ukv[:, :, :128]).reshape(512, 2048), 4, 16)
    W["WV"] = np.ascontiguousarray(ukv[:, :, 128:].reshape(4, 128, 2048).transpose(1, 0, 2))
    W["WDQ"] = tiled(inp["a_w_dq"][0], 16, 4)
    uq = inp["a_w_uq"][0].reshape(512, 16, 192)
    uq2 = np.concatenate([uq[:, :, :128], uq[:, :, 128:192], uq[:, :, 160:192], uq[:, :, 128:160]], axis=2)
    W["WUQ"] = tiled(np.ascontiguousarray(uq2).reshape(512, 16 * 256), 4, 16, 256)
    W["WO0"] = tiled(inp["a_w_o"][0], 16, 16)
    W["WO1"] = tiled(inp["b_w_o"][0], 16, 16)
    W["BWQ"] = tiled(inp["b_w_q"][0], 16, 16)
    W["SWK"] = tiled(inp["s_w_k"], 16, 16)
    W["SWV"] = np.ascontiguousarray(inp["s_w_v"].reshape(16, 128, 4, 512).transpose(2, 1, 0, 3))
    for l in range(2):
        W["WG%d" % l] = tiled(inp["f_w_gate"][l], 16, NFF)
        W["WU%d" % l] = tiled(inp["f_w_up"][l], 16, NFF)
        W["WD%d" % l] = np.ascontiguousarray(inp["f_w_down"][l].reshape(NFF // 4, 4, 128, 2048).transpose(0, 2, 1, 3))
        W["EG%d" % l] = tiled(inp["e_w_gate"][l], 16, 16)
        W["EP%d" % l] = tiled(inp["e_w_proj"][l], 2, 16)
    W["relb"] = np.ascontiguousarray(inp["b_rel_bias"][0])
    gv = np.zeros((128, NGV), f)

    def put(name, vec):
        c0, k = GVI[name]
        v = np.asarray(vec, f)
        if v.size == 128 * k:
            gv[:, c0:c0 + k] = v.reshape(k, 128).T
        else:
            gv[:v.size, c0] = v
    put("a_norm", inp["a_norm"][0]); put("a_g_q", inp["a_g_q"][0]); put("a_g_kv", inp["a_g_kv"][0])
    put("f_norm0", inp["f_norm"][0]); put("e_norm0", inp["e_norm"][0]); put("s_norm", inp["s_norm"])
    put("b_norm", inp["b_norm"][0]); put("f_norm1", inp["f_norm"][1]); put("e_norm1", inp["e_norm"][1])
    gq, gk = inp["a_g_qn"][0], inp["a_g_kn"][0]
    put("gqn_n", gq[:128]); put("gkn_n", gk[:128]); put("s_gkn", inp["s_g_kn"]); put("b_gqn", inp["b_g_qn"][0])
    sw = (np.arange(64) + 32) % 64
    put("gqr", gq[128:192]); put("gqsw", gq[128:192][sw]); put("gkr", gk[128:192]); put("gksw", gk[128:192][sw])
    invf = (np.float32(10000.0) ** (-np.arange(0, 64, 2, dtype=f) / f(64))).astype(f)
    put("invf", invf[np.arange(64) % 32])
    put("sgn", np.where(np.arange(64) < 32, -1.0, 1.0))
    W["gv"] = gv
    return W


def core_masks(r):
    mk = np.zeros((128, NMK), np.float32)
    p2s = POS2SUB[r]
    for s in range(NL0):
        for i, pp in enumerate(L0_PAST[s]):
            if not (p2s[pp] < p2s[s]):
                mk[:, MK0[(s, i)]] = NEG
    for s in range(1, NL0):
        for i, pp in enumerate(L1_PREV[s]):
            if not (p2s[pp] == p2s[s] - 1):
                mk[:, MK1[(s, i)]] = NEG
    return mk


def run(phases, in_maps):
    nc = get_prog(phases)
    names = set()
    produced = set()
    for ph in phases:
        names.update(PHASE_IN[ph])
        produced.update(PHASE_OUT[ph])
    maps = [{k: v for k, v in m.items() if k in names and k not in produced} for m in in_maps]
    res = run_bass_kernel_spmd(nc, maps, core_ids=list(range(8)))
    global LAST_RES
    LAST_RES = res
    return res.results


def make_maps(inp):
    W = prep_weights(inp)
    x, p, positions = inp["x"], inp["p"], inp["positions"]
    maps = []
    for c in range(8):
        b, r = c // 2, c % 2
        tk = core_tokens(r)
        m = dict(W)
        m["xT"] = fm(x[b][tk])
        m["pos"] = np.ascontiguousarray(positions[b][tk]).astype(np.int32)
        m["pT0"] = fm(p[0, b][tk[:T2]])
        m["pT1"] = fm(p[1, b][tk[:T2]])
        m["mk"] = core_masks(r)
        maps.append(m)
    return maps


def kernel(**inp):
    inp = {k: np.asarray(v) for k, v in inp.items()}
    maps = make_maps(inp)
    if FUSED:
        r3 = run([1, 2, 3], maps)
    else:
        r1 = run([1], maps)
        for c in range(8):
            maps[c].update({n: r1[c][n] for n in PHASE_OUT[1]})
        r2 = run([2], maps)
        for c in range(8):
            maps[c].update({n: r2[c][n] for n in PHASE_OUT[2]})
        r3 = run([3], maps)
    out = np.zeros((NB, SEQ, D), np.float32)
    for c in range(8):
        b, r = c // 2, c % 2
        o = np.asarray(r3[c]["outT"])
        out[b][core_tokens(r, NL0)[512:]] = o.transpose(2, 1, 0).reshape(T, D)
    return out
```

```python
import contextlib
import os
import numpy as np
import ml_dtypes
import concourse.bass as bass
import concourse.mybir as mybir
from concourse.bass_utils import run_bass_kernel_spmd

F32 = mybir.dt.float32
BF16 = mybir.dt.bfloat16
I32 = mybir.dt.int32
AF = mybir.ActivationFunctionType
ALU = mybir.AluOpType

D = 2048
SEQ = 4096
NB = 4
T = 2048
G = 1024
DFF = 5632
NFF = DFF // 128
EPS = 1e-6
NEG = -30000.0
POS2SUB = {0: [5, 0, 1, 6, 7, 2, 3, 4], 1: [1, 2, 3, 4, 5, 0, 6, 7]}
NL0 = 5
T1 = 4096
T2 = NL0 * 512
L0_GROUPS = [(0, 1), (2, 3), (4,)]
L1_GROUPS = [(1, 2), (3, 4)]
L0_PAST = {}
for _i in range(NL0):
    _u = set()
    for _r in (0, 1):
        _u |= {p for p in range(8) if POS2SUB[_r][p] < POS2SUB[_r][_i]}
    L0_PAST[_i] = sorted(_u)
L1_PREV = {}
for _i in range(1, NL0):
    _u = set()
    for _r in (0, 1):
        _ps = POS2SUB[_r][_i] - 1
        if _ps >= 0:
            _u.add(POS2SUB[_r].index(_ps))
    assert all(p < NL0 for p in _u)
    L1_PREV[_i] = sorted(_u)
FUSED = os.environ.get('KUNFUSED') is None
STOP = int(os.environ.get('KSTOP', '99'))

GV_SPEC = [("a_norm", 16), ("a_g_q", 4), ("a_g_kv", 4), ("f_norm0", 16), ("e_norm0", 16), ("s_norm", 16),
           ("b_norm", 16), ("f_norm1", 16), ("e_norm1", 16), ("gqn_n", 1), ("gkn_n", 1), ("s_gkn", 1), ("b_gqn", 1),
           ("gqr", 1), ("gqsw", 1), ("gkr", 1), ("gksw", 1), ("invf", 1), ("sgn", 1)]
GVI = {}
_c = 0
for _n, _k in GV_SPEC:
    GVI[_n] = (_c, _k)
    _c += _k
NGV = _c
MK0 = {}
_c = 0
for _s in range(NL0):
    for _i in range(len(L0_PAST[_s])):
        MK0[(_s, _i)] = _c
        _c += 1
MK1 = {}
for _s in range(1, NL0):
    for _i in range(len(L1_PREV[_s])):
        MK1[(_s, _i)] = _c
        _c += 1
NMK = _c


class R:
    __slots__ = ("name", "w", "rs", "excl")

    def __init__(self, name="", excl=False):
        self.name = name
        self.w = None
        self.rs = []
        self.excl = excl


class Prog:
    NDMASEM = 12

    def __init__(self, nc):
        self.nc = nc
        self.engs = {"pe": nc.tensor, "act": nc.scalar, "dve": nc.vector, "pool": nc.gpsimd, "sp": nc.sync}
        self.ops = []
        self.floor = 0
        self.esem = {k: nc.alloc_semaphore(name="es_" + k) for k in self.engs}
        self.ecount = {k: 0 for k in self.engs}
        self.dsem = {k: [nc.alloc_semaphore(name="ds_%s_%d" % (k, j)) for j in range(self.NDMASEM)] for k in ("sp", "pool")}
        self.dcount = {k: 0 for k in self.dsem}
        self.tok = []
        self.waited = {k: {} for k in self.engs}
        self.nwaits = 0

    def op(self, eng, fn, reads=(), writes=(), dma=False):
        i = len(self.ops)
        deps = set()
        fl = self.floor
        if any(r.excl for r in reads):
            writes = list(writes) + [r for r in reads if r.excl]
            reads = [r for r in reads if not r.excl]
        for r in reads:
            if r.w is not None and r.w >= fl:
                deps.add(r.w)
        for r in writes:
            if r.w is not None and r.w >= fl:
                deps.add(r.w)
            for x in r.rs:
                if x >= fl:
                    deps.add(x)
        for r in reads:
            r.rs.append(i)
        for r in writes:
            r.w = i
            r.rs = []
        self.ops.append((eng, fn, deps, dma))
        return i

    def dma(self, q, out, in_, reads=(), writes=()):
        return self.op(q, lambda e: e.dma_start(out=out, in_=in_), reads, writes, dma=True)

    def flush(self):
        ops = self.ops
        start = len(self.tok)
        n = len(ops)
        if start == n:
            return
        needed = {}
        last = {}
        for i in range(start, n):
            eng, fn, deps, dma = ops[i]
            if not dma:
                last[eng] = i
            for d in deps:
                p = ops[d]
                if (not p[3]) and (not dma) and p[0] == "pe" and eng == "pe":
                    continue
                needed[d] = True
        for ek, i in last.items():
            needed[i] = True
        for i in range(start, n):
            ek, fn, deps, dma = ops[i]
            e = self.engs[ek]
            for d in sorted(deps):
                p = ops[d]
                if (not p[3]) and (not dma) and p[0] == "pe" and ek == "pe":
                    continue
                key, sem, val = self.tok[d]
                if self.waited[ek].get(key, 0) >= val:
                    continue
                e.wait_ge(sem, val)
                self.nwaits += 1
                self.waited[ek][key] = val
            if dma:
                k = self.dcount[ek]
                self.dcount[ek] += 1
                j = k % self.NDMASEM
                sem = self.dsem[ek][j]
                key = ("d", ek, j)
                prev = 16 * (k // self.NDMASEM)
                if prev > 0 and self.waited[ek].get(key, 0) < prev:
                    e.wait_ge(sem, prev)
                    self.waited[ek][key] = prev
                ins = fn(e)
                ins.then_inc(sem, 16)
                self.tok.append((key, sem, prev + 16))
            else:
                ins = fn(e)
                if needed.get(i, False):
                    self.ecount[ek] += 1
                    ins.then_inc(self.esem[ek], 1)
                    self.tok.append((("e", ek), self.esem[ek], self.ecount[ek]))
                else:
                    self.tok.append(None)
        for i in range(start, n):
            if self.tok[i] is None:
                self.tok[i] = self.tok[last[ops[i][0]]]

    def wait_all_dma(self):
        for ek in self.dsem:
            e = self.engs[ek]
            k = self.dcount[ek]
            for j in range(self.NDMASEM):
                cnt = (k - j + self.NDMASEM - 1) // self.NDMASEM if k > j else 0
                key = ("d", ek, j)
                if cnt > 0 and self.waited[ek].get(key, 0) < 16 * cnt:
                    e.wait_ge(self.dsem[ek][j], 16 * cnt)
                    self.waited[ek][key] = 16 * cnt

    def barrier(self):
        self.flush()
        self.wait_all_dma()
        self.nc.all_engine_barrier()
        self.floor = len(self.ops)


class Ctx:
    def __init__(self, nc):
        self.nc = nc
        self.P = Prog(nc)
        self.banks = []
        for i in range(8):
            t = nc.alloc_psum_tensor("ps%d" % i, [128, 512], F32)
            self.banks.append((t, R("ps%d" % i, excl=True)))
        self.pair_i = 0
        self.stacks = []
        self.uid = 0
        self.ns = 2

    @property
    def gw(self):
        return 512 * self.ns

    @contextlib.contextmanager
    def scope(self):
        st = contextlib.ExitStack()
        self.stacks.append(st)
        try:
            yield
            self.P.barrier()
        finally:
            self.stacks.pop()
            st.close()

    def alloc(self, name, shape, dtype):
        self.uid += 1
        return self.stacks[-1].enter_context(self.nc.sbuf_tensor("%s_%d" % (name, self.uid), shape, dtype))

    def bankpair(self):
        i = self.pair_i
        self.pair_i = (i + 1) % 4
        return [self.banks[2 * i], self.banks[2 * i + 1]][:self.ns]

    def init_wring(self, nw=4, nslab=2):
        self.wring = [(self.alloc("wr", [128, 2048], BF16), R("wr%d" % i)) for i in range(nw)]
        self.wi = 0
        self.slabs = [(self.alloc("ws", [128, 8192], BF16), R("ws%d" % i)) for i in range(nslab)]
        self.si = 0

    def load_w(self, src, kc, m):
        t, r = self.wring[self.wi]
        self.wi = (self.wi + 1) % len(self.wring)
        v = t[:, 0:kc * m].rearrange("p (k m) -> p k m", k=kc)
        self.P.dma("pool", v, src, writes=[r])
        return v, r

    def load_slab(self, src, a, b):
        t, r = self.slabs[self.si]
        self.si = (self.si + 1) % len(self.slabs)
        v = t[:, 0:a * b].rearrange("p (a b) -> p a b", a=a)
        self.P.dma("pool", v, src, writes=[r])
        return v, r


def mm(cx, out, lhsT, rhs, start, stop, reads, wr):
    cx.P.op("pe", lambda e: e.matmul(out, lhsT=lhsT, rhs=rhs, start=start, stop=stop), reads=reads, writes=[wr])


def act(cx, out, in_, func, reads, writes, scale=None, bias=None):
    kw = {}
    if scale is not None:
        kw["scale"] = scale
    if bias is not None:
        kw["bias"] = bias
    cx.P.op("act", lambda e: e.activation(out=out, in_=in_, func=func, **kw), reads=reads, writes=writes)


def tt(cx, out, in0, in1, op, reads, writes, eng="dve"):
    cx.P.op(eng, lambda e: e.tensor_tensor(out=out, in0=in0, in1=in1, op=op), reads=reads, writes=writes)


def ts(cx, out, in0, s1, s2, op0, op1, reads, writes, eng="dve"):
    if op1 is None:
        cx.P.op(eng, lambda e: e.tensor_scalar(out=out, in0=in0, scalar1=s1, scalar2=None, op0=op0), reads=reads, writes=writes)
    else:
        cx.P.op(eng, lambda e: e.tensor_scalar(out=out, in0=in0, scalar1=s1, scalar2=s2, op0=op0, op1=op1), reads=reads, writes=writes)


def stt(cx, out, in0, scalar, in1, op0, op1, reads, writes):
    cx.P.op("dve", lambda e: e.scalar_tensor_tensor(out=out, in0=in0, scalar=scalar, in1=in1, op0=op0, op1=op1),
            reads=reads, writes=writes)


def cpy(cx, eng, out, in_, reads, writes):
    if eng == "act":
        act(cx, out, in_, AF.Copy, reads, writes)
    else:
        cx.P.op(eng, lambda e: e.tensor_copy(out=out, in_=in_), reads=reads, writes=writes)


def gcol(cx, name, i=0, p=128):
    c0, _ = GVI[name]
    return cx.gv[0:p, c0 + i:c0 + i + 1]


def rstd_from_banks(cx, banks, dim):
    for s, (bt, br) in enumerate(banks):
        act(cx, cx.lnt[:, s * 512:(s + 1) * 512], bt[:, :], AF.Ln, [br, cx.r_const], [cx.r_lnt], scale=1.0 / dim,
            bias=cx.epsc[:, 0:1])
    act(cx, cx.rstd[:, 0:cx.gw], cx.lnt[:, 0:cx.gw], AF.Exp, [cx.r_lnt], [cx.r_rstd], scale=-0.5)


def next_sq(cx):
    sq, rsq = cx.sqring[cx.sqi]
    cx.sqi = (cx.sqi + 1) % len(cx.sqring)
    return sq, rsq


def norm_fm(cx, srcs, gname, outs, dim):
    banks = cx.bankpair()
    n = len(srcs)
    for i, (s, rs) in enumerate(srcs):
        sq, rsq = next_sq(cx)
        act(cx, sq[:, 0:cx.gw], s[:, 0:cx.gw], AF.Square, [rs], [rsq])
        for sb, (bt, br) in enumerate(banks):
            mm(cx, bt[:, :], cx.ones[:, :], sq[:, sb * 512:(sb + 1) * 512], i == 0, i == n - 1, [rsq, cx.r_const], br)
    rstd_from_banks(cx, banks, dim)
    for i, ((s, rs), (o, ro)) in enumerate(zip(srcs, outs)):
        stt(cx, o[:, 0:cx.gw], s[:, 0:cx.gw], gcol(cx, gname, i), cx.rstd[:, 0:cx.gw], ALU.mult, ALU.mult,
            [rs, cx.r_rstd, cx.r_const], [ro])


def head_norm(cx, ba, extra, dim, gname, out, r_out_list):
    sq, rsq = next_sq(cx)
    bb = cx.bankpair()
    for s, (bt, br) in enumerate(ba):
        act(cx, sq[:, s * 512:(s + 1) * 512], bt[:, :], AF.Square, [br], [rsq])
    for s, (bt, br) in enumerate(bb):
        sl = slice(s * 512, (s + 1) * 512)
        mm(cx, bt[:, :], cx.ones[:, :], sq[:, sl], True, extra is None, [rsq, cx.r_const], br)
        if extra is not None:
            mm(cx, bt[:, :], cx.ones[0:64, :], extra[0][:, sl], False, True, [extra[1], cx.r_const], br)
    rstd_from_banks(cx, bb, dim)
    for s, (bt, br) in enumerate(ba):
        sl = slice(s * 512, (s + 1) * 512)
        stt(cx, out[:, sl], bt[:, :], gcol(cx, gname), cx.rstd[:, sl], ALU.mult, ALU.mult, [br, cx.r_rstd, cx.r_const],
            r_out_list)


def linear_fm(cx, wtile, kc_n, ins, n_oc, epilogue, m=128):
    for oc in range(n_oc):
        wt, rw = cx.load_w(wtile(oc), kc_n, m)
        banks = cx.bankpair()
        for kc in range(kc_n):
            a, ra = ins[kc]
            for s, (bt, br) in enumerate(banks):
                mm(cx, bt[0:m, :], wt[:, kc, :], a[:, s * 512:(s + 1) * 512], kc == 0, kc == kc_n - 1, [rw, ra], br)
        epilogue(oc, banks)


def setup_consts(cx, io, rope, nt=0):
    P = cx.P
    cx.r_const = R("const")
    cx.ones = cx.alloc("ones", [128, 128], BF16)
    cx.gv = cx.alloc("gv", [128, NGV], F32)
    cx.mk = cx.alloc("mk", [128, NMK], F32)
    cx.epsc = cx.alloc("epsc", [128, 2], F32)
    P.op("dve", lambda e: e.memset(cx.ones[:, :], 1.0), writes=[cx.r_const])
    P.op("dve", lambda e: e.memset(cx.epsc[:, 0:1], EPS), writes=[cx.r_const])
    P.op("dve", lambda e: e.memset(cx.epsc[:, 1:2], 0.0), writes=[cx.r_const])
    cx.zeroc = cx.epsc[:, 1:2]
    P.dma("sp", cx.gv[:, :], io["gv"], writes=[cx.r_const])
    P.dma("sp", cx.mk[:, :], io["mk"], writes=[cx.r_const])
    cx.sqring = [(cx.alloc("sq", [128, 1024], BF16), R("sq%d" % i)) for i in range(2)]
    cx.sqi = 0
    cx.rstd = cx.alloc("rstd", [128, 1024], F32)
    cx.r_rstd = R("rstd")
    cx.lnt = cx.alloc("lnt", [128, 1024], F32)
    cx.r_lnt = R("lnt")
    if rope is not None:
        cx.ropeC = cx.alloc("ropeC", [64, nt], F32)
        cx.ropeS = cx.alloc("ropeS", [64, nt], F32)
        cx.r_rope = R("rope")
        build_rope(cx, io, "g%sr" % rope, "g%ssw" % rope, nt)


def build_rope(cx, io, gr, gsw, nt):
    P = cx.P
    PI = float(np.pi)
    rr = cx.r_rope
    with cx.scope():
        posi = cx.alloc("posi", [64, nt], I32)
        ang = cx.alloc("ang", [64, nt], F32)
        kf = cx.alloc("kf", [64, nt], F32)
        ki = cx.alloc("ki", [64, nt], I32)
        r = cx.alloc("r", [64, nt], F32)
        m = cx.alloc("m", [64, nt], F32)
        P.dma("sp", posi[:, :], io["pos"][0:nt].partition_broadcast(64), writes=[rr])
        P.op("dve", lambda e: e.tensor_copy(out=ang[:, :], in_=posi[:, :]), reads=[rr], writes=[rr])
        ts(cx, ang[:, :], ang[:, :], gcol(cx, "invf", 0, 64), None, ALU.mult, None, [rr, cx.r_const], [rr])
        ts(cx, kf[:, :], ang[:, :], 1.0 / (2 * PI), None, ALU.mult, None, [rr], [rr])
        P.op("dve", lambda e: e.tensor_copy(out=ki[:, :], in_=kf[:, :]), reads=[rr], writes=[rr])
        P.op("dve", lambda e: e.tensor_copy(out=kf[:, :], in_=ki[:, :]), reads=[rr], writes=[rr])
        C1 = 6.28125
        C2 = float(2 * np.pi - 6.28125)
        stt(cx, r[:, :], kf[:, :], -C1, ang[:, :], ALU.mult, ALU.add, [rr], [rr])
        stt(cx, r[:, :], kf[:, :], -C2, r[:, :], ALU.mult, ALU.add, [rr], [rr])

        def wrap(x):
            ts(cx, m[:, :], x, PI, -2 * PI, ALU.is_gt, ALU.mult, [rr], [rr])
            tt(cx, x, x, m[:, :], ALU.add, [rr], [rr])
            ts(cx, m[:, :], x, -PI, 2 * PI, ALU.is_lt, ALU.mult, [rr], [rr])
            tt(cx, x, x, m[:, :], ALU.add, [rr], [rr])
            ts(cx, x, x, PI, -PI, ALU.min, ALU.max, [rr], [rr])

        wrap(r[:, :])
        act(cx, kf[:, :], r[:, :], AF.Sin, [rr], [rr])
        ts(cx, cx.ropeS[:, :], kf[:, :], gcol(cx, gsw, 0, 64), gcol(cx, "sgn", 0, 64), ALU.mult, ALU.mult,
           [rr, cx.r_const], [rr])
        ts(cx, r[:, :], r[:, :], PI / 2, None, ALU.add, None, [rr], [rr])
        wrap(r[:, :])
        act(cx, kf[:, :], r[:, :], AF.Sin, [rr], [rr])
        ts(cx, cx.ropeC[:, :], kf[:, :], gcol(cx, gr, 0, 64), None, ALU.mult, None, [rr, cx.r_const], [rr])


def alloc_stream(cx):
    cx.xT = cx.alloc("xT", [128, 16, G], F32)
    cx.rx = [R("x%d" % c) for c in range(16)]
    cx.hT = cx.alloc("hT", [128, 16, G], BF16)
    cx.rh = [R("h%d" % c) for c in range(16)]


def xchunks(cx):
    return [(cx.xT[:, c, :], cx.rx[c]) for c in range(16)]


def hchunks(cx):
    return [(cx.hT[:, c, :], cx.rh[c]) for c in range(16)]


def residual_epilogue(cx):
    def ep(oc, banks):
        for s, (bt, br) in enumerate(banks):
            xs = cx.xT[:, oc, s * 512:(s + 1) * 512]
            tt(cx, xs, xs, bt[:, :], ALU.add, [cx.rx[oc], br], [cx.rx[oc]])
    return ep


def slab_f32(cx, i):
    t, r = cx.slabs[i]
    return t[:, :].bitcast(F32).rearrange("p (a b) -> p a b", a=4), r


def phase1(cx, io):
    P = cx.P
    with cx.scope():
        setup_consts(cx, io, "k", T1)
        Ck, Sk = cx.ropeC, cx.ropeS
        cx.ns = 2
        cx.init_wring(nw=2)
        alloc_stream(cx)
        ckvf, r_ckvf = slab_f32(cx, 1)
        ckvT = cx.alloc("ckvT", [128, 4, G], BF16)
        r_ckvT = [R("ckvT%d" % i) for i in range(4)]
        krf = cx.alloc("krf", [64, G], F32)
        r_krf = R("krf")
        sqpe = cx.alloc("sqpe", [64, G], BF16)
        r_sqpe = R("sqpe")
        vt = [(cx.alloc("vt", [128, 512], BF16), R("vt%d" % i)) for i in range(2)]
        kno = [(cx.alloc("kno", [128, G], BF16), R("kno%d" % i)) for i in range(2)]
        kro = [(cx.alloc("kro", [64, G], BF16), R("kro%d" % i)) for i in range(2)]
        t1 = cx.lnt[0:64, :]
        r_t1 = cx.r_lnt
        r_out = R("p1out")
        if STOP == 0:
            return
        for g in range(T1 // G):
            tok = slice(g * G, (g + 1) * G)
            for c_ in range(16):
                P.dma("sp", cx.xT[:, c_, :], io["xT"][:, c_, tok], writes=[cx.rx[c_]])
            if STOP == 1:
                return
            norm_fm(cx, xchunks(cx), "a_norm", hchunks(cx), D)
            if STOP == 2:
                return

            def ep_ckv(oc, banks):
                for s, (bt, br) in enumerate(banks):
                    cpy(cx, "act" if s == 0 else "dve", ckvf[:, oc, s * 512:(s + 1) * 512], bt[:, :], [br], [r_ckvf])
            linear_fm(cx, lambda oc: io["WDKV"][oc], 16, hchunks(cx), 4, ep_ckv)
            if STOP == 3:
                return
            wt, rw = cx.load_w(io["WDKV"][4], 16, 128)
            bp = cx.bankpair()
            bs = cx.bankpair()
            for half, banks in ((0, bp), (1, bs)):
                for kc in range(16):
                    for s, (bt, br) in enumerate(banks):
                        mm(cx, bt[0:64, :], wt[:, kc, half * 64:(half + 1) * 64], cx.hT[:, kc, s * 512:(s + 1) * 512],
                           kc == 0, kc == 15, [rw, cx.rh[kc]], br)
            for s in range(2):
                sl = slice(s * 512, (s + 1) * 512)
                gsl = slice(g * G + s * 512, g * G + (s + 1) * 512)
                act(cx, sqpe[:, sl], bp[s][0][0:64, :], AF.Square, [bp[s][1]], [r_sqpe])
                tt(cx, krf[:, sl], bp[s][0][0:64, :], Ck[:, gsl], ALU.mult, [bp[s][1], cx.r_rope], [r_krf])
                tt(cx, t1[:, sl], bs[s][0][0:64, :], Sk[:, gsl], ALU.mult, [bs[s][1], cx.r_rope], [r_t1])
                tt(cx, krf[:, sl], krf[:, sl], t1[:, sl], ALU.add, [r_krf, r_t1], [r_krf])
            if STOP == 4:
                return
            norm_fm(cx, [(ckvf[:, i, :], r_ckvf) for i in range(4)], "a_g_kv",
                    [(ckvT[:, i, :], r_ckvT[i]) for i in range(4)], 512)
            if STOP == 5:
                return
            wv, r_wv = cx.load_slab(io["WV"], 4, 2048)
            assert r_wv is cx.slabs[0][1]
            cx.si = 0
            for tti in range(8):
                for cg in range(4):
                    bt, br = cx.banks[(tti * 4 + cg) % 8]
                    for kc in range(4):
                        mm(cx, bt[:, :], ckvT[:, kc, tti * 128:(tti + 1) * 128], wv[:, kc, cg * 512:(cg + 1) * 512],
                           kc == 0, kc == 3, [r_ckvT[kc], r_wv], br)
                    vtt, rvt = vt[cg % 2]
                    cpy(cx, "act" if cg % 2 == 0 else "dve", vtt[:, :], bt[:, :], [br], [rvt])
                    dst = io["VV"][cg * 4:(cg + 1) * 4, :, g * 8 + tti, :].rearrange("h p d -> p h d")
                    P.dma("sp", dst, vtt[:, :].rearrange("p (h d) -> p h d", h=4), reads=[rvt], writes=[])
            if STOP == 6:
                return
            def kmm(h):
                wt, rw = cx.load_w(io["WUKVN"][h], 4, 128)
                ba = cx.bankpair()
                for kc in range(4):
                    for s, (bt, br) in enumerate(ba):
                        mm(cx, bt[:, :], wt[:, kc, :], ckvT[:, kc, s * 512:(s + 1) * 512], kc == 0, kc == 3, [rw, r_ckvT[kc]], br)
                return ba
            ba_next = kmm(0)
            for h in range(16):
                ba = ba_next
                if h + 1 < 16:
                    ba_next = kmm(h + 1)
                ko, rko = kno[h % 2]
                kr_, rkr = kro[h % 2]
                head_norm(cx, ba, (sqpe, r_sqpe), 192, "gkn_n", ko, [rko])
                tt(cx, kr_[:, :], krf[:, :], cx.rstd[0:64, :], ALU.mult, [r_krf, cx.r_rstd], [rkr])
                P.dma("sp", io["KN"][h, :, tok], ko[:, :], reads=[rko], writes=[])
                P.dma("sp", io["KR"][h, :, tok], kr_[:, :], reads=[rkr], writes=[])


def ffn_block(cx, io, layer, fnorm):
    norm_fm(cx, xchunks(cx), fnorm, hchunks(cx), D)
    WG, WU, WD = io["WG%d" % layer], io["WU%d" % layer], io["WD%d" % layer]
    ns = cx.ns
    gb = [cx.banks[0], cx.banks[1]][:ns]
    ub = [cx.banks[2], cx.banks[3]][:ns]
    db = [[cx.banks[4], cx.banks[5]][:ns], [cx.banks[6], cx.banks[7]][:ns]]
    di = 0
    for sb in range(NFF // 4):
        aT, r_aT = cx.aT[sb % 2]
        for c in range(4):
            ff = sb * 4 + c
            wg, rwg = cx.load_w(WG[ff], 16, 128)
            wu, rwu = cx.load_w(WU[ff], 16, 128)
            sg, rsg = cx.sg[ff % 2]
            for kc in range(16):
                for s, (bt, br) in enumerate(gb):
                    mm(cx, bt[:, :], wg[:, kc, :], cx.hT[:, kc, s * 512:(s + 1) * 512], kc == 0, kc == 15, [rwg, cx.rh[kc]], br)
            for s, (bt, br) in enumerate(gb):
                act(cx, sg[:, s * 512:(s + 1) * 512], bt[:, :], AF.Silu, [br], [rsg])
            for kc in range(16):
                for s, (bt, br) in enumerate(ub):
                    mm(cx, bt[:, :], wu[:, kc, :], cx.hT[:, kc, s * 512:(s + 1) * 512], kc == 0, kc == 15, [rwu, cx.rh[kc]], br)
            for s, (bt, br) in enumerate(ub):
                sl = slice(s * 512, (s + 1) * 512)
                tt(cx, aT[:, c, sl], sg[:, sl], bt[:, :], ALU.mult, [rsg, br], [r_aT[c]])
        wd, rwd = cx.load_slab(WD[sb], 4, 2048)
        for oc in range(16):
            banks = db[di]
            di = 1 - di
            for c in range(4):
                for s, (bt, br) in enumerate(banks):
                    mm(cx, bt[:, :], wd[:, c, oc * 128:(oc + 1) * 128], aT[:, c, s * 512:(s + 1) * 512], c == 0, c == 3,
                       [rwd, r_aT[c]], br)
            for s, (bt, br) in enumerate(banks):
                xs = cx.xT[:, oc, s * 512:(s + 1) * 512]
                tt(cx, xs, xs, bt[:, :], ALU.add, [cx.rx[oc], br], [cx.rx[oc]])


def egate_block(cx, io, layer, enorm, tok):
    P = cx.P
    norm_fm(cx, xchunks(cx), enorm, hchunks(cx), D)
    EG, EP = io["EG%d" % layer], io["EP%d" % layer]
    P.dma("pool", cx.pT[:, :, 0:cx.gw], io["pT%d" % layer][:, :, tok], writes=[cx.r_pT])
    for oc in range(16):
        wg, rwg = cx.load_w(EG[oc], 16, 128)
        wp, rwp = cx.load_w(EP[oc], 2, 128)
        ba = cx.bankpair()
        bb = cx.bankpair()
        for kc in range(16):
            for s, (bt, br) in enumerate(ba):
                mm(cx, bt[:, :], wg[:, kc, :], cx.hT[:, kc, s * 512:(s + 1) * 512], kc == 0, kc == 15, [rwg, cx.rh[kc]], br)
        for kc in range(2):
            for s, (bt, br) in enumerate(bb):
                mm(cx, bt[:, :], wp[:, kc, :], cx.pT[:, kc, s * 512:(s + 1) * 512], kc == 0, kc == 1, [rwp, cx.r_pT], br)
        sg, rsg = cx.sg[oc % 2]
        for s in range(cx.ns):
            sl = slice(s * 512, (s + 1) * 512)
            act(cx, sg[:, sl], ba[s][0][:, :], AF.Sigmoid, [ba[s][1]], [rsg])
            tt(cx, sg[:, sl], sg[:, sl], bb[s][0][:, :], ALU.mult, [rsg, bb[s][1]], [rsg])
            xs = cx.xT[:, oc, sl]
            tt(cx, xs, xs, sg[:, sl], ALU.add, [cx.rx[oc], rsg], [cx.rx[oc]])


def alloc_ffn(cx):
    cx.aT = []
    for i in range(2):
        cx.aT.append((cx.alloc("aT", [128, 4, G], BF16), [R("aT%d_%d" % (i, c)) for c in range(4)]))
    cx.sg = [(cx.alloc("sg", [128, G], F32), R("sg%d" % i)) for i in range(2)]
    cx.pT = cx.alloc("pT", [128, 2, G], BF16)
    cx.r_pT = R("pT")


def attention(cx, h, sl, qn, rqn, qr, rqr, tiles, scale, bias_fn=None, st_banks=(0, 1), o_banks=(2, 3)):
    P = cx.P
    ob, orr = cx.banks[o_banks[0]]
    sb_, srr = cx.banks[o_banks[1]]
    n = len(tiles)
    nb = len(st_banks)
    la = nb - 1

    def qk(i):
        t = tiles[i]
        c0, c1 = t["c0"], t["c1"]
        stb, rst = cx.banks[st_banks[i % nb]]
        if qr is not None:
            mm(cx, stb[:, c0:c1], t["K"], qn[:, c0:c1], True, False, t["rK"] + [rqn], rst)
            mm(cx, stb[:, c0:c1], t["KR"], qr[:, c0:c1], False, True, t["rK"] + [rqr], rst)
        else:
            mm(cx, stb[:, c0:c1], t["K"], qn[:, c0:c1], True, True, t["rK"] + [rqn], rst)

    def rest(i):
        t = tiles[i]
        c0, c1 = t["c0"], t["c1"]
        stb, rst = cx.banks[st_banks[i % nb]]
        pt, rpt = cx.pring[cx.pi]
        cx.pi = (cx.pi + 1) % len(cx.pring)
        mcol = cx.mk[:, t["mask"]:t["mask"] + 1] if t["mask"] is not None else cx.zeroc
        if bias_fn is None:
            act(cx, pt[:, c0:c1], stb[:, c0:c1], AF.Exp, [rst, cx.r_const], [rpt], scale=scale, bias=mcol)
        else:
            tmp, rtmp = cx.tmpring[cx.ti]
            cx.ti = (cx.ti + 1) % len(cx.tmpring)
            b_ap, b_r = bias_fn(t)
            stt(cx, tmp[:, c0:c1], stb[:, c0:c1], scale, b_ap, ALU.mult, ALU.add, [rst, b_r], [rtmp])
            act(cx, pt[:, c0:c1], tmp[:, c0:c1], AF.Exp, [rtmp, cx.r_const], [rpt], scale=1.0, bias=mcol)
        if t["zero"] is not None:
            p0, p1, z0, z1 = t["zero"]
            cx.P.op("pool", lambda e, o=pt[p0:p1, z0:z1]: e.memset(o, 0.0), writes=[rpt])
        mm(cx, ob[:, c0:c1], t["V"], pt[:, c0:c1], i == 0, i == n - 1, t["rK"] + [rpt], orr)
        mm(cx, sb_[:, c0:c1], cx.ones[:, :], pt[:, c0:c1], i == 0, i == n - 1, [rpt, cx.r_const], srr)

    for i in range(min(la, n)):
        qk(i)
    for i in range(n):
        if i + la < n:
            qk(i + la)
        rest(i)
    act(cx, cx.rinv[:, :], sb_[:, :], AF.Ln, [srr], [cx.r_rinv])
    act(cx, cx.rinv[:, :], cx.rinv[:, :], AF.Exp, [cx.r_rinv], [cx.r_rinv], scale=-1.0)
    tt(cx, cx.hT[:, h, sl * 512:(sl + 1) * 512], ob[:, :], cx.rinv[:, :], ALU.mult, [orr, cx.r_rinv], [cx.rh[h]])


def alloc_attn(cx, npt=4):
    cx.pring = [(cx.alloc("pt", [128, 512], BF16), R("pt%d" % i)) for i in range(npt)]
    cx.pi = 0
    cx.rinv = cx.alloc("rinv", [128, 512], F32)
    cx.r_rinv = R("rinv")


def phase2(cx, io):
    P = cx.P
    with cx.scope():
        setup_consts(cx, io, "q", T2)
        Cq, Sq = cx.ropeC, cx.ropeS
        cx.ns = 2
        cx.init_wring(nw=3)
        alloc_stream(cx)
        r_out = R("p2out")
        sc0 = float(1.0 / np.sqrt(192.0))
        t0s, r0s = cx.slabs[0]
        t1s, r1s = cx.slabs[1]
        r_kr = [R("kr0"), R("kr1")]
        for grp in L0_GROUPS:
            cx.ns = len(grp)
            gw = cx.gw
            tok = slice(grp[0] * 512, grp[0] * 512 + gw)
            for c_ in range(16):
                P.dma("sp", cx.xT[:, c_, 0:gw], io["xT"][:, c_, tok], writes=[cx.rx[c_]])
            with cx.scope():
                alloc_attn(cx, 2)
                cqf, r_cqf = slab_f32(cx, 1)
                cqT = cx.alloc("cqT", [128, 4, G], BF16)
                r_cqT = [R("cqT%d" % i) for i in range(4)]
                qn = [(cx.alloc("qn", [128, 512], BF16), R("qn%d" % i)) for i in range(2)]
                qr = [(cx.alloc("qr", [64, 512], BF16), R("qr%d" % i)) for i in range(2)]
                sqn = cx.alloc("sqn", [128, 512], BF16)
                sqr = cx.alloc("sqr", [64, 512], BF16)
                r_sqq = R("sqq")
                rq = cx.rstd[:, 0:512]
                lq = cx.lnt[:, 0:512]
                r_rq, r_lq = cx.r_rstd, cx.r_lnt
                t1 = cx.lnt[0:64, 512:1024]
                t2 = cx.rstd[0:64, 512:1024]
                r_t1 = R("t1")
                r_t2 = R("t2")
                kvB = cx.alloc("kvB", [128, 8192], BF16)
                r_kvB = R("kvB")
                KVB = [(t0s, r0s), (kvB, r_kvB)]
                norm_fm(cx, xchunks(cx), "a_norm", hchunks(cx), D)

                def ep_cq(oc, banks):
                    for s_, (bt, br) in enumerate(banks):
                        cpy(cx, "act" if s_ == 0 else "dve", cqf[:, oc, s_ * 512:(s_ + 1) * 512], bt[:, :], [br], [r_cqf])
                linear_fm(cx, lambda oc: io["WDQ"][oc], 16, hchunks(cx), 4, ep_cq)
                norm_fm(cx, [(cqf[:, i, :], r_cqf) for i in range(4)], "a_g_q",
                        [(cqT[:, i, :], r_cqT[i]) for i in range(4)], 512)
                items = [(h, sl) for h in range(16) for sl in range(cx.ns)]
                wqs = {}

                def qpath(k):
                    h, sl = items[k]
                    if h not in wqs:
                        wqs[h] = cx.load_w(io["WUQ"][h], 4, 256)
                    wq, rwq = wqs[h]
                    slot = grp[sl]
                    csl = slice(sl * 512, (sl + 1) * 512)
                    gsl = slice(slot * 512, (slot + 1) * 512)
                    bA, rA = cx.banks[4]
                    bB, rB = cx.banks[5]
                    bC, rC = cx.banks[6]
                    bD, rD = cx.banks[7]
                    for kc in range(4):
                        mm(cx, bA[:, :], wq[:, kc, 0:128], cqT[:, kc, csl], kc == 0, kc == 3, [rwq, r_cqT[kc]], rA)
                    for kc in range(4):
                        mm(cx, bB[0:64, :], wq[:, kc, 128:192], cqT[:, kc, csl], kc == 0, kc == 3, [rwq, r_cqT[kc]], rB)
                    for kc in range(4):
                        mm(cx, bC[0:64, :], wq[:, kc, 192:256], cqT[:, kc, csl], kc == 0, kc == 3, [rwq, r_cqT[kc]], rC)
                    act(cx, sqn[:, :], bA[:, :], AF.Square, [rA], [r_sqq])
                    act(cx, sqr[:, :], bB[0:64, :], AF.Square, [rB], [r_sqq])
                    mm(cx, bD[:, :], cx.ones[:, :], sqn[:, :], True, False, [r_sqq, cx.r_const], rD)
                    mm(cx, bD[:, :], cx.ones[0:64, :], sqr[:, :], False, True, [r_sqq, cx.r_const], rD)
                    act(cx, lq, bD[:, :], AF.Ln, [rD, cx.r_const], [r_lq], scale=1.0 / 192, bias=cx.epsc[:, 0:1])
                    act(cx, rq, lq, AF.Exp, [r_lq], [r_rq], scale=-0.5)
                    qnt, rqn = qn[k % 2]
                    qrt, rqr = qr[k % 2]
                    stt(cx, qnt[:, :], bA[:, :], gcol(cx, "gqn_n"), rq, ALU.mult, ALU.mult, [rA, r_rq, cx.r_const], [rqn])
                    tt(cx, t1[:, :], bB[0:64, :], Cq[:, gsl], ALU.mult, [rB, cx.r_rope], [r_t1])
                    tt(cx, t2[:, :], bC[0:64, :], Sq[:, gsl], ALU.mult, [rC, cx.r_rope], [r_t2])
                    tt(cx, t1[:, :], t1[:, :], t2[:, :], ALU.add, [r_t1, r_t2], [r_t1])
                    tt(cx, qrt[:, :], t1[:, :], rq[0:64, :], ALU.mult, [r_t1, r_rq], [rqr])

                qpath(0)
                for k, (h, sl) in enumerate(items):
                    kvt, r_kv = KVB[h % 2]
                    KA = kvt[:, 0:4096]
                    VA = kvt[:, 4096:8192].rearrange("p (t d) -> p t d", t=32)
                    KRA = t1s[0:64, (h % 2) * 4096:(h % 2 + 1) * 4096]
                    r_kra = r_kr[h % 2]
                    if sl == 0:
                        P.dma("sp", KA, io["KN"][h], writes=[r_kv])
                        P.dma("sp", KRA, io["KR"][h], reads=r_cqT, writes=[r_kra])
                        P.dma("sp", VA, io["VV"][h], writes=[r_kv])
                    if k + 1 < len(items):
                        qpath(k + 1)
                    slot = grp[sl]
                    qnt, rqn = qn[k % 2]
                    qrt, rqr = qr[k % 2]
                    tiles = []
                    for t in range(4):
                        ks = slice(slot * 512 + t * 128, slot * 512 + (t + 1) * 128)
                        tiles.append(dict(K=KA[:, ks], KR=KRA[:, ks], V=VA[:, slot * 4 + t, :], rK=[r_kv, r_kra],
                                          c0=t * 128, c1=512, mask=None, zero=(64, 128, t * 128, t * 128 + 64)))
                    for i, pp in enumerate(L0_PAST[slot]):
                        for t in range(4):
                            ks = slice(pp * 512 + t * 128, pp * 512 + (t + 1) * 128)
                            tiles.append(dict(K=KA[:, ks], KR=KRA[:, ks], V=VA[:, pp * 4 + t, :], rK=[r_kv, r_kra],
                                              c0=0, c1=512, mask=MK0[(slot, i)], zero=None))
                    attention(cx, h, sl, qnt, rqn, qrt, rqr, tiles, sc0)
            with cx.scope():
                alloc_ffn(cx)
                linear_fm(cx, lambda oc: io["WO0"][oc], 16, hchunks(cx), 16, residual_epilogue(cx))
                ffn_block(cx, io, 0, "f_norm0")
                egate_block(cx, io, 0, "e_norm0", tok)
                P.dma("sp", io["xres"][:, :, tok], cx.xT[:, :, 0:gw], reads=cx.rx, writes=[r_out])
            with cx.scope():
                sko = [(cx.alloc("sko", [128, G], BF16), R("sko%d" % i)) for i in range(2)]
                svt = [(cx.alloc("svt", [128, 512], BF16), R("svt%d" % i)) for i in range(2)]
                norm_fm(cx, xchunks(cx), "s_norm", hchunks(cx), D)
                for h in range(16):
                    wt, rw = cx.load_w(io["SWK"][h], 16, 128)
                    ba = cx.bankpair()
                    for kc in range(16):
                        for s_, (bt, br) in enumerate(ba):
                            mm(cx, bt[:, :], wt[:, kc, :], cx.hT[:, kc, s_ * 512:(s_ + 1) * 512], kc == 0, kc == 15,
                               [rw, cx.rh[kc]], br)
                    ko, rko = sko[h % 2]
                    head_norm(cx, ba, None, 128, "s_gkn", ko, [rko])
                    P.dma("sp", io["SK"][h, :, tok], ko[:, 0:gw], reads=[rko], writes=[])
                for cg in range(4):
                    wsl, rws = cx.load_slab(io["SWV"][cg], 16, 512)
                    for tti in range(4 * cx.ns):
                        bt, br = cx.banks[(cg * 8 + tti) % 8]
                        for kc in range(16):
                            mm(cx, bt[:, :], cx.hT[:, kc, tti * 128:(tti + 1) * 128], wsl[:, kc, :], kc == 0, kc == 15,
                               [cx.rh[kc], rws], br)
                        v_, rv_ = svt[tti % 2]
                        cpy(cx, "act" if tti % 2 == 0 else "dve", v_[:, :], bt[:, :], [br], [rv_])
                        dst = io["SV"][cg * 4:(cg + 1) * 4, :, grp[0] * 4 + tti, :].rearrange("h p d -> p h d")
                        P.dma("sp", dst, v_[:, :].rearrange("p (h d) -> p h d", h=4), reads=[rv_], writes=[])
        cx.ns = 2


def phase3(cx, io):
    P = cx.P
    with cx.scope():
        setup_consts(cx, io, None)
        cx.ns = 2
        cx.init_wring()
        alloc_stream(cx)
        r_out = R("p3out")
        r_ext = R("ext")
        ext = io["ext"]
        tab = io["relb"]
        Z = io["Z"]
        with cx.scope():
            tab_sb = cx.alloc("tab_sb", [16, 513], F32)
            ext_sb = cx.alloc("ext_sb", [16, 1535], F32)
            P.dma("sp", tab_sb[:, :], tab, writes=[r_ext])
            P.op("dve", lambda e: e.memset(ext_sb[:, :], 0.0), writes=[r_ext])
            ts(cx, ext_sb[:, 0:255], ext_sb[:, 0:255], tab_sb[:, 0:1], None, ALU.add, None, [r_ext], [r_ext])
            ts(cx, ext_sb[:, 768:1535], ext_sb[:, 768:1535], tab_sb[:, 512:513], None, ALU.add, None, [r_ext], [r_ext])
            P.op("dve", lambda e: e.tensor_copy(out=ext_sb[:, 255:768], in_=tab_sb[:, :]), reads=[r_ext], writes=[r_ext])
            P.dma("sp", ext, ext_sb[:, :], reads=[r_ext], writes=[r_ext])
            srcb = bass.AP(tensor=ext.tensor, offset=0, ap=[[1535, 16], [0, 128], [1, 1535]])
            P.dma("sp", Z[:, :, 0:1535], srcb, reads=[r_ext], writes=[r_ext])
        sc1 = float(1.0 / np.sqrt(128.0))
        for gi, grp in enumerate(L1_GROUPS):
            tok = slice(grp[0] * 512, grp[0] * 512 + G)
            otok = slice(gi * G, (gi + 1) * G)
            for c_ in range(16):
                P.dma("sp", cx.xT[:, c_, :], io["xres"][:, c_, tok], writes=[cx.rx[c_]])
            with cx.scope():
                alloc_attn(cx)
                cx.tmpring = [(cx.alloc("tmp", [128, 512], F32), R("tmp%d" % i)) for i in range(2)]
                cx.ti = 0
                qT = cx.alloc("qT", [128, 16, G], BF16)
                r_qT = [R("qT%d" % i) for i in range(16)]
                TB = [(cx.alloc("TB", [128, 1024], F32), R("TB%d" % i)) for i in range(2)]
                norm_fm(cx, xchunks(cx), "b_norm", hchunks(cx), D)
                for h in range(16):
                    wt, rw = cx.load_w(io["BWQ"][h], 16, 128)
                    ba = cx.bankpair()
                    for kc in range(16):
                        for s_, (bt, br) in enumerate(ba):
                            mm(cx, bt[:, :], wt[:, kc, :], cx.hT[:, kc, s_ * 512:(s_ + 1) * 512], kc == 0, kc == 15,
                               [rw, cx.rh[kc]], br)
                    head_norm(cx, ba, None, 128, "b_gqn", qT[:, h, :], [r_qT[h]])
                for h in range(16):
                    t0s, r0s = cx.slabs[h % 2]
                    SKA = t0s[:, 0:T2]
                    SVA = t0s[:, 4096:4096 + T2].rearrange("p (t d) -> p t d", t=4 * NL0)
                    P.dma("sp", SKA, io["SK"][h], writes=[r0s])
                    P.dma("sp", SVA, io["SV"][h], writes=[r0s])
                    tb, rtb = TB[h % 2]
                    src = bass.AP(tensor=Z.tensor, offset=h * 128 * 1536 + 127, ap=[[1535, 128], [1, 1024]])
                    P.dma("sp", tb[:, :], src, reads=[r_ext], writes=[rtb])
                    for sl in range(2):
                        slot = grp[sl]
                        csl = slice(sl * 512, (sl + 1) * 512)
                        tiles = []
                        for t in range(4):
                            ks = slice(slot * 512 + t * 128, slot * 512 + (t + 1) * 128)
                            tiles.append(dict(K=SKA[:, ks], V=SVA[:, slot * 4 + t, :], rK=[r0s], rel=t,
                                              c0=t * 128, c1=512, mask=None, zero=(64, 128, t * 128, t * 128 + 64)))
                        for i, pp in enumerate(L1_PREV[slot]):
                            for t in range(4):
                                ks = slice(pp * 512 + t * 128, pp * 512 + (t + 1) * 128)
                                tiles.append(dict(K=SKA[:, ks], V=SVA[:, pp * 4 + t, :], rK=[r0s], rel=t - 4,
                                                  c0=0, c1=(t + 1) * 128, mask=MK1[(slot, i)],
                                                  zero=(0, 64, t * 128 + 64, t * 128 + 128)))

                        def bias_fn(t, tb=tb, rtb=rtb):
                            j0 = 384 - t["rel"] * 128
                            return tb[:, j0 + t["c0"]:j0 + t["c1"]], rtb
                        attention(cx, h, sl, qT[:, h, csl], r_qT[h], None, None, tiles, sc1, bias_fn,
                                  st_banks=(0, 1, 4, 5), o_banks=((2, 3) if (2 * h + sl) % 2 == 0 else (6, 7)))
            with cx.scope():
                alloc_ffn(cx)
                linear_fm(cx, lambda oc: io["WO1"][oc], 16, hchunks(cx), 16, residual_epilogue(cx))
                ffn_block(cx, io, 1, "f_norm1")
                egate_block(cx, io, 1, "e_norm1", tok)
                P.dma("sp", io["outT"][:, :, otok], cx.xT[:, :, :], reads=cx.rx, writes=[r_out])


def tiled(w, kc, oc, m=128):
    return np.ascontiguousarray(w.reshape(kc, 128, oc, m).transpose(2, 1, 0, 3))


DRAM_SPECS = {
    "xT": ([128, 16, T1], F32), "pos": ([T1], I32), "gv": ([128, NGV], F32), "mk": ([128, NMK], F32),
    "WDKV": ([5, 128, 16, 128], F32), "WUKVN": ([16, 128, 4, 128], F32), "WV": ([128, 4, 2048], F32),
    "KN": ([16, 128, T1], BF16), "KR": ([16, 64, T1], BF16), "VV": ([16, 128, 32, 128], BF16),
    "WDQ": ([4, 128, 16, 128], F32), "WUQ": ([16, 128, 4, 256], F32), "WO0": ([16, 128, 16, 128], F32),
    "WG0": ([NFF, 128, 16, 128], F32), "WU0": ([NFF, 128, 16, 128], F32), "WD0": ([NFF // 4, 128, 4, 2048], F32),
    "EG0": ([16, 128, 16, 128], F32), "EP0": ([16, 128, 2, 128], F32), "pT0": ([128, 2, T2], F32),
    "SWK": ([16, 128, 16, 128], F32), "SWV": ([4, 128, 16, 512], F32),
    "xres": ([128, 16, T2], F32), "SK": ([16, 128, T2], BF16), "SV": ([16, 128, 4 * NL0, 128], BF16),
    "BWQ": ([16, 128, 16, 128], F32), "WO1": ([16, 128, 16, 128], F32),
    "WG1": ([NFF, 128, 16, 128], F32), "WU1": ([NFF, 128, 16, 128], F32), "WD1": ([NFF // 4, 128, 4, 2048], F32),
    "EG1": ([16, 128, 16, 128], F32), "EP1": ([16, 128, 2, 128], F32), "pT1": ([128, 2, T2], F32),
    "relb": ([16, 513], F32), "ext": ([16, 1535], F32), "Z": ([16, 128, 1536], F32), "outT": ([128, 16, T], F32),
}
PHASE_IN = {
    1: ["xT", "pos", "gv", "mk", "WDKV", "WUKVN", "WV"],
    2: ["xT", "pos", "gv", "mk", "KN", "KR", "VV", "WDQ", "WUQ", "WO0", "WG0", "WU0", "WD0",
        "EG0", "EP0", "pT0", "SWK", "SWV"],
    3: ["xres", "gv", "mk", "SK", "SV", "BWQ", "WO1", "WG1", "WU1", "WD1", "EG1", "EP1", "pT1", "relb"],
}
PHASE_OUT = {1: ["KN", "KR", "VV"], 2: ["xres", "SK", "SV"], 3: ["outT"]}
PHASE_INT = {1: [], 2: [], 3: ["ext", "Z"]}


def build_program(phases):
    fused = len(phases) > 1
    nc = bass.Bass("TRN2", target_bir_lowering=False)
    io = {}
    internal = set()
    if fused:
        for ph in phases:
            internal.update(PHASE_OUT[ph])
        internal.discard("outT")
    for ph in phases:
        for n in PHASE_IN[ph] + PHASE_OUT[ph] + PHASE_INT[ph]:
            if n in io:
                continue
            shape, dt = DRAM_SPECS[n]
            if n in internal or n in PHASE_INT[ph]:
                kind = "Internal"
            elif n in PHASE_OUT[ph]:
                kind = "ExternalOutput"
            else:
                kind = "ExternalInput"
            io[n] = nc.dram_tensor(n, shape, dt, kind=kind).ap()
    cx = Ctx(nc)
    for ph in phases:
        {1: phase1, 2: phase2, 3: phase3}[ph](cx, io)
    cx.P.flush()
    cx.P.wait_all_dma()
    return nc


_PROGS = {}
LAST_RES = None


def get_prog(phases):
    key = tuple(phases)
    if key not in _PROGS:
        _PROGS[key] = build_program(phases)
    return _PROGS[key]


def core_tokens(r, npos=8):
    return np.concatenate([np.arange(s * 512, (s + 1) * 512) for s in POS2SUB[r][:npos]])


def fm(a):
    t, f = a.shape
    return np.ascontiguousarray(a.T.reshape(f // 128, 128, t).transpose(1, 0, 2))


def prep_weights(inp):
    f = np.float32
    W = {}
    a_w_dkv = inp["a_w_dkv"][0]
    pe = a_w_dkv[:, 512:576]
    pe_sw = np.concatenate([pe[:, 32:64], pe[:, 0:32]], axis=1)
    W["WDKV"] = tiled(np.concatenate([a_w_dkv[:, :512], pe, pe_sw], axis=1), 16, 5)
    ukv = inp["a_w_ukv"][0].reshape(512, 16, 256)
    W["WUKVN"] = tiled(np.ascontiguousarray(ukv[:, :, :128]).reshape(512, 2048), 4, 16)
    W["WV"] = np.ascontiguousarray(ukv[:, :, 128:].reshape(4, 128, 2048).transpose(1, 0, 2))
    W["WDQ"] = tiled(inp["a_w_dq"][0], 16, 4)
    uq = inp["a_w_uq"][0].reshape(512, 16, 192)
    uq2 = np.concatenate([uq[:, :, :128], uq[:, :, 128:192], uq[:, :, 160:192], uq[:, :, 128:160]], axis=2)
    W["WUQ"] = tiled(np.ascontiguousarray(uq2).reshape(512, 16 * 256), 4, 16, 256)
    W["WO0"] = tiled(inp["a_w_o"][0], 16, 16)
    W["WO1"] = tiled(inp["b_w_o"][0], 16, 16)
    W["BWQ"] = tiled(inp["b_w_q"][0], 16, 16)
    W["SWK"] = tiled(inp["s_w_k"], 16, 16)
    W["SWV"] = np.ascontiguousarray(inp["s_w_v"].reshape(16, 128, 4, 512).transpose(2, 1, 0, 3))
    for l in range(2):
        W["WG%d" % l] = tiled(inp["f_w_gate"][l], 16, NFF)
        W["WU%d" % l] = tiled(inp["f_w_up"][l], 16, NFF)
        W["WD%d" % l] = np.ascontiguousarray(inp["f_w_down"][l].reshape(NFF // 4, 4, 128, 2048).transpose(0, 2, 1, 3))
        W["EG%d" % l] = tiled(inp["e_w_gate"][l], 16, 16)
        W["EP%d" % l] = tiled(inp["e_w_proj"][l], 2, 16)
    W["relb"] = np.ascontiguousarray(inp["b_rel_bias"][0])
    gv = np.zeros((128, NGV), f)

    def put(name, vec):
        c0, k = GVI[name]
        v = np.asarray(vec, f)
        if v.size == 128 * k:
            gv[:, c0:c0 + k] = v.reshape(k, 128).T
        else:
            gv[:v.size, c0] = v
    put("a_norm", inp["a_norm"][0]); put("a_g_q", inp["a_g_q"][0]); put("a_g_kv", inp["a_g_kv"][0])
    put("f_norm0", inp["f_norm"][0]); put("e_norm0", inp["e_norm"][0]); put("s_norm", inp["s_norm"])
    put("b_norm", inp["b_norm"][0]); put("f_norm1", inp["f_norm"][1]); put("e_norm1", inp["e_norm"][1])
    gq, gk = inp["a_g_qn"][0], inp["a_g_kn"][0]
    put("gqn_n", gq[:128]); put("gkn_n", gk[:128]); put("s_gkn", inp["s_g_kn"]); put("b_gqn", inp["b_g_qn"][0])
    sw = (np.arange(64) + 32) % 64
    put("gqr", gq[128:192]); put("gqsw", gq[128:192][sw]); put("gkr", gk[128:192]); put("gksw", gk[128:192][sw])
    invf = (np.float32(10000.0) ** (-np.arange(0, 64, 2, dtype=f) / f(64))).astype(f)
    put("invf", invf[np.arange(64) % 32])
    put("sgn", np.where(np.arange(64) < 32, -1.0, 1.0))
    W["gv"] = gv
    return W


def core_masks(r):
    mk = np.zeros((128, NMK), np.float32)
    p2s = POS2SUB[r]
    for s in range(NL0):
        for i, pp in enumerate(L0_PAST[s]):
            if not (p2s[pp] < p2s[s]):
                mk[:, MK0[(s, i)]] = NEG
    for s in range(1, NL0):
        for i, pp in enumerate(L1_PREV[s]):
            if not (p2s[pp] == p2s[s] - 1):
                mk[:, MK1[(s, i)]] = NEG
    return mk


def run(phases, in_maps):
    nc = get_prog(phases)
    names = set()
    produced = set()
    for ph in phases:
        names.update(PHASE_IN[ph])
        produced.update(PHASE_OUT[ph])
    maps = [{k: v for k, v in m.items() if k in names and k not in produced} for m in in_maps]
    res = run_bass_kernel_spmd(nc, maps, core_ids=list(range(8)))
    global LAST_RES
    LAST_RES = res
    return res.results


def make_maps(inp):
    W = prep_weights(inp)
    x, p, positions = inp["x"], inp["p"], inp["positions"]
    maps = []
    for c in range(8):
        b, r = c // 2, c % 2
        tk = core_tokens(r)
        m = dict(W)
        m["xT"] = fm(x[b][tk])
        m["pos"] = np.ascontiguousarray(positions[b][tk]).astype(np.int32)
        m["pT0"] = fm(p[0, b][tk[:T2]])
        m["pT1"] = fm(p[1, b][tk[:T2]])
        m["mk"] = core_masks(r)
        maps.append(m)
    return maps


def kernel(**inp):
    inp = {k: np.asarray(v) for k, v in inp.items()}
    maps = make_maps(inp)
    if FUSED:
        r3 = run([1, 2, 3], maps)
    else:
        r1 = run([1], maps)
        for c in range(8):
            maps[c].update({n: r1[c][n] for n in PHASE_OUT[1]})
        r2 = run([2], maps)
        for c in range(8):
            maps[c].update({n: r2[c][n] for n in PHASE_OUT[2]})
        r3 = run([3], maps)
    out = np.zeros((NB, SEQ, D), np.float32)
    for c in range(8):
        b, r = c // 2, c % 2
        o = np.asarray(r3[c]["outT"])
        out[b][core_tokens(r, NL0)[512:]] = o.transpose(2, 1, 0).reshape(T, D)
    return out
```

```python
import contextlib
import os
import numpy as np
import ml_dtypes
import concourse.bass as bass
import concourse.mybir as mybir
from concourse.bass_utils import run_bass_kernel_spmd

F32 = mybir.dt.float32
BF16 = mybir.dt.bfloat16
I32 = mybir.dt.int32
AF = mybir.ActivationFunctionType
ALU = mybir.AluOpType

D = 2048
SEQ = 4096
NB = 4
T = 2048
G = 1024
DFF = 5632
NFF = DFF // 128
EPS = 1e-6
NEG = -30000.0
POS2SUB = {0: [5, 0, 1, 6, 7, 2, 3, 4], 1: [1, 2, 3, 4, 5, 0, 6, 7]}
NL0 = 5
T1 = 4096
T2 = NL0 * 512
L0_GROUPS = [(0, 1), (2, 3), (4,)]
L1_GROUPS = [(1, 2), (3, 4)]
L0_PAST = {}
for _i in range(NL0):
    _u = set()
    for _r in (0, 1):
        _u |= {p for p in range(8) if POS2SUB[_r][p] < POS2SUB[_r][_i]}
    L0_PAST[_i] = sorted(_u)
L1_PREV = {}
for _i in range(1, NL0):
    _u = set()
    for _r in (0, 1):
        _ps = POS2SUB[_r][_i] - 1
        if _ps >= 0:
            _u.add(POS2SUB[_r].index(_ps))
    assert all(p < NL0 for p in _u)
    L1_PREV[_i] = sorted(_u)
FUSED = os.environ.get('KUNFUSED') is None
STOP = int(os.environ.get('KSTOP', '99'))

GV_SPEC = [("a_norm", 16), ("a_g_q", 4), ("a_g_kv", 4), ("f_norm0", 16), ("e_norm0", 16), ("s_norm", 16),
           ("b_norm", 16), ("f_norm1", 16), ("e_norm1", 16), ("gqn_n", 1), ("gkn_n", 1), ("s_gkn", 1), ("b_gqn", 1),
           ("gqr", 1), ("gqsw", 1), ("gkr", 1), ("gksw", 1), ("invf", 1), ("sgn", 1)]
GVI = {}
_c = 0
for _n, _k in GV_SPEC:
    GVI[_n] = (_c, _k)
    _c += _k
NGV = _c
MK0 = {}
_c = 0
for _s in range(NL0):
    for _i in range(len(L0_PAST[_s])):
        MK0[(_s, _i)] = _c
        _c += 1
MK1 = {}
for _s in range(1, NL0):
    for _i in range(len(L1_PREV[_s])):
        MK1[(_s, _i)] = _c
        _c += 1
NMK = _c


class R:
    __slots__ = ("name", "w", "rs", "excl")

    def __init__(self, name="", excl=False):
        self.name = name
        self.w = None
        self.rs = []
        self.excl = excl


class Prog:
    NDMASEM = 12

    def __init__(self, nc):
        self.nc = nc
        self.engs = {"pe": nc.tensor, "act": nc.scalar, "dve": nc.vector, "pool": nc.gpsimd, "sp": nc.sync}
        self.ops = []
        self.floor = 0
        self.esem = {k: nc.alloc_semaphore(name="es_" + k) for k in self.engs}
        self.ecount = {k: 0 for k in self.engs}
        self.dsem = {k: [nc.alloc_semaphore(name="ds_%s_%d" % (k, j)) for j in range(self.NDMASEM)] for k in ("sp", "pool")}
        self.dcount = {k: 0 for k in self.dsem}
        self.tok = []
        self.waited = {k: {} for k in self.engs}
        self.nwaits = 0

    def op(self, eng, fn, reads=(), writes=(), dma=False):
        i = len(self.ops)
        deps = set()
        fl = self.floor
        if any(r.excl for r in reads):
            writes = list(writes) + [r for r in reads if r.excl]
            reads = [r for r in reads if not r.excl]
        for r in reads:
            if r.w is not None and r.w >= fl:
                deps.add(r.w)
        for r in writes:
            if r.w is not None and r.w >= fl:
                deps.add(r.w)
            for x in r.rs:
                if x >= fl:
                    deps.add(x)
        for r in reads:
            r.rs.append(i)
        for r in writes:
            r.w = i
            r.rs = []
        self.ops.append((eng, fn, deps, dma))
        return i

    def dma(self, q, out, in_, reads=(), writes=()):
        return self.op(q, lambda e: e.dma_start(out=out, in_=in_), reads, writes, dma=True)

    def flush(self):
        ops = self.ops
        start = len(self.tok)
        n = len(ops)
        if start == n:
            return
        needed = {}
        last = {}
        for i in range(start, n):
            eng, fn, deps, dma = ops[i]
            if not dma:
                last[eng] = i
            for d in deps:
                p = ops[d]
                if (not p[3]) and (not dma) and p[0] == "pe" and eng == "pe":
                    continue
                needed[d] = True
        for ek, i in last.items():
            needed[i] = True
        for i in range(start, n):
            ek, fn, deps, dma = ops[i]
            e = self.engs[ek]
            for d in sorted(deps):
                p = ops[d]
                if (not p[3]) and (not dma) and p[0] == "pe" and ek == "pe":
                    continue
                key, sem, val = self.tok[d]
                if self.waited[ek].get(key, 0) >= val:
                    continue
                e.wait_ge(sem, val)
                self.nwaits += 1
                self.waited[ek][key] = val
            if dma:
                k = self.dcount[ek]
                self.dcount[ek] += 1
                j = k % self.NDMASEM
                sem = self.dsem[ek][j]
                key = ("d", ek, j)
                prev = 16 * (k // self.NDMASEM)
                if prev > 0 and self.waited[ek].get(key, 0) < prev:
                    e.wait_ge(sem, prev)
                    self.waited[ek][key] = prev
                ins = fn(e)
                ins.then_inc(sem, 16)
                self.tok.append((key, sem, prev + 16))
            else:
                ins = fn(e)
                if needed.get(i, False):
                    self.ecount[ek] += 1
                    ins.then_inc(self.esem[ek], 1)
                    self.tok.append((("e", ek), self.esem[ek], self.ecount[ek]))
                else:
                    self.tok.append(None)
        for i in range(start, n):
            if self.tok[i] is None:
                self.tok[i] = self.tok[last[ops[i][0]]]

    def wait_all_dma(self):
        for ek in self.dsem:
            e = self.engs[ek]
            k = self.dcount[ek]
            for j in range(self.NDMASEM):
                cnt = (k - j + self.NDMASEM - 1) // self.NDMASEM if k > j else 0
                key = ("d", ek, j)
                if cnt > 0 and self.waited[ek].get(key, 0) < 16 * cnt:
                    e.wait_ge(self.dsem[ek][j], 16 * cnt)
                    self.waited[ek][key] = 16 * cnt

    def barrier(self):
        self.flush()
        self.wait_all_dma()
        self.nc.all_engine_barrier()
        self.floor = len(self.ops)


class Ctx:
    def __init__(self, nc):
        self.nc = nc
        self.P = Prog(nc)
        self.banks = []
        for i in range(8):
            t = nc.alloc_psum_tensor("ps%d" % i, [128, 512], F32)
            self.banks.append((t, R("ps%d" % i, excl=True)))
        self.pair_i = 0
        self.stacks = []
        self.uid = 0
        self.ns = 2

    @property
    def gw(self):
        return 512 * self.ns

    @contextlib.contextmanager
    def scope(self):
        st = contextlib.ExitStack()
        self.stacks.append(st)
        try:
            yield
            self.P.barrier()
        finally:
            self.stacks.pop()
            st.close()

    def alloc(self, name, shape, dtype):
        self.uid += 1
        return self.stacks[-1].enter_context(self.nc.sbuf_tensor("%s_%d" % (name, self.uid), shape, dtype))

    def bankpair(self):
        i = self.pair_i
        self.pair_i = (i + 1) % 4
        return [self.banks[2 * i], self.banks[2 * i + 1]][:self.ns]

    def init_wring(self, nw=4, nslab=2):
        self.wring = [(self.alloc("wr", [128, 2048], BF16), R("wr%d" % i)) for i in range(nw)]
        self.wi = 0
        self.slabs = [(self.alloc("ws", [128, 8192], BF16), R("ws%d" % i)) for i in range(nslab)]
        self.si = 0

    def load_w(self, src, kc, m):
        t, r = self.wring[self.wi]
        self.wi = (self.wi + 1) % len(self.wring)
        v = t[:, 0:kc * m].rearrange("p (k m) -> p k m", k=kc)
        self.P.dma("pool", v, src, writes=[r])
        return v, r

    def load_slab(self, src, a, b):
        t, r = self.slabs[self.si]
        self.si = (self.si + 1) % len(self.slabs)
        v = t[:, 0:a * b].rearrange("p (a b) -> p a b", a=a)
        self.P.dma("pool", v, src, writes=[r])
        return v, r


def mm(cx, out, lhsT, rhs, start, stop, reads, wr):
    cx.P.op("pe", lambda e: e.matmul(out, lhsT=lhsT, rhs=rhs, start=start, stop=stop), reads=reads, writes=[wr])


def act(cx, out, in_, func, reads, writes, scale=None, bias=None):
    kw = {}
    if scale is not None:
        kw["scale"] = scale
    if bias is not None:
        kw["bias"] = bias
    cx.P.op("act", lambda e: e.activation(out=out, in_=in_, func=func, **kw), reads=reads, writes=writes)


def tt(cx, out, in0, in1, op, reads, writes, eng="dve"):
    cx.P.op(eng, lambda e: e.tensor_tensor(out=out, in0=in0, in1=in1, op=op), reads=reads, writes=writes)


def ts(cx, out, in0, s1, s2, op0, op1, reads, writes, eng="dve"):
    if op1 is None:
        cx.P.op(eng, lambda e: e.tensor_scalar(out=out, in0=in0, scalar1=s1, scalar2=None, op0=op0), reads=reads, writes=writes)
    else:
        cx.P.op(eng, lambda e: e.tensor_scalar(out=out, in0=in0, scalar1=s1, scalar2=s2, op0=op0, op1=op1), reads=reads, writes=writes)


def stt(cx, out, in0, scalar, in1, op0, op1, reads, writes):
    cx.P.op("dve", lambda e: e.scalar_tensor_tensor(out=out, in0=in0, scalar=scalar, in1=in1, op0=op0, op1=op1),
            reads=reads, writes=writes)


def cpy(cx, eng, out, in_, reads, writes):
    if eng == "act":
        act(cx, out, in_, AF.Copy, reads, writes)
    else:
        cx.P.op(eng, lambda e: e.tensor_copy(out=out, in_=in_), reads=reads, writes=writes)


def gcol(cx, name, i=0, p=128):
    c0, _ = GVI[name]
    return cx.gv[0:p, c0 + i:c0 + i + 1]


def rstd_from_banks(cx, banks, dim):
    for s, (bt, br) in enumerate(banks):
        act(cx, cx.lnt[:, s * 512:(s + 1) * 512], bt[:, :], AF.Ln, [br, cx.r_const], [cx.r_lnt], scale=1.0 / dim,
            bias=cx.epsc[:, 0:1])
    act(cx, cx.rstd[:, 0:cx.gw], cx.lnt[:, 0:cx.gw], AF.Exp, [cx.r_lnt], [cx.r_rstd], scale=-0.5)


def next_sq(cx):
    sq, rsq = cx.sqring[cx.sqi]
    cx.sqi = (cx.sqi + 1) % len(cx.sqring)
    return sq, rsq


def norm_fm(cx, srcs, gname, outs, dim):
    banks = cx.bankpair()
    n = len(srcs)
    for i, (s, rs) in enumerate(srcs):
        sq, rsq = next_sq(cx)
        act(cx, sq[:, 0:cx.gw], s[:, 0:cx.gw], AF.Square, [rs], [rsq])
        for sb, (bt, br) in enumerate(banks):
            mm(cx, bt[:, :], cx.ones[:, :], sq[:, sb * 512:(sb + 1) * 512], i == 0, i == n - 1, [rsq, cx.r_const], br)
    rstd_from_banks(cx, banks, dim)
    for i, ((s, rs), (o, ro)) in enumerate(zip(srcs, outs)):
        stt(cx, o[:, 0:cx.gw], s[:, 0:cx.gw], gcol(cx, gname, i), cx.rstd[:, 0:cx.gw], ALU.mult, ALU.mult,
            [rs, cx.r_rstd, cx.r_const], [ro])


def head_norm(cx, ba, extra, dim, gname, out, r_out_list):
    sq, rsq = next_sq(cx)
    bb = cx.bankpair()
    for s, (bt, br) in enumerate(ba):
        act(cx, sq[:, s * 512:(s + 1) * 512], bt[:, :], AF.Square, [br], [rsq])
    for s, (bt, br) in enumerate(bb):
        sl = slice(s * 512, (s + 1) * 512)
        mm(cx, bt[:, :], cx.ones[:, :], sq[:, sl], True, extra is None, [rsq, cx.r_const], br)
        if extra is not None:
            mm(cx, bt[:, :], cx.ones[0:64, :], extra[0][:, sl], False, True, [extra[1], cx.r_const], br)
    rstd_from_banks(cx, bb, dim)
    for s, (bt, br) in enumerate(ba):
        sl = slice(s * 512, (s + 1) * 512)
        stt(cx, out[:, sl], bt[:, :], gcol(cx, gname), cx.rstd[:, sl], ALU.mult, ALU.mult, [br, cx.r_rstd, cx.r_const],
            r_out_list)


def linear_fm(cx, wtile, kc_n, ins, n_oc, epilogue, m=128):
    for oc in range(n_oc):
        wt, rw = cx.load_w(wtile(oc), kc_n, m)
        banks = cx.bankpair()
        for kc in range(kc_n):
            a, ra = ins[kc]
            for s, (bt, br) in enumerate(banks):
                mm(cx, bt[0:m, :], wt[:, kc, :], a[:, s * 512:(s + 1) * 512], kc == 0, kc == kc_n - 1, [rw, ra], br)
        epilogue(oc, banks)


def setup_consts(cx, io, rope, nt=0):
    P = cx.P
    cx.r_const = R("const")
    cx.ones = cx.alloc("ones", [128, 128], BF16)
    cx.gv = cx.alloc("gv", [128, NGV], F32)
    cx.mk = cx.alloc("mk", [128, NMK], F32)
    cx.epsc = cx.alloc("epsc", [128, 2], F32)
    P.op("dve", lambda e: e.memset(cx.ones[:, :], 1.0), writes=[cx.r_const])
    P.op("dve", lambda e: e.memset(cx.epsc[:, 0:1], EPS), writes=[cx.r_const])
    P.op("dve", lambda e: e.memset(cx.epsc[:, 1:2], 0.0), writes=[cx.r_const])
    cx.zeroc = cx.epsc[:, 1:2]
    P.dma("sp", cx.gv[:, :], io["gv"], writes=[cx.r_const])
    P.dma("sp", cx.mk[:, :], io["mk"], writes=[cx.r_const])
    cx.sqring = [(cx.alloc("sq", [128, 1024], BF16), R("sq%d" % i)) for i in range(2)]
    cx.sqi = 0
    cx.rstd = cx.alloc("rstd", [128, 1024], F32)
    cx.r_rstd = R("rstd")
    cx.lnt = cx.alloc("lnt", [128, 1024], F32)
    cx.r_lnt = R("lnt")
    if rope is not None:
        cx.ropeC = cx.alloc("ropeC", [64, nt], F32)
        cx.ropeS = cx.alloc("ropeS", [64, nt], F32)
        cx.r_rope = R("rope")
        build_rope(cx, io, "g%sr" % rope, "g%ssw" % rope, nt)


def build_rope(cx, io, gr, gsw, nt):
    P = cx.P
    PI = float(np.pi)
    rr = cx.r_rope
    with cx.scope():
        posi = cx.alloc("posi", [64, nt], I32)
        ang = cx.alloc("ang", [64, nt], F32)
        kf = cx.alloc("kf", [64, nt], F32)
        ki = cx.alloc("ki", [64, nt], I32)
        r = cx.alloc("r", [64, nt], F32)
        m = cx.alloc("m", [64, nt], F32)
        P.dma("sp", posi[:, :], io["pos"][0:nt].partition_broadcast(64), writes=[rr])
        P.op("dve", lambda e: e.tensor_copy(out=ang[:, :], in_=posi[:, :]), reads=[rr], writes=[rr])
        ts(cx, ang[:, :], ang[:, :], gcol(cx, "invf", 0, 64), None, ALU.mult, None, [rr, cx.r_const], [rr])
        ts(cx, kf[:, :], ang[:, :], 1.0 / (2 * PI), None, ALU.mult, None, [rr], [rr])
        P.op("dve", lambda e: e.tensor_copy(out=ki[:, :], in_=kf[:, :]), reads=[rr], writes=[rr])
        P.op("dve", lambda e: e.tensor_copy(out=kf[:, :], in_=ki[:, :]), reads=[rr], writes=[rr])
        C1 = 6.28125
        C2 = float(2 * np.pi - 6.28125)
        stt(cx, r[:, :], kf[:, :], -C1, ang[:, :], ALU.mult, ALU.add, [rr], [rr])
        stt(cx, r[:, :], kf[:, :], -C2, r[:, :], ALU.mult, ALU.add, [rr], [rr])

        def wrap(x):
            ts(cx, m[:, :], x, PI, -2 * PI, ALU.is_gt, ALU.mult, [rr], [rr])
            tt(cx, x, x, m[:, :], ALU.add, [rr], [rr])
            ts(cx, m[:, :], x, -PI, 2 * PI, ALU.is_lt, ALU.mult, [rr], [rr])
            tt(cx, x, x, m[:, :], ALU.add, [rr], [rr])
            ts(cx, x, x, PI, -PI, ALU.min, ALU.max, [rr], [rr])

        wrap(r[:, :])
        act(cx, kf[:, :], r[:, :], AF.Sin, [rr], [rr])
        ts(cx, cx.ropeS[:, :], kf[:, :], gcol(cx, gsw, 0, 64), gcol(cx, "sgn", 0, 64), ALU.mult, ALU.mult,
           [rr, cx.r_const], [rr])
        ts(cx, r[:, :], r[:, :], PI / 2, None, ALU.add, None, [rr], [rr])
        wrap(r[:, :])
        act(cx, kf[:, :], r[:, :], AF.Sin, [rr], [rr])
        ts(cx, cx.ropeC[:, :], kf[:, :], gcol(cx, gr, 0, 64), None, ALU.mult, None, [rr, cx.r_const], [rr])


def alloc_stream(cx):
    cx.xT = cx.alloc("xT", [128, 16, G], F32)
    cx.rx = [R("x%d" % c) for c in range(16)]
    cx.hT = cx.alloc("hT", [128, 16, G], BF16)
    cx.rh = [R("h%d" % c) for c in range(16)]


def xchunks(cx):
    return [(cx.xT[:, c, :], cx.rx[c]) for c in range(16)]


def hchunks(cx):
    return [(cx.hT[:, c, :], cx.rh[c]) for c in range(16)]


def residual_epilogue(cx):
    def ep(oc, banks):
        for s, (bt, br) in enumerate(banks):
            xs = cx.xT[:, oc, s * 512:(s + 1) * 512]
            tt(cx, xs, xs, bt[:, :], ALU.add, [cx.rx[oc], br], [cx.rx[oc]])
    return ep


def slab_f32(cx, i):
    t, r = cx.slabs[i]
    return t[:, :].bitcast(F32).rearrange("p (a b) -> p a b", a=4), r


def phase1(cx, io):
    P = cx.P
    with cx.scope():
        setup_consts(cx, io, "k", T1)
        Ck, Sk = cx.ropeC, cx.ropeS
        cx.ns = 2
        cx.init_wring(nw=2)
        alloc_stream(cx)
        ckvf, r_ckvf = slab_f32(cx, 1)
        ckvT = cx.alloc("ckvT", [128, 4, G], BF16)
        r_ckvT = [R("ckvT%d" % i) for i in range(4)]
        krf = cx.alloc("krf", [64, G], F32)
        r_krf = R("krf")
        sqpe = cx.alloc("sqpe", [64, G], BF16)
        r_sqpe = R("sqpe")
        vt = [(cx.alloc("vt", [128, 512], BF16), R("vt%d" % i)) for i in range(2)]
        kno = [(cx.alloc("kno", [128, G], BF16), R("kno%d" % i)) for i in range(2)]
        kro = [(cx.alloc("kro", [64, G], BF16), R("kro%d" % i)) for i in range(2)]
        t1 = cx.lnt[0:64, :]
        r_t1 = cx.r_lnt
        r_out = R("p1out")
        if STOP == 0:
            return
        for g in range(T1 // G):
            tok = slice(g * G, (g + 1) * G)
            for c_ in range(16):
                P.dma("sp", cx.xT[:, c_, :], io["xT"][:, c_, tok], writes=[cx.rx[c_]])
            if STOP == 1:
                return
            norm_fm(cx, xchunks(cx), "a_norm", hchunks(cx), D)
            if STOP == 2:
                return

            def ep_ckv(oc, banks):
                for s, (bt, br) in enumerate(banks):
                    cpy(cx, "act" if s == 0 else "dve", ckvf[:, oc, s * 512:(s + 1) * 512], bt[:, :], [br], [r_ckvf])
            linear_fm(cx, lambda oc: io["WDKV"][oc], 16, hchunks(cx), 4, ep_ckv)
            if STOP == 3:
                return
            wt, rw = cx.load_w(io["WDKV"][4], 16, 128)
            bp = cx.bankpair()
            bs = cx.bankpair()
            for half, banks in ((0, bp), (1, bs)):
                for kc in range(16):
                    for s, (bt, br) in enumerate(banks):
                        mm(cx, bt[0:64, :], wt[:, kc, half * 64:(half + 1) * 64], cx.hT[:, kc, s * 512:(s + 1) * 512],
                           kc == 0, kc == 15, [rw, cx.rh[kc]], br)
            for s in range(2):
                sl = slice(s * 512, (s + 1) * 512)
                gsl = slice(g * G + s * 512, g * G + (s + 1) * 512)
                act(cx, sqpe[:, sl], bp[s][0][0:64, :], AF.Square, [bp[s][1]], [r_sqpe])
                tt(cx, krf[:, sl], bp[s][0][0:64, :], Ck[:, gsl], ALU.mult, [bp[s][1], cx.r_rope], [r_krf])
                tt(cx, t1[:, sl], bs[s][0][0:64, :], Sk[:, gsl], ALU.mult, [bs[s][1], cx.r_rope], [r_t1])
                tt(cx, krf[:, sl], krf[:, sl], t1[:, sl], ALU.add, [r_krf, r_t1], [r_krf])
            if STOP == 4:
                return
            norm_fm(cx, [(ckvf[:, i, :], r_ckvf) for i in range(4)], "a_g_kv",
                    [(ckvT[:, i, :], r_ckvT[i]) for i in range(4)], 512)
            if STOP == 5:
                return
            wv, r_wv = cx.load_slab(io["WV"], 4, 2048)
            assert r_wv is cx.slabs[0][1]
            cx.si = 0
            for tti in range(8):
                for cg in range(4):
                    bt, br = cx.banks[(tti * 4 + cg) % 8]
                    for kc in range(4):
                        mm(cx, bt[:, :], ckvT[:, kc, tti * 128:(tti + 1) * 128], wv[:, kc, cg * 512:(cg + 1) * 512],
                           kc == 0, kc == 3, [r_ckvT[kc], r_wv], br)
                    vtt, rvt = vt[cg % 2]
                    cpy(cx, "act" if cg % 2 == 0 else "dve", vtt[:, :], bt[:, :], [br], [rvt])
                    dst = io["VV"][cg * 4:(cg + 1) * 4, :, g * 8 + tti, :].rearrange("h p d -> p h d")
                    P.dma("sp", dst, vtt[:, :].rearrange("p (h d) -> p h d", h=4), reads=[rvt], writes=[])
            if STOP == 6:
                return
            def kmm(h):
                wt, rw = cx.load_w(io["WUKVN"][h], 4, 128)
                ba = cx.bankpair()
                for kc in range(4):
                    for s, (bt, br) in enumerate(ba):
                        mm(cx, bt[:, :], wt[:, kc, :], ckvT[:, kc, s * 512:(s + 1) * 512], kc == 0, kc == 3, [rw, r_ckvT[kc]], br)
                return ba
            ba_next = kmm(0)
            for h in range(16):
                ba = ba_next
                if h + 1 < 16:
                    ba_next = kmm(h + 1)
                ko, rko = kno[h % 2]
                kr_, rkr = kro[h % 2]
                head_norm(cx, ba, (sqpe, r_sqpe), 192, "gkn_n", ko, [rko])
                tt(cx, kr_[:, :], krf[:, :], cx.rstd[0:64, :], ALU.mult, [r_krf, cx.r_rstd], [rkr])
                P.dma("sp", io["KN"][h, :, tok], ko[:, :], reads=[rko], writes=[])
                P.dma("sp", io["KR"][h, :, tok], kr_[:, :], reads=[rkr], writes=[])


def ffn_block(cx, io, layer, fnorm):
    norm_fm(cx, xchunks(cx), fnorm, hchunks(cx), D)
    WG, WU, WD = io["WG%d" % layer], io["WU%d" % layer], io["WD%d" % layer]
    ns = cx.ns
    gb = [cx.banks[0], cx.banks[1]][:ns]
    ub = [cx.banks[2], cx.banks[3]][:ns]
    db = [[cx.banks[4], cx.banks[5]][:ns], [cx.banks[6], cx.banks[7]][:ns]]
    di = 0
    for sb in range(NFF // 4):
        aT, r_aT = cx.aT[sb % 2]
        for c in range(4):
            ff = sb * 4 + c
            wg, rwg = cx.load_w(WG[ff], 16, 128)
            wu, rwu = cx.load_w(WU[ff], 16, 128)
            sg, rsg = cx.sg[ff % 2]
            for kc in range(16):
                for s, (bt, br) in enumerate(gb):
                    mm(cx, bt[:, :], wg[:, kc, :], cx.hT[:, kc, s * 512:(s + 1) * 512], kc == 0, kc == 15, [rwg, cx.rh[kc]], br)
            for s, (bt, br) in enumerate(gb):
                act(cx, sg[:, s * 512:(s + 1) * 512], bt[:, :], AF.Silu, [br], [rsg])
            for kc in range(16):
                for s, (bt, br) in enumerate(ub):
                    mm(cx, bt[:, :], wu[:, kc, :], cx.hT[:, kc, s * 512:(s + 1) * 512], kc == 0, kc == 15, [rwu, cx.rh[kc]], br)
            for s, (bt, br) in enumerate(ub):
                sl = slice(s * 512, (s + 1) * 512)
                tt(cx, aT[:, c, sl], sg[:, sl], bt[:, :], ALU.mult, [rsg, br], [r_aT[c]])
        wd, rwd = cx.load_slab(WD[sb], 4, 2048)
        for oc in range(16):
            banks = db[di]
            di = 1 - di
            for c in range(4):
                for s, (bt, br) in enumerate(banks):
                    mm(cx, bt[:, :], wd[:, c, oc * 128:(oc + 1) * 128], aT[:, c, s * 512:(s + 1) * 512], c == 0, c == 3,
                       [rwd, r_aT[c]], br)
            for s, (bt, br) in enumerate(banks):
                xs = cx.xT[:, oc, s * 512:(s + 1) * 512]
                tt(cx, xs, xs, bt[:, :], ALU.add, [cx.rx[oc], br], [cx.rx[oc]])


def egate_block(cx, io, layer, enorm, tok):
    P = cx.P
    norm_fm(cx, xchunks(cx), enorm, hchunks(cx), D)
    EG, EP = io["EG%d" % layer], io["EP%d" % layer]
    P.dma("pool", cx.pT[:, :, 0:cx.gw], io["pT%d" % layer][:, :, tok], writes=[cx.r_pT])
    for oc in range(16):
        wg, rwg = cx.load_w(EG[oc], 16, 128)
        wp, rwp = cx.load_w(EP[oc], 2, 128)
        ba = cx.bankpair()
        bb = cx.bankpair()
        for kc in range(16):
            for s, (bt, br) in enumerate(ba):
                mm(cx, bt[:, :], wg[:, kc, :], cx.hT[:, kc, s * 512:(s + 1) * 512], kc == 0, kc == 15, [rwg, cx.rh[kc]], br)
        for kc in range(2):
            for s, (bt, br) in enumerate(bb):
                mm(cx, bt[:, :], wp[:, kc, :], cx.pT[:, kc, s * 512:(s + 1) * 512], kc == 0, kc == 1, [rwp, cx.r_pT], br)
        sg, rsg = cx.sg[oc % 2]
        for s in range(cx.ns):
            sl = slice(s * 512, (s + 1) * 512)
            act(cx, sg[:, sl], ba[s][0][:, :], AF.Sigmoid, [ba[s][1]], [rsg])
            tt(cx, sg[:, sl], sg[:, sl], bb[s][0][:, :], ALU.mult, [rsg, bb[s][1]], [rsg])
            xs = cx.xT[:, oc, sl]
            tt(cx, xs, xs, sg[:, sl], ALU.add, [cx.rx[oc], rsg], [cx.rx[oc]])


def alloc_ffn(cx):
    cx.aT = []
    for i in range(2):
        cx.aT.append((cx.alloc("aT", [128, 4, G], BF16), [R("aT%d_%d" % (i, c)) for c in range(4)]))
    cx.sg = [(cx.alloc("sg", [128, G], F32), R("sg%d" % i)) for i in range(2)]
    cx.pT = cx.alloc("pT", [128, 2, G], BF16)
    cx.r_pT = R("pT")


def attention(cx, h, sl, qn, rqn, qr, rqr, tiles, scale, bias_fn=None, st_banks=(0, 1), o_banks=(2, 3)):
    P = cx.P
    ob, orr = cx.banks[o_banks[0]]
    sb_, srr = cx.banks[o_banks[1]]
    n = len(tiles)
    nb = len(st_banks)
    la = nb - 1

    def qk(i):
        t = tiles[i]
        c0, c1 = t["c0"], t["c1"]
        stb, rst = cx.banks[st_banks[i % nb]]
        if qr is not None:
            mm(cx, stb[:, c0:c1], t["K"], qn[:, c0:c1], True, False, t["rK"] + [rqn], rst)
            mm(cx, stb[:, c0:c1], t["KR"], qr[:, c0:c1], False, True, t["rK"] + [rqr], rst)
        else:
            mm(cx, stb[:, c0:c1], t["K"], qn[:, c0:c1], True, True, t["rK"] + [rqn], rst)

    def rest(i):
        t = tiles[i]
        c0, c1 = t["c0"], t["c1"]
        stb, rst = cx.banks[st_banks[i % nb]]
        pt, rpt = cx.pring[cx.pi]
        cx.pi = (cx.pi + 1) % len(cx.pring)
        mcol = cx.mk[:, t["mask"]:t["mask"] + 1] if t["mask"] is not None else cx.zeroc
        if bias_fn is None:
            act(cx, pt[:, c0:c1], stb[:, c0:c1], AF.Exp, [rst, cx.r_const], [rpt], scale=scale, bias=mcol)
        else:
            tmp, rtmp = cx.tmpring[cx.ti]
            cx.ti = (cx.ti + 1) % len(cx.tmpring)
            b_ap, b_r = bias_fn(t)
            stt(cx, tmp[:, c0:c1], stb[:, c0:c1], scale, b_ap, ALU.mult, ALU.add, [rst, b_r], [rtmp])
            act(cx, pt[:, c0:c1], tmp[:, c0:c1], AF.Exp, [rtmp, cx.r_const], [rpt], scale=1.0, bias=mcol)
        if t["zero"] is not None:
            p0, p1, z0, z1 = t["zero"]
            cx.P.op("pool", lambda e, o=pt[p0:p1, z0:z1]: e.memset(o, 0.0), writes=[rpt])
        mm(cx, ob[:, c0:c1], t["V"], pt[:, c0:c1], i == 0, i == n - 1, t["rK"] + [rpt], orr)
        mm(cx, sb_[:, c0:c1], cx.ones[:, :], pt[:, c0:c1], i == 0, i == n - 1, [rpt, cx.r_const], srr)

    for i in range(min(la, n)):
        qk(i)
    for i in range(n):
        if i + la < n:
            qk(i + la)
        rest(i)
    act(cx, cx.rinv[:, :], sb_[:, :], AF.Ln, [srr], [cx.r_rinv])
    act(cx, cx.rinv[:, :], cx.rinv[:, :], AF.Exp, [cx.r_rinv], [cx.r_rinv], scale=-1.0)
    tt(cx, cx.hT[:, h, sl * 512:(sl + 1) * 512], ob[:, :], cx.rinv[:, :], ALU.mult, [orr, cx.r_rinv], [cx.rh[h]])


def alloc_attn(cx, npt=4):
    cx.pring = [(cx.alloc("pt", [128, 512], BF16), R("pt%d" % i)) for i in range(npt)]
    cx.pi = 0
    cx.rinv = cx.alloc("rinv", [128, 512], F32)
    cx.r_rinv = R("rinv")


def phase2(cx, io):
    P = cx.P
    with cx.scope():
        setup_consts(cx, io, "q", T2)
        Cq, Sq = cx.ropeC, cx.ropeS
        cx.ns = 2
        cx.init_wring(nw=3)
        alloc_stream(cx)
        r_out = R("p2out")
        sc0 = float(1.0 / np.sqrt(192.0))
        t0s, r0s = cx.slabs[0]
        t1s, r1s = cx.slabs[1]
        r_kr = [R("kr0"), R("kr1")]
        for grp in L0_GROUPS:
            cx.ns = len(grp)
            gw = cx.gw
            tok = slice(grp[0] * 512, grp[0] * 512 + gw)
            for c_ in range(16):
                P.dma("sp", cx.xT[:, c_, 0:gw], io["xT"][:, c_, tok], writes=[cx.rx[c_]])
            with cx.scope():
                alloc_attn(cx, 2)
                cqf, r_cqf = slab_f32(cx, 1)
                cqT = cx.alloc("cqT", [128, 4, G], BF16)
                r_cqT = [R("cqT%d" % i) for i in range(4)]
                qn = [(cx.alloc("qn", [128, 512], BF16), R("qn%d" % i)) for i in range(2)]
                qr = [(cx.alloc("qr", [64, 512], BF16), R("qr%d" % i)) for i in range(2)]
                sqn = cx.alloc("sqn", [128, 512], BF16)
                sqr = cx.alloc("sqr", [64, 512], BF16)
                r_sqq = R("sqq")
                rq = cx.rstd[:, 0:512]
                lq = cx.lnt[:, 0:512]
                r_rq, r_lq = cx.r_rstd, cx.r_lnt
                t1 = cx.lnt[0:64, 512:1024]
                t2 = cx.rstd[0:64, 512:1024]
                r_t1 = R("t1")
                r_t2 = R("t2")
                kvB = cx.alloc("kvB", [128, 8192], BF16)
                r_kvB = R("kvB")
                KVB = [(t0s, r0s), (kvB, r_kvB)]
                norm_fm(cx, xchunks(cx), "a_norm", hchunks(cx), D)

                def ep_cq(oc, banks):
                    for s_, (bt, br) in enumerate(banks):
                        cpy(cx, "act" if s_ == 0 else "dve", cqf[:, oc, s_ * 512:(s_ + 1) * 512], bt[:, :], [br], [r_cqf])
                linear_fm(cx, lambda oc: io["WDQ"][oc], 16, hchunks(cx), 4, ep_cq)
                norm_fm(cx, [(cqf[:, i, :], r_cqf) for i in range(4)], "a_g_q",
                        [(cqT[:, i, :], r_cqT[i]) for i in range(4)], 512)
                items = [(h, sl) for h in range(16) for sl in range(cx.ns)]
                wqs = {}

                def qpath(k):
                    h, sl = items[k]
                    if h not in wqs:
                        wqs[h] = cx.load_w(io["WUQ"][h], 4, 256)
                    wq, rwq = wqs[h]
                    slot = grp[sl]
                    csl = slice(sl * 512, (sl + 1) * 512)
                    gsl = slice(slot * 512, (slot + 1) * 512)
                    bA, rA = cx.banks[4]
                    bB, rB = cx.banks[5]
                    bC, rC = cx.banks[6]
                    bD, rD = cx.banks[7]
                    for kc in range(4):
                        mm(cx, bA[:, :], wq[:, kc, 0:128], cqT[:, kc, csl], kc == 0, kc == 3, [rwq, r_cqT[kc]], rA)
                    for kc in range(4):
                        mm(cx, bB[0:64, :], wq[:, kc, 128:192], cqT[:, kc, csl], kc == 0, kc == 3, [rwq, r_cqT[kc]], rB)
                    for kc in range(4):
                        mm(cx, bC[0:64, :], wq[:, kc, 192:256], cqT[:, kc, csl], kc == 0, kc == 3, [rwq, r_cqT[kc]], rC)
                    act(cx, sqn[:, :], bA[:, :], AF.Square, [rA], [r_sqq])
                    act(cx, sqr[:, :], bB[0:64, :], AF.Square, [rB], [r_sqq])
                    mm(cx, bD[:, :], cx.ones[:, :], sqn[:, :], True, False, [r_sqq, cx.r_const], rD)
                    mm(cx, bD[:, :], cx.ones[0:64, :], sqr[:, :], False, True, [r_sqq, cx.r_const], rD)
                    act(cx, lq, bD[:, :], AF.Ln, [rD, cx.r_const], [r_lq], scale=1.0 / 192, bias=cx.epsc[:, 0:1])
                    act(cx, rq, lq, AF.Exp, [r_lq], [r_rq], scale=-0.5)
                    qnt, rqn = qn[k % 2]
                    qrt, rqr = qr[k % 2]
                    stt(cx, qnt[:, :], bA[:, :], gcol(cx, "gqn_n"), rq, ALU.mult, ALU.mult, [rA, r_rq, cx.r_const], [rqn])
                    tt(cx, t1[:, :], bB[0:64, :], Cq[:, gsl], ALU.mult, [rB, cx.r_rope], [r_t1])
                    tt(cx, t2[:, :], bC[0:64, :], Sq[:, gsl], ALU.mult, [rC, cx.r_rope], [r_t2])
                    tt(cx, t1[:, :], t1[:, :], t2[:, :], ALU.add, [r_t1, r_t2], [r_t1])
                    tt(cx, qrt[:, :], t1[:, :], rq[0:64, :], ALU.mult, [r_t1, r_rq], [rqr])

                qpath(0)
                for k, (h, sl) in enumerate(items):
                    kvt, r_kv = KVB[h % 2]
                    KA = kvt[:, 0:4096]
                    VA = kvt[:, 4096:8192].rearrange("p (t d) -> p t d", t=32)
                    KRA = t1s[0:64, (h % 2) * 4096:(h % 2 + 1) * 4096]
                    r_kra = r_kr[h % 2]
                    if sl == 0:
                        P.dma("sp", KA, io["KN"][h], writes=[r_kv])
                        P.dma("sp", KRA, io["KR"][h], reads=r_cqT, writes=[r_kra])
                        P.dma("sp", VA, io["VV"][h], writes=[r_kv])
                    if k + 1 < len(items):
                        qpath(k + 1)
                    slot = grp[sl]
                    qnt, rqn = qn[k % 2]
                    qrt, rqr = qr[k % 2]
                    tiles = []
                    for t in range(4):
                        ks = slice(slot * 512 + t * 128, slot * 512 + (t + 1) * 128)
                        tiles.append(dict(K=KA[:, ks], KR=KRA[:, ks], V=VA[:, slot * 4 + t, :], rK=[r_kv, r_kra],
                                          c0=t * 128, c1=512, mask=None, zero=(64, 128, t * 128, t * 128 + 64)))
                    for i, pp in enumerate(L0_PAST[slot]):
                        for t in range(4):
                            ks = slice(pp * 512 + t * 128, pp * 512 + (t + 1) * 128)
                            tiles.append(dict(K=KA[:, ks], KR=KRA[:, ks], V=VA[:, pp * 4 + t, :], rK=[r_kv, r_kra],
                                              c0=0, c1=512, mask=MK0[(slot, i)], zero=None))
                    attention(cx, h, sl, qnt, rqn, qrt, rqr, tiles, sc0)
            with cx.scope():
                alloc_ffn(cx)
                linear_fm(cx, lambda oc: io["WO0"][oc], 16, hchunks(cx), 16, residual_epilogue(cx))
                ffn_block(cx, io, 0, "f_norm0")
                egate_block(cx, io, 0, "e_norm0", tok)
                P.dma("sp", io["xres"][:, :, tok], cx.xT[:, :, 0:gw], reads=cx.rx, writes=[r_out])
            with cx.scope():
                sko = [(cx.alloc("sko", [128, G], BF16), R("sko%d" % i)) for i in range(2)]
                svt = [(cx.alloc("svt", [128, 512], BF16), R("svt%d" % i)) for i in range(2)]
                norm_fm(cx, xchunks(cx), "s_norm", hchunks(cx), D)
                for h in range(16):
                    wt, rw = cx.load_w(io["SWK"][h], 16, 128)
                    ba = cx.bankpair()
                    for kc in range(16):
                        for s_, (bt, br) in enumerate(ba):
                            mm(cx, bt[:, :], wt[:, kc, :], cx.hT[:, kc, s_ * 512:(s_ + 1) * 512], kc == 0, kc == 15,
                               [rw, cx.rh[kc]], br)
                    ko, rko = sko[h % 2]
                    head_norm(cx, ba, None, 128, "s_gkn", ko, [rko])
                    P.dma("sp", io["SK"][h, :, tok], ko[:, 0:gw], reads=[rko], writes=[r_out])
                for cg in range(4):
                    wsl, rws = cx.load_slab(io["SWV"][cg], 16, 512)
                    for tti in range(4 * cx.ns):
                        bt, br = cx.banks[(cg * 8 + tti) % 8]
                        for kc in range(16):
                            mm(cx, bt[:, :], cx.hT[:, kc, tti * 128:(tti + 1) * 128], wsl[:, kc, :], kc == 0, kc == 15,
                               [cx.rh[kc], rws], br)
                        v_, rv_ = svt[tti % 2]
                        cpy(cx, "act" if tti % 2 == 0 else "dve", v_[:, :], bt[:, :], [br], [rv_])
                        dst = io["SV"][cg * 4:(cg + 1) * 4, :, grp[0] * 4 + tti, :].rearrange("h p d -> p h d")
                        P.dma("sp", dst, v_[:, :].rearrange("p (h d) -> p h d", h=4), reads=[rv_], writes=[r_out])
        cx.ns = 2


def phase3(cx, io):
    P = cx.P
    with cx.scope():
        setup_consts(cx, io, None)
        cx.ns = 2
        cx.init_wring()
        alloc_stream(cx)
        r_out = R("p3out")
        r_ext = R("ext")
        ext = io["ext"]
        tab = io["relb"]
        Z = io["Z"]
        with cx.scope():
            tab_sb = cx.alloc("tab_sb", [16, 513], F32)
            ext_sb = cx.alloc("ext_sb", [16, 1535], F32)
            P.dma("sp", tab_sb[:, :], tab, writes=[r_ext])
            P.op("dve", lambda e: e.memset(ext_sb[:, :], 0.0), writes=[r_ext])
            ts(cx, ext_sb[:, 0:255], ext_sb[:, 0:255], tab_sb[:, 0:1], None, ALU.add, None, [r_ext], [r_ext])
            ts(cx, ext_sb[:, 768:1535], ext_sb[:, 768:1535], tab_sb[:, 512:513], None, ALU.add, None, [r_ext], [r_ext])
            P.op("dve", lambda e: e.tensor_copy(out=ext_sb[:, 255:768], in_=tab_sb[:, :]), reads=[r_ext], writes=[r_ext])
            P.dma("sp", ext, ext_sb[:, :], reads=[r_ext], writes=[r_ext])
            srcb = bass.AP(tensor=ext.tensor, offset=0, ap=[[1535, 16], [0, 128], [1, 1535]])
            P.dma("sp", Z[:, :, 0:1535], srcb, reads=[r_ext], writes=[r_ext])
        sc1 = float(1.0 / np.sqrt(128.0))
        for gi, grp in enumerate(L1_GROUPS):
            tok = slice(grp[0] * 512, grp[0] * 512 + G)
            otok = slice(gi * G, (gi + 1) * G)
            for c_ in range(16):
                P.dma("sp", cx.xT[:, c_, :], io["xres"][:, c_, tok], writes=[cx.rx[c_]])
            with cx.scope():
                alloc_attn(cx)
                cx.tmpring = [(cx.alloc("tmp", [128, 512], F32), R("tmp%d" % i)) for i in range(2)]
                cx.ti = 0
                qT = cx.alloc("qT", [128, 16, G], BF16)
                r_qT = [R("qT%d" % i) for i in range(16)]
                TB = [(cx.alloc("TB", [128, 1024], F32), R("TB%d" % i)) for i in range(2)]
                norm_fm(cx, xchunks(cx), "b_norm", hchunks(cx), D)
                for h in range(16):
                    wt, rw = cx.load_w(io["BWQ"][h], 16, 128)
                    ba = cx.bankpair()
                    for kc in range(16):
                        for s_, (bt, br) in enumerate(ba):
                            mm(cx, bt[:, :], wt[:, kc, :], cx.hT[:, kc, s_ * 512:(s_ + 1) * 512], kc == 0, kc == 15,
                               [rw, cx.rh[kc]], br)
                    head_norm(cx, ba, None, 128, "b_gqn", qT[:, h, :], [r_qT[h]])
                for h in range(16):
                    t0s, r0s = cx.slabs[h % 2]
                    SKA = t0s[:, 0:T2]
                    SVA = t0s[:, 4096:4096 + T2].rearrange("p (t d) -> p t d", t=4 * NL0)
                    P.dma("sp", SKA, io["SK"][h], writes=[r0s])
                    P.dma("sp", SVA, io["SV"][h], writes=[r0s])
                    tb, rtb = TB[h % 2]
                    src = bass.AP(tensor=Z.tensor, offset=h * 128 * 1536 + 127, ap=[[1535, 128], [1, 1024]])
                    P.dma("sp", tb[:, :], src, reads=[r_ext], writes=[rtb])
                    for sl in range(2):
                        slot = grp[sl]
                        csl = slice(sl * 512, (sl + 1) * 512)
                        tiles = []
                        for t in range(4):
                            ks = slice(slot * 512 + t * 128, slot * 512 + (t + 1) * 128)
                            tiles.append(dict(K=SKA[:, ks], V=SVA[:, slot * 4 + t, :], rK=[r0s], rel=t,
                                              c0=t * 128, c1=512, mask=None, zero=(64, 128, t * 128, t * 128 + 64)))
                        for i, pp in enumerate(L1_PREV[slot]):
                            for t in range(4):
                                ks = slice(pp * 512 + t * 128, pp * 512 + (t + 1) * 128)
                                tiles.append(dict(K=SKA[:, ks], V=SVA[:, pp * 4 + t, :], rK=[r0s], rel=t - 4,
                                                  c0=0, c1=(t + 1) * 128, mask=MK1[(slot, i)],
                                                  zero=(0, 64, t * 128 + 64, t * 128 + 128)))

                        def bias_fn(t, tb=tb, rtb=rtb):
                            j0 = 384 - t["rel"] * 128
                            return tb[:, j0 + t["c0"]:j0 + t["c1"]], rtb
                        attention(cx, h, sl, qT[:, h, csl], r_qT[h], None, None, tiles, sc1, bias_fn,
                                  st_banks=(0, 1, 4, 5), o_banks=((2, 3) if (2 * h + sl) % 2 == 0 else (6, 7)))
            with cx.scope():
                alloc_ffn(cx)
                linear_fm(cx, lambda oc: io["WO1"][oc], 16, hchunks(cx), 16, residual_epilogue(cx))
                ffn_block(cx, io, 1, "f_norm1")
                egate_block(cx, io, 1, "e_norm1", tok)
                P.dma("sp", io["outT"][:, :, otok], cx.xT[:, :, :], reads=cx.rx, writes=[r_out])


def tiled(w, kc, oc, m=128):
    return np.ascontiguousarray(w.reshape(kc, 128, oc, m).transpose(2, 1, 0, 3))


DRAM_SPECS = {
    "xT": ([128, 16, T1], F32), "pos": ([T1], I32), "gv": ([128, NGV], F32), "mk": ([128, NMK], F32),
    "WDKV": ([5, 128, 16, 128], F32), "WUKVN": ([16, 128, 4, 128], F32), "WV": ([128, 4, 2048], F32),
    "KN": ([16, 128, T1], BF16), "KR": ([16, 64, T1], BF16), "VV": ([16, 128, 32, 128], BF16),
    "WDQ": ([4, 128, 16, 128], F32), "WUQ": ([16, 128, 4, 256], F32), "WO0": ([16, 128, 16, 128], F32),
    "WG0": ([NFF, 128, 16, 128], F32), "WU0": ([NFF, 128, 16, 128], F32), "WD0": ([NFF // 4, 128, 4, 2048], F32),
    "EG0": ([16, 128, 16, 128], F32), "EP0": ([16, 128, 2, 128], F32), "pT0": ([128, 2, T2], F32),
    "SWK": ([16, 128, 16, 128], F32), "SWV": ([4, 128, 16, 512], F32),
    "xres": ([128, 16, T2], F32), "SK": ([16, 128, T2], BF16), "SV": ([16, 128, 4 * NL0, 128], BF16),
    "BWQ": ([16, 128, 16, 128], F32), "WO1": ([16, 128, 16, 128], F32),
    "WG1": ([NFF, 128, 16, 128], F32), "WU1": ([NFF, 128, 16, 128], F32), "WD1": ([NFF // 4, 128, 4, 2048], F32),
    "EG1": ([16, 128, 16, 128], F32), "EP1": ([16, 128, 2, 128], F32), "pT1": ([128, 2, T2], F32),
    "relb": ([16, 513], F32), "ext": ([16, 1535], F32), "Z": ([16, 128, 1536], F32), "outT": ([128, 16, T], F32),
}
PHASE_IN = {
    1: ["xT", "pos", "gv", "mk", "WDKV", "WUKVN", "WV"],
    2: ["xT", "pos", "gv", "mk", "KN", "KR", "VV", "WDQ", "WUQ", "WO0", "WG0", "WU0", "WD0",
        "EG0", "EP0", "pT0", "SWK", "SWV"],
    3: ["xres", "gv", "mk", "SK", "SV", "BWQ", "WO1", "WG1", "WU1", "WD1", "EG1", "EP1", "pT1", "relb"],
}
PHASE_OUT = {1: ["KN", "KR", "VV"], 2: ["xres", "SK", "SV"], 3: ["outT"]}
PHASE_INT = {1: [], 2: [], 3: ["ext", "Z"]}


def build_program(phases):
    fused = len(phases) > 1
    nc = bass.Bass("TRN2", target_bir_lowering=False)
    io = {}
    internal = set()
    if fused:
        for ph in phases:
            internal.update(PHASE_OUT[ph])
        internal.discard("outT")
    for ph in phases:
        for n in PHASE_IN[ph] + PHASE_OUT[ph] + PHASE_INT[ph]:
            if n in io:
                continue
            shape, dt = DRAM_SPECS[n]
            if n in internal or n in PHASE_INT[ph]:
                kind = "Internal"
            elif n in PHASE_OUT[ph]:
                kind = "ExternalOutput"
            else:
                kind = "ExternalInput"
            io[n] = nc.dram_tensor(n, shape, dt, kind=kind).ap()
    cx = Ctx(nc)
    for ph in phases:
        {1: phase1, 2: phase2, 3: phase3}[ph](cx, io)
    cx.P.flush()
    cx.P.wait_all_dma()
    return nc


_PROGS = {}
LAST_RES = None


def get_prog(phases):
    key = tuple(phases)
    if key not in _PROGS:
        _PROGS[key] = build_program(phases)
    return _PROGS[key]


def core_tokens(r, npos=8):
    return np.concatenate([np.arange(s * 512, (s + 1) * 512) for s in POS2SUB[r][:npos]])


def fm(a):
    t, f = a.shape
    return np.ascontiguousarray(a.T.reshape(f // 128, 128, t).transpose(1, 0, 2))


def prep_weights(inp):
    f = np.float32
    W = {}
    a_w_dkv = inp["a_w_dkv"][0]
    pe = a_w_dkv[:, 512:576]
    pe_sw = np.concatenate([pe[:, 32:64], pe[:, 0:32]], axis=1)
    W["WDKV"] = tiled(np.concatenate([a_w_dkv[:, :512], pe, pe_sw], axis=1), 16, 5)
    ukv = inp["a_w_ukv"][0].reshape(512, 16, 256)
    W["WUKVN"] = tiled(np.ascontiguousarray(ukv[:, :, :128]).reshape(512, 2048), 4, 16)
    W["WV"] = np.ascontiguousarray(ukv[:, :, 128:].reshape(4, 128, 2048).transpose(1, 0, 2))
    W["WDQ"] = tiled(inp["a_w_dq"][0], 16, 4)
    uq = inp["a_w_uq"][0].reshape(512, 16, 192)
    uq2 = np.concatenate([uq[:, :, :128], uq[:, :, 128:192], uq[:, :, 160:192], uq[:, :, 128:160]], axis=2)
    W["WUQ"] = tiled(np.ascontiguousarray(uq2).reshape(512, 16 * 256), 4, 16, 256)
    W["WO0"] = tiled(inp["a_w_o"][0], 16, 16)
    W["WO1"] = tiled(inp["b_w_o"][0], 16, 16)
    W["BWQ"] = tiled(inp["b_w_q"][0], 16, 16)
    W["SWK"] = tiled(inp["s_w_k"], 16, 16)
    W["SWV"] = np.ascontiguousarray(inp["s_w_v"].reshape(16, 128, 4, 512).transpose(2, 1, 0, 3))
    for l in range(2):
        W["WG%d" % l] = tiled(inp["f_w_gate"][l], 16, NFF)
        W["WU%d" % l] = tiled(inp["f_w_up"][l], 16, NFF)
        W["WD%d" % l] = np.ascontiguousarray(inp["f_w_down"][l].reshape(NFF // 4, 4, 128, 2048).transpose(0, 2, 1, 3))
        W["EG%d" % l] = tiled(inp["e_w_gate"][l], 16, 16)
        W["EP%d" % l] = tiled(inp["e_w_proj"][l], 2, 16)
    W["relb"] = np.ascontiguousarray(inp["b_rel_bias"][0])
    gv = np.zeros((128, NGV), f)

    def put(name, vec):
        c0, k = GVI[name]
        v = np.asarray(vec, f)
        if v.size == 128 * k:
            gv[:, c0:c0 + k] = v.reshape(k, 128).T
        else:
            gv[:v.size, c0] = v
    put("a_norm", inp["a_norm"][0]); put("a_g_q", inp["a_g_q"][0]); put("a_g_kv", inp["a_g_kv"][0])
    put("f_norm0", inp["f_norm"][0]); put("e_norm0", inp["e_norm"][0]); put("s_norm", inp["s_norm"])
    put("b_norm", inp["b_norm"][0]); put("f_norm1", inp["f_norm"][1]); put("e_norm1", inp["e_norm"][1])
    gq, gk = inp["a_g_qn"][0], inp["a_g_kn"][0]
    put("gqn_n", gq[:128]); put("gkn_n", gk[:128]); put("s_gkn", inp["s_g_kn"]); put("b_gqn", inp["b_g_qn"][0])
    sw = (np.arange(64) + 32) % 64
    put("gqr", gq[128:192]); put("gqsw", gq[128:192][sw]); put("gkr", gk[128:192]); put("gksw", gk[128:192][sw])
    invf = (np.float32(10000.0) ** (-np.arange(0, 64, 2, dtype=f) / f(64))).astype(f)
    put("invf", invf[np.arange(64) % 32])
    put("sgn", np.where(np.arange(64) < 32, -1.0, 1.0))
    W["gv"] = gv
    return W


def core_masks(r):
    mk = np.zeros((128, NMK), np.float32)
    p2s = POS2SUB[r]
    for s in range(NL0):
        for i, pp in enumerate(L0_PAST[s]):
            if not (p2s[pp] < p2s[s]):
                mk[:, MK0[(s, i)]] = NEG
    for s in range(1, NL0):
        for i, pp in enumerate(L1_PREV[s]):
            if not (p2s[pp] == p2s[s] - 1):
                mk[:, MK1[(s, i)]] = NEG
    return mk


def run(phases, in_maps):
    nc = get_prog(phases)
    names = set()
    produced = set()
    for ph in phases:
        names.update(PHASE_IN[ph])
        produced.update(PHASE_OUT[ph])
    maps = [{k: v for k, v in m.items() if k in names and k not in produced} for m in in_maps]
    res = run_bass_kernel_spmd(nc, maps, core_ids=list(range(8)))
    global LAST_RES
    LAST_RES = res
    return res.results


def make_maps(inp):
    W = prep_weights(inp)
    x, p, positions = inp["x"], inp["p"], inp["positions"]
    maps = []
    for c in range(8):
        b, r = c // 2, c % 2
        tk = core_tokens(r)
        m = dict(W)
        m["xT"] = fm(x[b][tk])
        m["pos"] = np.ascontiguousarray(positions[b][tk]).astype(np.int32)
        m["pT0"] = fm(p[0, b][tk[:T2]])
        m["pT1"] = fm(p[1, b][tk[:T2]])
        m["mk"] = core_masks(r)
        maps.append(m)
    return maps


def kernel(**inp):
    inp = {k: np.asarray(v) for k, v in inp.items()}
    maps = make_maps(inp)
    if FUSED:
        r3 = run([1, 2, 3], maps)
    else:
        r1 = run([1], maps)
        for c in range(8):
            maps[c].update({n: r1[c][n] for n in PHASE_OUT[1]})
        r2 = run([2], maps)
        for c in range(8):
            maps[c].update({n: r2[c][n] for n in PHASE_OUT[2]})
        r3 = run([3], maps)
    out = np.zeros((NB, SEQ, D), np.float32)
    for c in range(8):
        b, r = c // 2, c % 2
        o = np.asarray(r3[c]["outT"])
        out[b][core_tokens(r, NL0)[512:]] = o.transpose(2, 1, 0).reshape(T, D)
    return out
```

```python
import contextlib
import os
import numpy as np
import ml_dtypes
import concourse.bass as bass
import concourse.mybir as mybir
from concourse.bass_utils import run_bass_kernel_spmd

F32 = mybir.dt.float32
BF16 = mybir.dt.bfloat16
I32 = mybir.dt.int32
AF = mybir.ActivationFunctionType
ALU = mybir.AluOpType

D = 2048
SEQ = 4096
NB = 4
T = 2048
G = 1024
DFF = 5632
NFF = DFF // 128
EPS = 1e-6
NEG = -30000.0
POS2SUB = {0: [5, 0, 1, 6, 7, 2, 3, 4], 1: [1, 2, 3, 4, 5, 0, 6, 7]}
NL0 = 5
T1 = 4096
T2 = NL0 * 512
L0_GROUPS = [(0, 1), (2, 3), (4,)]
L1_GROUPS = [(1, 2), (3, 4)]
L0_PAST = {}
for _i in range(NL0):
    _u = set()
    for _r in (0, 1):
        _u |= {p for p in range(8) if POS2SUB[_r][p] < POS2SUB[_r][_i]}
    L0_PAST[_i] = sorted(_u)
L1_PREV = {}
for _i in range(1, NL0):
    _u = set()
    for _r in (0, 1):
        _ps = POS2SUB[_r][_i] - 1
        if _ps >= 0:
            _u.add(POS2SUB[_r].index(_ps))
    assert all(p < NL0 for p in _u)
    L1_PREV[_i] = sorted(_u)
FUSED = os.environ.get('KUNFUSED') is None
STOP = int(os.environ.get('KSTOP', '99'))

GV_SPEC = [("a_norm", 16), ("a_g_q", 4), ("a_g_kv", 4), ("f_norm0", 16), ("e_norm0", 16), ("s_norm", 16),
           ("b_norm", 16), ("f_norm1", 16), ("e_norm1", 16), ("gqn_n", 1), ("gkn_n", 1), ("s_gkn", 1), ("b_gqn", 1),
           ("gqr", 1), ("gqsw", 1), ("gkr", 1), ("gksw", 1), ("invf", 1), ("sgn", 1)]
GVI = {}
_c = 0
for _n, _k in GV_SPEC:
    GVI[_n] = (_c, _k)
    _c += _k
NGV = _c
MK0 = {}
_c = 0
for _s in range(NL0):
    for _i in range(len(L0_PAST[_s])):
        MK0[(_s, _i)] = _c
        _c += 1
MK1 = {}
for _s in range(1, NL0):
    for _i in range(len(L1_PREV[_s])):
        MK1[(_s, _i)] = _c
        _c += 1
NMK = _c


class R:
    __slots__ = ("name", "w", "rs", "excl")

    def __init__(self, name="", excl=False):
        self.name = name
        self.w = None
        self.rs = []
        self.excl = excl


class Prog:
    NDMASEM = 12

    def __init__(self, nc):
        self.nc = nc
        self.engs = {"pe": nc.tensor, "act": nc.scalar, "dve": nc.vector, "pool": nc.gpsimd, "sp": nc.sync}
        self.ops = []
        self.floor = 0
        self.esem = {k: nc.alloc_semaphore(name="es_" + k) for k in self.engs}
        self.ecount = {k: 0 for k in self.engs}
        self.dsem = {k: [nc.alloc_semaphore(name="ds_%s_%d" % (k, j)) for j in range(self.NDMASEM)] for k in ("sp", "pool")}
        self.dcount = {k: 0 for k in self.dsem}
        self.tok = []
        self.waited = {k: {} for k in self.engs}
        self.nwaits = 0

    def op(self, eng, fn, reads=(), writes=(), dma=False):
        i = len(self.ops)
        deps = set()
        fl = self.floor
        if any(r.excl for r in reads):
            writes = list(writes) + [r for r in reads if r.excl]
            reads = [r for r in reads if not r.excl]
        for r in reads:
            if r.w is not None and r.w >= fl:
                deps.add(r.w)
        for r in writes:
            if r.w is not None and r.w >= fl:
                deps.add(r.w)
            for x in r.rs:
                if x >= fl:
                    deps.add(x)
        for r in reads:
            r.rs.append(i)
        for r in writes:
            r.w = i
            r.rs = []
        self.ops.append((eng, fn, deps, dma))
        return i

    def dma(self, q, out, in_, reads=(), writes=()):
        return self.op(q, lambda e: e.dma_start(out=out, in_=in_), reads, writes, dma=True)

    def flush(self):
        ops = self.ops
        start = len(self.tok)
        n = len(ops)
        if start == n:
            return
        needed = {}
        last = {}
        for i in range(start, n):
            eng, fn, deps, dma = ops[i]
            if not dma:
                last[eng] = i
            for d in deps:
                p = ops[d]
                if (not p[3]) and (not dma) and p[0] == "pe" and eng == "pe":
                    continue
                needed[d] = True
        for ek, i in last.items():
            needed[i] = True
        for i in range(start, n):
            ek, fn, deps, dma = ops[i]
            e = self.engs[ek]
            for d in sorted(deps):
                p = ops[d]
                if (not p[3]) and (not dma) and p[0] == "pe" and ek == "pe":
                    continue
                key, sem, val = self.tok[d]
                if self.waited[ek].get(key, 0) >= val:
                    continue
                e.wait_ge(sem, val)
                self.nwaits += 1
                self.waited[ek][key] = val
            if dma:
                k = self.dcount[ek]
                self.dcount[ek] += 1
                j = k % self.NDMASEM
                sem = self.dsem[ek][j]
                key = ("d", ek, j)
                prev = 16 * (k // self.NDMASEM)
                if prev > 0 and self.waited[ek].get(key, 0) < prev:
                    e.wait_ge(sem, prev)
                    self.waited[ek][key] = prev
                ins = fn(e)
                ins.then_inc(sem, 16)
                self.tok.append((key, sem, prev + 16))
            else:
                ins = fn(e)
                if needed.get(i, False):
                    self.ecount[ek] += 1
                    ins.then_inc(self.esem[ek], 1)
                    self.tok.append((("e", ek), self.esem[ek], self.ecount[ek]))
                else:
                    self.tok.append(None)
        for i in range(start, n):
            if self.tok[i] is None:
                self.tok[i] = self.tok[last[ops[i][0]]]

    def wait_all_dma(self):
        for ek in self.dsem:
            e = self.engs[ek]
            k = self.dcount[ek]
            for j in range(self.NDMASEM):
                cnt = (k - j + self.NDMASEM - 1) // self.NDMASEM if k > j else 0
                key = ("d", ek, j)
                if cnt > 0 and self.waited[ek].get(key, 0) < 16 * cnt:
                    e.wait_ge(self.dsem[ek][j], 16 * cnt)
                    self.waited[ek][key] = 16 * cnt

    def barrier(self):
        self.flush()
        self.wait_all_dma()
        self.nc.all_engine_barrier()
        self.floor = len(self.ops)


class Ctx:
    def __init__(self, nc):
        self.nc = nc
        self.P = Prog(nc)
        self.banks = []
        for i in range(8):
            t = nc.alloc_psum_tensor("ps%d" % i, [128, 512], F32)
            self.banks.append((t, R("ps%d" % i, excl=True)))
        self.pair_i = 0
        self.stacks = []
        self.uid = 0
        self.ns = 2

    @property
    def gw(self):
        return 512 * self.ns

    @contextlib.contextmanager
    def scope(self):
        st = contextlib.ExitStack()
        self.stacks.append(st)
        try:
            yield
            self.P.barrier()
        finally:
            self.stacks.pop()
            st.close()

    def alloc(self, name, shape, dtype):
        self.uid += 1
        return self.stacks[-1].enter_context(self.nc.sbuf_tensor("%s_%d" % (name, self.uid), shape, dtype))

    def bankpair(self):
        i = self.pair_i
        self.pair_i = (i + 1) % 4
        return [self.banks[2 * i], self.banks[2 * i + 1]][:self.ns]

    def init_wring(self, nw=4, nslab=2):
        self.wring = [(self.alloc("wr", [128, 2048], BF16), R("wr%d" % i)) for i in range(nw)]
        self.wi = 0
        self.slabs = [(self.alloc("ws", [128, 8192], BF16), R("ws%d" % i)) for i in range(nslab)]
        self.si = 0

    def load_w(self, src, kc, m):
        t, r = self.wring[self.wi]
        self.wi = (self.wi + 1) % len(self.wring)
        v = t[:, 0:kc * m].rearrange("p (k m) -> p k m", k=kc)
        self.P.dma("pool", v, src, writes=[r])
        return v, r

    def load_slab(self, src, a, b):
        t, r = self.slabs[self.si]
        self.si = (self.si + 1) % len(self.slabs)
        v = t[:, 0:a * b].rearrange("p (a b) -> p a b", a=a)
        self.P.dma("pool", v, src, writes=[r])
        return v, r


def mm(cx, out, lhsT, rhs, start, stop, reads, wr):
    cx.P.op("pe", lambda e: e.matmul(out, lhsT=lhsT, rhs=rhs, start=start, stop=stop), reads=reads, writes=[wr])


def act(cx, out, in_, func, reads, writes, scale=None, bias=None):
    kw = {}
    if scale is not None:
        kw["scale"] = scale
    if bias is not None:
        kw["bias"] = bias
    cx.P.op("act", lambda e: e.activation(out=out, in_=in_, func=func, **kw), reads=reads, writes=writes)


def tt(cx, out, in0, in1, op, reads, writes, eng="dve"):
    cx.P.op(eng, lambda e: e.tensor_tensor(out=out, in0=in0, in1=in1, op=op), reads=reads, writes=writes)


def ts(cx, out, in0, s1, s2, op0, op1, reads, writes, eng="dve"):
    if op1 is None:
        cx.P.op(eng, lambda e: e.tensor_scalar(out=out, in0=in0, scalar1=s1, scalar2=None, op0=op0), reads=reads, writes=writes)
    else:
        cx.P.op(eng, lambda e: e.tensor_scalar(out=out, in0=in0, scalar1=s1, scalar2=s2, op0=op0, op1=op1), reads=reads, writes=writes)


def stt(cx, out, in0, scalar, in1, op0, op1, reads, writes):
    cx.P.op("dve", lambda e: e.scalar_tensor_tensor(out=out, in0=in0, scalar=scalar, in1=in1, op0=op0, op1=op1),
            reads=reads, writes=writes)


def cpy(cx, eng, out, in_, reads, writes):
    if eng == "act":
        act(cx, out, in_, AF.Copy, reads, writes)
    else:
        cx.P.op(eng, lambda e: e.tensor_copy(out=out, in_=in_), reads=reads, writes=writes)


def gcol(cx, name, i=0, p=128):
    c0, _ = GVI[name]
    return cx.gv[0:p, c0 + i:c0 + i + 1]


def rstd_from_banks(cx, banks, dim):
    for s, (bt, br) in enumerate(banks):
        act(cx, cx.lnt[:, s * 512:(s + 1) * 512], bt[:, :], AF.Ln, [br, cx.r_const], [cx.r_lnt], scale=1.0 / dim,
            bias=cx.epsc[:, 0:1])
    act(cx, cx.rstd[:, 0:cx.gw], cx.lnt[:, 0:cx.gw], AF.Exp, [cx.r_lnt], [cx.r_rstd], scale=-0.5)


def next_sq(cx):
    sq, rsq = cx.sqring[cx.sqi]
    cx.sqi = (cx.sqi + 1) % len(cx.sqring)
    return sq, rsq


def norm_fm(cx, srcs, gname, outs, dim):
    banks = cx.bankpair()
    n = len(srcs)
    for i, (s, rs) in enumerate(srcs):
        sq, rsq = next_sq(cx)
        act(cx, sq[:, 0:cx.gw], s[:, 0:cx.gw], AF.Square, [rs], [rsq])
        for sb, (bt, br) in enumerate(banks):
            mm(cx, bt[:, :], cx.ones[:, :], sq[:, sb * 512:(sb + 1) * 512], i == 0, i == n - 1, [rsq, cx.r_const], br)
    rstd_from_banks(cx, banks, dim)
    for i, ((s, rs), (o, ro)) in enumerate(zip(srcs, outs)):
        stt(cx, o[:, 0:cx.gw], s[:, 0:cx.gw], gcol(cx, gname, i), cx.rstd[:, 0:cx.gw], ALU.mult, ALU.mult,
            [rs, cx.r_rstd, cx.r_const], [ro])


def head_norm(cx, ba, extra, dim, gname, out, r_out_list):
    sq, rsq = next_sq(cx)
    bb = cx.bankpair()
    for s, (bt, br) in enumerate(ba):
        act(cx, sq[:, s * 512:(s + 1) * 512], bt[:, :], AF.Square, [br], [rsq])
    for s, (bt, br) in enumerate(bb):
        sl = slice(s * 512, (s + 1) * 512)
        mm(cx, bt[:, :], cx.ones[:, :], sq[:, sl], True, extra is None, [rsq, cx.r_const], br)
        if extra is not None:
            mm(cx, bt[:, :], cx.ones[0:64, :], extra[0][:, sl], False, True, [extra[1], cx.r_const], br)
    rstd_from_banks(cx, bb, dim)
    for s, (bt, br) in enumerate(ba):
        sl = slice(s * 512, (s + 1) * 512)
        stt(cx, out[:, sl], bt[:, :], gcol(cx, gname), cx.rstd[:, sl], ALU.mult, ALU.mult, [br, cx.r_rstd, cx.r_const],
            r_out_list)


def linear_fm(cx, wtile, kc_n, ins, n_oc, epilogue, m=128):
    for oc in range(n_oc):
        wt, rw = cx.load_w(wtile(oc), kc_n, m)
        banks = cx.bankpair()
        for kc in range(kc_n):
            a, ra = ins[kc]
            for s, (bt, br) in enumerate(banks):
                mm(cx, bt[0:m, :], wt[:, kc, :], a[:, s * 512:(s + 1) * 512], kc == 0, kc == kc_n - 1, [rw, ra], br)
        epilogue(oc, banks)


def setup_consts(cx, io, rope, nt=0):
    P = cx.P
    cx.r_const = R("const")
    cx.ones = cx.alloc("ones", [128, 128], BF16)
    cx.gv = cx.alloc("gv", [128, NGV], F32)
    cx.mk = cx.alloc("mk", [128, NMK], F32)
    cx.epsc = cx.alloc("epsc", [128, 2], F32)
    P.op("dve", lambda e: e.memset(cx.ones[:, :], 1.0), writes=[cx.r_const])
    P.op("dve", lambda e: e.memset(cx.epsc[:, 0:1], EPS), writes=[cx.r_const])
    P.op("dve", lambda e: e.memset(cx.epsc[:, 1:2], 0.0), writes=[cx.r_const])
    cx.zeroc = cx.epsc[:, 1:2]
    P.dma("sp", cx.gv[:, :], io["gv"], writes=[cx.r_const])
    P.dma("sp", cx.mk[:, :], io["mk"], writes=[cx.r_const])
    cx.sqring = [(cx.alloc("sq", [128, 1024], BF16), R("sq%d" % i)) for i in range(2)]
    cx.sqi = 0
    cx.rstd = cx.alloc("rstd", [128, 1024], F32)
    cx.r_rstd = R("rstd")
    cx.lnt = cx.alloc("lnt", [128, 1024], F32)
    cx.r_lnt = R("lnt")
    if rope is not None:
        cx.ropeC = cx.alloc("ropeC", [64, nt], F32)
        cx.ropeS = cx.alloc("ropeS", [64, nt], F32)
        cx.r_rope = R("rope")
        build_rope(cx, io, "g%sr" % rope, "g%ssw" % rope, nt)


def build_rope(cx, io, gr, gsw, nt):
    P = cx.P
    PI = float(np.pi)
    rr = cx.r_rope
    with cx.scope():
        posi = cx.alloc("posi", [64, nt], I32)
        ang = cx.alloc("ang", [64, nt], F32)
        kf = cx.alloc("kf", [64, nt], F32)
        ki = cx.alloc("ki", [64, nt], I32)
        r = cx.alloc("r", [64, nt], F32)
        m = cx.alloc("m", [64, nt], F32)
        P.dma("sp", posi[:, :], io["pos"][0:nt].partition_broadcast(64), writes=[rr])
        P.op("dve", lambda e: e.tensor_copy(out=ang[:, :], in_=posi[:, :]), reads=[rr], writes=[rr])
        ts(cx, ang[:, :], ang[:, :], gcol(cx, "invf", 0, 64), None, ALU.mult, None, [rr, cx.r_const], [rr])
        ts(cx, kf[:, :], ang[:, :], 1.0 / (2 * PI), None, ALU.mult, None, [rr], [rr])
        P.op("dve", lambda e: e.tensor_copy(out=ki[:, :], in_=kf[:, :]), reads=[rr], writes=[rr])
        P.op("dve", lambda e: e.tensor_copy(out=kf[:, :], in_=ki[:, :]), reads=[rr], writes=[rr])
        C1 = 6.28125
        C2 = float(2 * np.pi - 6.28125)
        stt(cx, r[:, :], kf[:, :], -C1, ang[:, :], ALU.mult, ALU.add, [rr], [rr])
        stt(cx, r[:, :], kf[:, :], -C2, r[:, :], ALU.mult, ALU.add, [rr], [rr])

        def wrap(x):
            ts(cx, m[:, :], x, PI, -2 * PI, ALU.is_gt, ALU.mult, [rr], [rr])
            tt(cx, x, x, m[:, :], ALU.add, [rr], [rr])
            ts(cx, x, x, PI, -PI, ALU.min, ALU.max, [rr], [rr])

        wrap(r[:, :])
        act(cx, kf[:, :], r[:, :], AF.Sin, [rr], [rr])
        ts(cx, cx.ropeS[:, :], kf[:, :], gcol(cx, gsw, 0, 64), gcol(cx, "sgn", 0, 64), ALU.mult, ALU.mult,
           [rr, cx.r_const], [rr])
        ts(cx, r[:, :], r[:, :], PI / 2, None, ALU.add, None, [rr], [rr])
        wrap(r[:, :])
        act(cx, kf[:, :], r[:, :], AF.Sin, [rr], [rr])
        ts(cx, cx.ropeC[:, :], kf[:, :], gcol(cx, gr, 0, 64), None, ALU.mult, None, [rr, cx.r_const], [rr])


def alloc_stream(cx):
    cx.xT = cx.alloc("xT", [128, 16, G], F32)
    cx.rx = [R("x%d" % c) for c in range(16)]
    cx.hT = cx.alloc("hT", [128, 16, G], BF16)
    cx.rh = [R("h%d" % c) for c in range(16)]


def xchunks(cx):
    return [(cx.xT[:, c, :], cx.rx[c]) for c in range(16)]


def hchunks(cx):
    return [(cx.hT[:, c, :], cx.rh[c]) for c in range(16)]


def residual_epilogue(cx):
    def ep(oc, banks):
        for s, (bt, br) in enumerate(banks):
            xs = cx.xT[:, oc, s * 512:(s + 1) * 512]
            tt(cx, xs, xs, bt[:, :], ALU.add, [cx.rx[oc], br], [cx.rx[oc]])
    return ep


def slab_f32(cx, i):
    t, r = cx.slabs[i]
    return t[:, :].bitcast(F32).rearrange("p (a b) -> p a b", a=4), r


def phase1(cx, io):
    P = cx.P
    with cx.scope():
        setup_consts(cx, io, "k", T1)
        Ck, Sk = cx.ropeC, cx.ropeS
        cx.ns = 2
        cx.init_wring(nw=2)
        alloc_stream(cx)
        ckvf, r_ckvf = slab_f32(cx, 1)
        ckvT = cx.alloc("ckvT", [128, 4, G], BF16)
        r_ckvT = [R("ckvT%d" % i) for i in range(4)]
        krf = cx.alloc("krf", [64, G], F32)
        r_krf = R("krf")
        sqpe = cx.alloc("sqpe", [64, G], BF16)
        r_sqpe = R("sqpe")
        vt = [(cx.alloc("vt", [128, 512], BF16), R("vt%d" % i)) for i in range(2)]
        kno = [(cx.alloc("kno", [128, G], BF16), R("kno%d" % i)) for i in range(2)]
        kro = [(cx.alloc("kro", [64, G], BF16), R("kro%d" % i)) for i in range(2)]
        t1 = cx.lnt[0:64, :]
        r_t1 = cx.r_lnt
        r_out = R("p1out")
        if STOP == 0:
            return
        for g in range(T1 // G):
            tok = slice(g * G, (g + 1) * G)
            for c_ in range(16):
                P.dma("sp", cx.xT[:, c_, :], io["xT"][:, c_, tok], writes=[cx.rx[c_]])
            if STOP == 1:
                return
            norm_fm(cx, xchunks(cx), "a_norm", hchunks(cx), D)
            if STOP == 2:
                return

            def ep_ckv(oc, banks):
                for s, (bt, br) in enumerate(banks):
                    cpy(cx, "act" if s == 0 else "dve", ckvf[:, oc, s * 512:(s + 1) * 512], bt[:, :], [br], [r_ckvf])
            linear_fm(cx, lambda oc: io["WDKV"][oc], 16, hchunks(cx), 4, ep_ckv)
            if STOP == 3:
                return
            wt, rw = cx.load_w(io["WDKV"][4], 16, 128)
            bp = cx.bankpair()
            bs = cx.bankpair()
            for half, banks in ((0, bp), (1, bs)):
                for kc in range(16):
                    for s, (bt, br) in enumerate(banks):
                        mm(cx, bt[0:64, :], wt[:, kc, half * 64:(half + 1) * 64], cx.hT[:, kc, s * 512:(s + 1) * 512],
                           kc == 0, kc == 15, [rw, cx.rh[kc]], br)
            for s in range(2):
                sl = slice(s * 512, (s + 1) * 512)
                gsl = slice(g * G + s * 512, g * G + (s + 1) * 512)
                act(cx, sqpe[:, sl], bp[s][0][0:64, :], AF.Square, [bp[s][1]], [r_sqpe])
                tt(cx, krf[:, sl], bp[s][0][0:64, :], Ck[:, gsl], ALU.mult, [bp[s][1], cx.r_rope], [r_krf])
                tt(cx, t1[:, sl], bs[s][0][0:64, :], Sk[:, gsl], ALU.mult, [bs[s][1], cx.r_rope], [r_t1])
                tt(cx, krf[:, sl], krf[:, sl], t1[:, sl], ALU.add, [r_krf, r_t1], [r_krf])
            if STOP == 4:
                return
            norm_fm(cx, [(ckvf[:, i, :], r_ckvf) for i in range(4)], "a_g_kv",
                    [(ckvT[:, i, :], r_ckvT[i]) for i in range(4)], 512)
            if STOP == 5:
                return
            wv, r_wv = cx.load_slab(io["WV"], 4, 2048)
            assert r_wv is cx.slabs[0][1]
            cx.si = 0
            for tti in range(8):
                for cg in range(4):
                    bt, br = cx.banks[(tti * 4 + cg) % 8]
                    for kc in range(4):
                        mm(cx, bt[:, :], ckvT[:, kc, tti * 128:(tti + 1) * 128], wv[:, kc, cg * 512:(cg + 1) * 512],
                           kc == 0, kc == 3, [r_ckvT[kc], r_wv], br)
                    vtt, rvt = vt[cg % 2]
                    cpy(cx, "act" if cg % 2 == 0 else "dve", vtt[:, :], bt[:, :], [br], [rvt])
                    dst = io["VV"][cg * 4:(cg + 1) * 4, :, g * 8 + tti, :].rearrange("h p d -> p h d")
                    P.dma("sp", dst, vtt[:, :].rearrange("p (h d) -> p h d", h=4), reads=[rvt], writes=[])
            if STOP == 6:
                return
            def kmm(h):
                wt, rw = cx.load_w(io["WUKVN"][h], 4, 128)
                ba = cx.bankpair()
                for kc in range(4):
                    for s, (bt, br) in enumerate(ba):
                        mm(cx, bt[:, :], wt[:, kc, :], ckvT[:, kc, s * 512:(s + 1) * 512], kc == 0, kc == 3, [rw, r_ckvT[kc]], br)
                return ba
            ba_next = kmm(0)
            for h in range(16):
                ba = ba_next
                if h + 1 < 16:
                    ba_next = kmm(h + 1)
                ko, rko = kno[h % 2]
                kr_, rkr = kro[h % 2]
                head_norm(cx, ba, (sqpe, r_sqpe), 192, "gkn_n", ko, [rko])
                tt(cx, kr_[:, :], krf[:, :], cx.rstd[0:64, :], ALU.mult, [r_krf, cx.r_rstd], [rkr])
                P.dma("sp", io["KN"][h, :, tok], ko[:, :], reads=[rko], writes=[])
                P.dma("sp", io["KR"][h, :, tok], kr_[:, :], reads=[rkr], writes=[])


def ffn_block(cx, io, layer, fnorm):
    norm_fm(cx, xchunks(cx), fnorm, hchunks(cx), D)
    WG, WU, WD = io["WG%d" % layer], io["WU%d" % layer], io["WD%d" % layer]
    ns = cx.ns
    gb = [cx.banks[0], cx.banks[1]][:ns]
    ub = [cx.banks[2], cx.banks[3]][:ns]
    db = [[cx.banks[4], cx.banks[5]][:ns], [cx.banks[6], cx.banks[7]][:ns]]
    di = 0
    for sb in range(NFF // 4):
        aT, r_aT = cx.aT[sb % 2]
        for c in range(4):
            ff = sb * 4 + c
            wg, rwg = cx.load_w(WG[ff], 16, 128)
            wu, rwu = cx.load_w(WU[ff], 16, 128)
            sg, rsg = cx.sg[ff % 2]
            for kc in range(16):
                for s, (bt, br) in enumerate(gb):
                    mm(cx, bt[:, :], wg[:, kc, :], cx.hT[:, kc, s * 512:(s + 1) * 512], kc == 0, kc == 15, [rwg, cx.rh[kc]], br)
            for s, (bt, br) in enumerate(gb):
                act(cx, sg[:, s * 512:(s + 1) * 512], bt[:, :], AF.Silu, [br], [rsg])
            for kc in range(16):
                for s, (bt, br) in enumerate(ub):
                    mm(cx, bt[:, :], wu[:, kc, :], cx.hT[:, kc, s * 512:(s + 1) * 512], kc == 0, kc == 15, [rwu, cx.rh[kc]], br)
            for s, (bt, br) in enumerate(ub):
                sl = slice(s * 512, (s + 1) * 512)
                tt(cx, aT[:, c, sl], sg[:, sl], bt[:, :], ALU.mult, [rsg, br], [r_aT[c]])
        wd, rwd = cx.load_slab(WD[sb], 4, 2048)
        for oc in range(16):
            banks = db[di]
            di = 1 - di
            for c in range(4):
                for s, (bt, br) in enumerate(banks):
                    mm(cx, bt[:, :], wd[:, c, oc * 128:(oc + 1) * 128], aT[:, c, s * 512:(s + 1) * 512], c == 0, c == 3,
                       [rwd, r_aT[c]], br)
            for s, (bt, br) in enumerate(banks):
                xs = cx.xT[:, oc, s * 512:(s + 1) * 512]
                tt(cx, xs, xs, bt[:, :], ALU.add, [cx.rx[oc], br], [cx.rx[oc]])


def egate_block(cx, io, layer, enorm, tok):
    P = cx.P
    norm_fm(cx, xchunks(cx), enorm, hchunks(cx), D)
    EG, EP = io["EG%d" % layer], io["EP%d" % layer]
    P.dma("pool", cx.pT[:, :, 0:cx.gw], io["pT%d" % layer][:, :, tok], writes=[cx.r_pT])
    for oc in range(16):
        wg, rwg = cx.load_w(EG[oc], 16, 128)
        wp, rwp = cx.load_w(EP[oc], 2, 128)
        ba = cx.bankpair()
        bb = cx.bankpair()
        for kc in range(16):
            for s, (bt, br) in enumerate(ba):
                mm(cx, bt[:, :], wg[:, kc, :], cx.hT[:, kc, s * 512:(s + 1) * 512], kc == 0, kc == 15, [rwg, cx.rh[kc]], br)
        for kc in range(2):
            for s, (bt, br) in enumerate(bb):
                mm(cx, bt[:, :], wp[:, kc, :], cx.pT[:, kc, s * 512:(s + 1) * 512], kc == 0, kc == 1, [rwp, cx.r_pT], br)
        sg, rsg = cx.sg[oc % 2]
        for s in range(cx.ns):
            sl = slice(s * 512, (s + 1) * 512)
            act(cx, sg[:, sl], ba[s][0][:, :], AF.Sigmoid, [ba[s][1]], [rsg])
            tt(cx, sg[:, sl], sg[:, sl], bb[s][0][:, :], ALU.mult, [rsg, bb[s][1]], [rsg])
            xs = cx.xT[:, oc, sl]
            tt(cx, xs, xs, sg[:, sl], ALU.add, [cx.rx[oc], rsg], [cx.rx[oc]])


def alloc_ffn(cx):
    cx.aT = []
    for i in range(2):
        cx.aT.append((cx.alloc("aT", [128, 4, G], BF16), [R("aT%d_%d" % (i, c)) for c in range(4)]))
    cx.sg = [(cx.alloc("sg", [128, G], F32), R("sg%d" % i)) for i in range(2)]
    cx.pT = cx.alloc("pT", [128, 2, G], BF16)
    cx.r_pT = R("pT")


def attention(cx, h, sl, qn, rqn, qr, rqr, tiles, scale, bias_fn=None, st_banks=(0, 1), o_banks=(2, 3)):
    P = cx.P
    ob, orr = cx.banks[o_banks[0]]
    sb_, srr = cx.banks[o_banks[1]]
    n = len(tiles)
    nb = len(st_banks)
    la = nb - 1

    def qk(i):
        t = tiles[i]
        c0, c1 = t["c0"], t["c1"]
        stb, rst = cx.banks[st_banks[i % nb]]
        if qr is not None:
            mm(cx, stb[:, c0:c1], t["K"], qn[:, c0:c1], True, False, t["rK"] + [rqn], rst)
            mm(cx, stb[:, c0:c1], t["KR"], qr[:, c0:c1], False, True, t["rK"] + [rqr], rst)
        else:
            mm(cx, stb[:, c0:c1], t["K"], qn[:, c0:c1], True, True, t["rK"] + [rqn], rst)

    def rest(i):
        t = tiles[i]
        c0, c1 = t["c0"], t["c1"]
        stb, rst = cx.banks[st_banks[i % nb]]
        pt, rpt = cx.pring[cx.pi]
        cx.pi = (cx.pi + 1) % len(cx.pring)
        mcol = cx.mk[:, t["mask"]:t["mask"] + 1] if t["mask"] is not None else cx.zeroc
        if bias_fn is None:
            act(cx, pt[:, c0:c1], stb[:, c0:c1], AF.Exp, [rst, cx.r_const], [rpt], scale=scale, bias=mcol)
        else:
            tmp, rtmp = cx.tmpring[cx.ti]
            cx.ti = (cx.ti + 1) % len(cx.tmpring)
            b_ap, b_r = bias_fn(t)
            stt(cx, tmp[:, c0:c1], stb[:, c0:c1], scale, b_ap, ALU.mult, ALU.add, [rst, b_r], [rtmp])
            act(cx, pt[:, c0:c1], tmp[:, c0:c1], AF.Exp, [rtmp, cx.r_const], [rpt], scale=1.0, bias=mcol)
        if t["zero"] is not None:
            p0, p1, z0, z1 = t["zero"]
            cx.P.op("pool", lambda e, o=pt[p0:p1, z0:z1]: e.memset(o, 0.0), writes=[rpt])
        mm(cx, ob[:, c0:c1], t["V"], pt[:, c0:c1], i == 0, i == n - 1, t["rK"] + [rpt], orr)
        mm(cx, sb_[:, c0:c1], cx.ones[:, :], pt[:, c0:c1], i == 0, i == n - 1, [rpt, cx.r_const], srr)

    for i in range(min(la, n)):
        qk(i)
    for i in range(n):
        if i + la < n:
            qk(i + la)
        rest(i)
    act(cx, cx.rinv[:, :], sb_[:, :], AF.Ln, [srr], [cx.r_rinv])
    act(cx, cx.rinv[:, :], cx.rinv[:, :], AF.Exp, [cx.r_rinv], [cx.r_rinv], scale=-1.0)
    tt(cx, cx.hT[:, h, sl * 512:(sl + 1) * 512], ob[:, :], cx.rinv[:, :], ALU.mult, [orr, cx.r_rinv], [cx.rh[h]])


def alloc_attn(cx, npt=4):
    cx.pring = [(cx.alloc("pt", [128, 512], BF16), R("pt%d" % i)) for i in range(npt)]
    cx.pi = 0
    cx.rinv = cx.alloc("rinv", [128, 512], F32)
    cx.r_rinv = R("rinv")


def phase2(cx, io):
    P = cx.P
    with cx.scope():
        setup_consts(cx, io, "q", T2)
        Cq, Sq = cx.ropeC, cx.ropeS
        cx.ns = 2
        cx.init_wring(nw=3)
        alloc_stream(cx)
        r_out = R("p2out")
        sc0 = float(1.0 / np.sqrt(192.0))
        t0s, r0s = cx.slabs[0]
        t1s, r1s = cx.slabs[1]
        r_kr = [R("kr0"), R("kr1")]
        for grp in L0_GROUPS:
            cx.ns = len(grp)
            gw = cx.gw
            tok = slice(grp[0] * 512, grp[0] * 512 + gw)
            for c_ in range(16):
                P.dma("sp", cx.xT[:, c_, 0:gw], io["xT"][:, c_, tok], writes=[cx.rx[c_]])
            with cx.scope():
                alloc_attn(cx, 2)
                cqf, r_cqf = slab_f32(cx, 1)
                cqT = cx.alloc("cqT", [128, 4, G], BF16)
                r_cqT = [R("cqT%d" % i) for i in range(4)]
                qn = [(cx.alloc("qn", [128, 512], BF16), R("qn%d" % i)) for i in range(2)]
                qr = [(cx.alloc("qr", [64, 512], BF16), R("qr%d" % i)) for i in range(2)]
                sqn = cx.alloc("sqn", [128, 512], BF16)
                sqr = cx.alloc("sqr", [64, 512], BF16)
                r_sqq = R("sqq")
                rq = cx.rstd[:, 0:512]
                lq = cx.lnt[:, 0:512]
                r_rq, r_lq = cx.r_rstd, cx.r_lnt
                t1 = cx.lnt[0:64, 512:1024]
                t2 = cx.rstd[0:64, 512:1024]
                r_t1 = R("t1")
                r_t2 = R("t2")
                kvB = cx.alloc("kvB", [128, 8192], BF16)
                r_kvB = R("kvB")
                KVB = [(t0s, r0s), (kvB, r_kvB)]
                norm_fm(cx, xchunks(cx), "a_norm", hchunks(cx), D)

                def ep_cq(oc, banks):
                    for s_, (bt, br) in enumerate(banks):
                        cpy(cx, "act" if s_ == 0 else "dve", cqf[:, oc, s_ * 512:(s_ + 1) * 512], bt[:, :], [br], [r_cqf])
                linear_fm(cx, lambda oc: io["WDQ"][oc], 16, hchunks(cx), 4, ep_cq)
                norm_fm(cx, [(cqf[:, i, :], r_cqf) for i in range(4)], "a_g_q",
                        [(cqT[:, i, :], r_cqT[i]) for i in range(4)], 512)
                items = [(h, sl) for h in range(16) for sl in range(cx.ns)]
                wqs = {}

                def qpath(k):
                    h, sl = items[k]
                    if h not in wqs:
                        wqs[h] = cx.load_w(io["WUQ"][h], 4, 256)
                    wq, rwq = wqs[h]
                    slot = grp[sl]
                    csl = slice(sl * 512, (sl + 1) * 512)
                    gsl = slice(slot * 512, (slot + 1) * 512)
                    bA, rA = cx.banks[4]
                    bB, rB = cx.banks[5]
                    bC, rC = cx.banks[6]
                    bD, rD = cx.banks[7]
                    for kc in range(4):
                        mm(cx, bA[:, :], wq[:, kc, 0:128], cqT[:, kc, csl], kc == 0, kc == 3, [rwq, r_cqT[kc]], rA)
                    for kc in range(4):
                        mm(cx, bB[0:64, :], wq[:, kc, 128:192], cqT[:, kc, csl], kc == 0, kc == 3, [rwq, r_cqT[kc]], rB)
                    for kc in range(4):
                        mm(cx, bC[0:64, :], wq[:, kc, 192:256], cqT[:, kc, csl], kc == 0, kc == 3, [rwq, r_cqT[kc]], rC)
                    act(cx, sqn[:, :], bA[:, :], AF.Square, [rA], [r_sqq])
                    act(cx, sqr[:, :], bB[0:64, :], AF.Square, [rB], [r_sqq])
                    mm(cx, bD[:, :], cx.ones[:, :], sqn[:, :], True, False, [r_sqq, cx.r_const], rD)
                    mm(cx, bD[:, :], cx.ones[0:64, :], sqr[:, :], False, True, [r_sqq, cx.r_const], rD)
                    act(cx, lq, bD[:, :], AF.Ln, [rD, cx.r_const], [r_lq], scale=1.0 / 192, bias=cx.epsc[:, 0:1])
                    act(cx, rq, lq, AF.Exp, [r_lq], [r_rq], scale=-0.5)
                    qnt, rqn = qn[k % 2]
                    qrt, rqr = qr[k % 2]
                    stt(cx, qnt[:, :], bA[:, :], gcol(cx, "gqn_n"), rq, ALU.mult, ALU.mult, [rA, r_rq, cx.r_const], [rqn])
                    tt(cx, t1[:, :], bB[0:64, :], Cq[:, gsl], ALU.mult, [rB, cx.r_rope], [r_t1])
                    tt(cx, t2[:, :], bC[0:64, :], Sq[:, gsl], ALU.mult, [rC, cx.r_rope], [r_t2])
                    tt(cx, t1[:, :], t1[:, :], t2[:, :], ALU.add, [r_t1, r_t2], [r_t1])
                    tt(cx, qrt[:, :], t1[:, :], rq[0:64, :], ALU.mult, [r_t1, r_rq], [rqr])

                qpath(0)
                for k, (h, sl) in enumerate(items):
                    kvt, r_kv = KVB[h % 2]
                    KA = kvt[:, 0:4096]
                    VA = kvt[:, 4096:8192].rearrange("p (t d) -> p t d", t=32)
                    KRA = t1s[0:64, (h % 2) * 4096:(h % 2 + 1) * 4096]
                    r_kra = r_kr[h % 2]
                    if sl == 0:
                        P.dma("sp", KA, io["KN"][h], writes=[r_kv])
                        P.dma("sp", KRA, io["KR"][h], reads=r_cqT, writes=[r_kra])
                        P.dma("sp", VA, io["VV"][h], writes=[r_kv])
                    if k + 1 < len(items):
                        qpath(k + 1)
                    slot = grp[sl]
                    qnt, rqn = qn[k % 2]
                    qrt, rqr = qr[k % 2]
                    tiles = []
                    for t in range(4):
                        ks = slice(slot * 512 + t * 128, slot * 512 + (t + 1) * 128)
                        tiles.append(dict(K=KA[:, ks], KR=KRA[:, ks], V=VA[:, slot * 4 + t, :], rK=[r_kv, r_kra],
                                          c0=t * 128, c1=512, mask=None, zero=(64, 128, t * 128, t * 128 + 64)))
                    for i, pp in enumerate(L0_PAST[slot]):
                        for t in range(4):
                            ks = slice(pp * 512 + t * 128, pp * 512 + (t + 1) * 128)
                            tiles.append(dict(K=KA[:, ks], KR=KRA[:, ks], V=VA[:, pp * 4 + t, :], rK=[r_kv, r_kra],
                                              c0=0, c1=512, mask=MK0[(slot, i)], zero=None))
                    attention(cx, h, sl, qnt, rqn, qrt, rqr, tiles, sc0)
            with cx.scope():
                alloc_ffn(cx)
                linear_fm(cx, lambda oc: io["WO0"][oc], 16, hchunks(cx), 16, residual_epilogue(cx))
                ffn_block(cx, io, 0, "f_norm0")
                egate_block(cx, io, 0, "e_norm0", tok)
                P.dma("sp", io["xres"][:, :, tok], cx.xT[:, :, 0:gw], reads=cx.rx, writes=[r_out])
            with cx.scope():
                sko = [(cx.alloc("sko", [128, G], BF16), R("sko%d" % i)) for i in range(2)]
                svt = [(cx.alloc("svt", [128, 512], BF16), R("svt%d" % i)) for i in range(2)]
                norm_fm(cx, xchunks(cx), "s_norm", hchunks(cx), D)
                for h in range(16):
                    wt, rw = cx.load_w(io["SWK"][h], 16, 128)
                    ba = cx.bankpair()
                    for kc in range(16):
                        for s_, (bt, br) in enumerate(ba):
                            mm(cx, bt[:, :], wt[:, kc, :], cx.hT[:, kc, s_ * 512:(s_ + 1) * 512], kc == 0, kc == 15,
                               [rw, cx.rh[kc]], br)
                    ko, rko = sko[h % 2]
                    head_norm(cx, ba, None, 128, "s_gkn", ko, [rko])
                    P.dma("sp", io["SK"][h, :, tok], ko[:, 0:gw], reads=[rko], writes=[r_out])
                for cg in range(4):
                    wsl, rws = cx.load_slab(io["SWV"][cg], 16, 512)
                    for tti in range(4 * cx.ns):
                        bt, br = cx.banks[(cg * 8 + tti) % 8]
                        for kc in range(16):
                            mm(cx, bt[:, :], cx.hT[:, kc, tti * 128:(tti + 1) * 128], wsl[:, kc, :], kc == 0, kc == 15,
                               [cx.rh[kc], rws], br)
                        v_, rv_ = svt[tti % 2]
                        cpy(cx, "act" if tti % 2 == 0 else "dve", v_[:, :], bt[:, :], [br], [rv_])
                        dst = io["SV"][cg * 4:(cg + 1) * 4, :, grp[0] * 4 + tti, :].rearrange("h p d -> p h d")
                        P.dma("sp", dst, v_[:, :].rearrange("p (h d) -> p h d", h=4), reads=[rv_], writes=[r_out])
        cx.ns = 2


def phase3(cx, io):
    P = cx.P
    with cx.scope():
        setup_consts(cx, io, None)
        cx.ns = 2
        cx.init_wring()
        alloc_stream(cx)
        r_out = R("p3out")
        r_ext = R("ext")
        ext = io["ext"]
        tab = io["relb"]
        Z = io["Z"]
        with cx.scope():
            tab_sb = cx.alloc("tab_sb", [16, 513], F32)
            ext_sb = cx.alloc("ext_sb", [16, 1535], F32)
            P.dma("sp", tab_sb[:, :], tab, writes=[r_ext])
            P.op("dve", lambda e: e.memset(ext_sb[:, :], 0.0), writes=[r_ext])
            ts(cx, ext_sb[:, 0:255], ext_sb[:, 0:255], tab_sb[:, 0:1], None, ALU.add, None, [r_ext], [r_ext])
            ts(cx, ext_sb[:, 768:1535], ext_sb[:, 768:1535], tab_sb[:, 512:513], None, ALU.add, None, [r_ext], [r_ext])
            P.op("dve", lambda e: e.tensor_copy(out=ext_sb[:, 255:768], in_=tab_sb[:, :]), reads=[r_ext], writes=[r_ext])
            P.dma("sp", ext, ext_sb[:, :], reads=[r_ext], writes=[r_ext])
            srcb = bass.AP(tensor=ext.tensor, offset=0, ap=[[1535, 16], [0, 128], [1, 1535]])
            P.dma("sp", Z[:, :, 0:1535], srcb, reads=[r_ext], writes=[r_ext])
        sc1 = float(1.0 / np.sqrt(128.0))
        for gi, grp in enumerate(L1_GROUPS):
            tok = slice(grp[0] * 512, grp[0] * 512 + G)
            otok = slice(gi * G, (gi + 1) * G)
            for c_ in range(16):
                P.dma("sp", cx.xT[:, c_, :], io["xres"][:, c_, tok], writes=[cx.rx[c_]])
            with cx.scope():
                alloc_attn(cx)
                cx.tmpring = [(cx.alloc("tmp", [128, 512], F32), R("tmp%d" % i)) for i in range(2)]
                cx.ti = 0
                qT = cx.alloc("qT", [128, 16, G], BF16)
                r_qT = [R("qT%d" % i) for i in range(16)]
                TB = [(cx.alloc("TB", [128, 1024], F32), R("TB%d" % i)) for i in range(2)]
                norm_fm(cx, xchunks(cx), "b_norm", hchunks(cx), D)
                for h in range(16):
                    wt, rw = cx.load_w(io["BWQ"][h], 16, 128)
                    ba = cx.bankpair()
                    for kc in range(16):
                        for s_, (bt, br) in enumerate(ba):
                            mm(cx, bt[:, :], wt[:, kc, :], cx.hT[:, kc, s_ * 512:(s_ + 1) * 512], kc == 0, kc == 15,
                               [rw, cx.rh[kc]], br)
                    head_norm(cx, ba, None, 128, "b_gqn", qT[:, h, :], [r_qT[h]])
                for h in range(16):
                    t0s, r0s = cx.slabs[h % 2]
                    SKA = t0s[:, 0:T2]
                    SVA = t0s[:, 4096:4096 + T2].rearrange("p (t d) -> p t d", t=4 * NL0)
                    P.dma("sp", SKA, io["SK"][h], writes=[r0s])
                    P.dma("sp", SVA, io["SV"][h], writes=[r0s])
                    tb, rtb = TB[h % 2]
                    src = bass.AP(tensor=Z.tensor, offset=h * 128 * 1536 + 127, ap=[[1535, 128], [1, 1024]])
                    P.dma("sp", tb[:, :], src, reads=[r_ext], writes=[rtb])
                    for sl in range(2):
                        slot = grp[sl]
                        csl = slice(sl * 512, (sl + 1) * 512)
                        tiles = []
                        for t in range(4):
                            ks = slice(slot * 512 + t * 128, slot * 512 + (t + 1) * 128)
                            tiles.append(dict(K=SKA[:, ks], V=SVA[:, slot * 4 + t, :], rK=[r0s], rel=t,
                                              c0=t * 128, c1=512, mask=None, zero=(64, 128, t * 128, t * 128 + 64)))
                        for i, pp in enumerate(L1_PREV[slot]):
                            for t in range(4):
                                ks = slice(pp * 512 + t * 128, pp * 512 + (t + 1) * 128)
                                tiles.append(dict(K=SKA[:, ks], V=SVA[:, pp * 4 + t, :], rK=[r0s], rel=t - 4,
                                                  c0=0, c1=(t + 1) * 128, mask=MK1[(slot, i)],
                                                  zero=(0, 64, t * 128 + 64, t * 128 + 128)))

                        def bias_fn(t, tb=tb, rtb=rtb):
                            j0 = 384 - t["rel"] * 128
                            return tb[:, j0 + t["c0"]:j0 + t["c1"]], rtb
                        attention(cx, h, sl, qT[:, h, csl], r_qT[h], None, None, tiles, sc1, bias_fn,
                                  st_banks=(0, 1, 4, 5), o_banks=((2, 3) if (2 * h + sl) % 2 == 0 else (6, 7)))
            with cx.scope():
                alloc_ffn(cx)
                linear_fm(cx, lambda oc: io["WO1"][oc], 16, hchunks(cx), 16, residual_epilogue(cx))
                ffn_block(cx, io, 1, "f_norm1")
                egate_block(cx, io, 1, "e_norm1", tok)
                P.dma("sp", io["outT"][:, :, otok], cx.xT[:, :, :], reads=cx.rx, writes=[r_out])


def tiled(w, kc, oc, m=128):
    return np.ascontiguousarray(w.reshape(kc, 128, oc, m).transpose(2, 1, 0, 3))


DRAM_SPECS = {
    "xT": ([128, 16, T1], F32), "pos": ([T1], I32), "gv": ([128, NGV], F32), "mk": ([128, NMK], F32),
    "WDKV": ([5, 128, 16, 128], F32), "WUKVN": ([16, 128, 4, 128], F32), "WV": ([128, 4, 2048], F32),
    "KN": ([16, 128, T1], BF16), "KR": ([16, 64, T1], BF16), "VV": ([16, 128, 32, 128], BF16),
    "WDQ": ([4, 128, 16, 128], F32), "WUQ": ([16, 128, 4, 256], F32), "WO0": ([16, 128, 16, 128], F32),
    "WG0": ([NFF, 128, 16, 128], F32), "WU0": ([NFF, 128, 16, 128], F32), "WD0": ([NFF // 4, 128, 4, 2048], F32),
    "EG0": ([16, 128, 16, 128], F32), "EP0": ([16, 128, 2, 128], F32), "pT0": ([128, 2, T2], F32),
    "SWK": ([16, 128, 16, 128], F32), "SWV": ([4, 128, 16, 512], F32),
    "xres": ([128, 16, T2], F32), "SK": ([16, 128, T2], BF16), "SV": ([16, 128, 4 * NL0, 128], BF16),
    "BWQ": ([16, 128, 16, 128], F32), "WO1": ([16, 128, 16, 128], F32),
    "WG1": ([NFF, 128, 16, 128], F32), "WU1": ([NFF, 128, 16, 128], F32), "WD1": ([NFF // 4, 128, 4, 2048], F32),
    "EG1": ([16, 128, 16, 128], F32), "EP1": ([16, 128, 2, 128], F32), "pT1": ([128, 2, T2], F32),
    "relb": ([16, 513], F32), "ext": ([16, 1535], F32), "Z": ([16, 128, 1536], F32), "outT": ([128, 16, T], F32),
}
PHASE_IN = {
    1: ["xT", "pos", "gv", "mk", "WDKV", "WUKVN", "WV"],
    2: ["xT", "pos", "gv", "mk", "KN", "KR", "VV", "WDQ", "WUQ", "WO0", "WG0", "WU0", "WD0",
        "EG0", "EP0", "pT0", "SWK", "SWV"],
    3: ["xres", "gv", "mk", "SK", "SV", "BWQ", "WO1", "WG1", "WU1", "WD1", "EG1", "EP1", "pT1", "relb"],
}
PHASE_OUT = {1: ["KN", "KR", "VV"], 2: ["xres", "SK", "SV"], 3: ["outT"]}
PHASE_INT = {1: [], 2: [], 3: ["ext", "Z"]}


def build_program(phases):
    fused = len(phases) > 1
    nc = bass.Bass("TRN2", target_bir_lowering=False)
    io = {}
    internal = set()
    if fused:
        for ph in phases:
            internal.update(PHASE_OUT[ph])
        internal.discard("outT")
    for ph in phases:
        for n in PHASE_IN[ph] + PHASE_OUT[ph] + PHASE_INT[ph]:
            if n in io:
                continue
            shape, dt = DRAM_SPECS[n]
            if n in internal or n in PHASE_INT[ph]:
                kind = "Internal"
            elif n in PHASE_OUT[ph]:
                kind = "ExternalOutput"
            else:
                kind = "ExternalInput"
            io[n] = nc.dram_tensor(n, shape, dt, kind=kind).ap()
    cx = Ctx(nc)
    for ph in phases:
        {1: phase1, 2: phase2, 3: phase3}[ph](cx, io)
    cx.P.flush()
    cx.P.wait_all_dma()
    return nc


_PROGS = {}
LAST_RES = None


def get_prog(phases):
    key = tuple(phases)
    if key not in _PROGS:
        _PROGS[key] = build_program(phases)
    return _PROGS[key]


def core_tokens(r, npos=8):
    return np.concatenate([np.arange(s * 512, (s + 1) * 512) for s in POS2SUB[r][:npos]])


def fm(a):
    t, f = a.shape
    return np.ascontiguousarray(a.T.reshape(f // 128, 128, t).transpose(1, 0, 2))


def prep_weights(inp):
    f = np.float32
    W = {}
    a_w_dkv = inp["a_w_dkv"][0]
    pe = a_w_dkv[:, 512:576]
    pe_sw = np.concatenate([pe[:, 32:64], pe[:, 0:32]], axis=1)
    W["WDKV"] = tiled(np.concatenate([a_w_dkv[:, :512], pe, pe_sw], axis=1), 16, 5)
    ukv = inp["a_w_ukv"][0].reshape(512, 16, 256)
    W["WUKVN"] = tiled(np.ascontiguousarray(ukv[:, :, :128]).reshape(512, 2048), 4, 16)
    W["WV"] = np.ascontiguousarray(ukv[:, :, 128:].reshape(4, 128, 2048).transpose(1, 0, 2))
    W["WDQ"] = tiled(inp["a_w_dq"][0], 16, 4)
    uq = inp["a_w_uq"][0].reshape(512, 16, 192)
    uq2 = np.concatenate([uq[:, :, :128], uq[:, :, 128:192], uq[:, :, 160:192], uq[:, :, 128:160]], axis=2)
    W["WUQ"] = tiled(np.ascontiguousarray(uq2).reshape(512, 16 * 256), 4, 16, 256)
    W["WO0"] = tiled(inp["a_w_o"][0], 16, 16)
    W["WO1"] = tiled(inp["b_w_o"][0], 16, 16)
    W["BWQ"] = tiled(inp["b_w_q"][0], 16, 16)
    W["SWK"] = tiled(inp["s_w_k"], 16, 16)
    W["SWV"] = np.ascontiguousarray(inp["s_w_v"].reshape(16, 128, 4, 512).transpose(2, 1, 0, 3))
    for l in range(2):
        W["WG%d" % l] = tiled(inp["f_w_gate"][l], 16, NFF)
        W["WU%d" % l] = tiled(inp["f_w_up"][l], 16, NFF)
        W["WD%d" % l] = np.ascontiguousarray(inp["f_w_down"][l].reshape(NFF // 4, 4, 128, 2048).transpose(0, 2, 1, 3))
        W["EG%d" % l] = tiled(inp["e_w_gate"][l], 16, 16)
        W["EP%d" % l] = tiled(inp["e_w_proj"][l], 2, 16)
    W["relb"] = np.ascontiguousarray(inp["b_rel_bias"][0])
    gv = np.zeros((128, NGV), f)

    def put(name, vec):
        c0, k = GVI[name]
        v = np.asarray(vec, f)
        if v.size == 128 * k:
            gv[:, c0:c0 + k] = v.reshape(k, 128).T
        else:
            gv[:v.size, c0] = v
    put("a_norm", inp["a_norm"][0]); put("a_g_q", inp["a_g_q"][0]); put("a_g_kv", inp["a_g_kv"][0])
    put("f_norm0", inp["f_norm"][0]); put("e_norm0", inp["e_norm"][0]); put("s_norm", inp["s_norm"])
    put("b_norm", inp["b_norm"][0]); put("f_norm1", inp["f_norm"][1]); put("e_norm1", inp["e_norm"][1])
    gq, gk = inp["a_g_qn"][0], inp["a_g_kn"][0]
    put("gqn_n", gq[:128]); put("gkn_n", gk[:128]); put("s_gkn", inp["s_g_kn"]); put("b_gqn", inp["b_g_qn"][0])
    sw = (np.arange(64) + 32) % 64
    put("gqr", gq[128:192]); put("gqsw", gq[128:192][sw]); put("gkr", gk[128:192]); put("gksw", gk[128:192][sw])
    invf = (np.float32(10000.0) ** (-np.arange(0, 64, 2, dtype=f) / f(64))).astype(f)
    put("invf", invf[np.arange(64) % 32])
    put("sgn", np.where(np.arange(64) < 32, -1.0, 1.0))
    W["gv"] = gv
    return W


def core_masks(r):
    mk = np.zeros((128, NMK), np.float32)
    p2s = POS2SUB[r]
    for s in range(NL0):
        for i, pp in enumerate(L0_PAST[s]):
            if not (p2s[pp] < p2s[s]):
                mk[:, MK0[(s, i)]] = NEG
    for s in range(1, NL0):
        for i, pp in enumerate(L1_PREV[s]):
            if not (p2s[pp] == p2s[s] - 1):
                mk[:, MK1[(s, i)]] = NEG
    return mk


def run(phases, in_maps):
    nc = get_prog(phases)
    names = set()
    produced = set()
    for ph in phases:
        names.update(PHASE_IN[ph])
        produced.update(PHASE_OUT[ph])
    maps = [{k: v for k, v in m.items() if k in names and k not in produced} for m in in_maps]
    res = run_bass_kernel_spmd(nc, maps, core_ids=list(range(8)))
    global LAST_RES
    LAST_RES = res
    return res.results


def make_maps(inp):
    W = prep_weights(inp)
    x, p, positions = inp["x"], inp["p"], inp["positions"]
    maps = []
    for c in range(8):
        b, r = c // 2, c % 2
        tk = core_tokens(r)
        m = dict(W)
        m["xT"] = fm(x[b][tk])
        m["pos"] = np.ascontiguousarray(positions[b][tk]).astype(np.int32)
        m["pT0"] = fm(p[0, b][tk[:T2]])
        m["pT1"] = fm(p[1, b][tk[:T2]])
        m["mk"] = core_masks(r)
        maps.append(m)
    return maps


def kernel(**inp):
    inp = {k: np.asarray(v) for k, v in inp.items()}
    maps = make_maps(inp)
    if FUSED:
        r3 = run([1, 2, 3], maps)
    else:
        r1 = run([1], maps)
        for c in range(8):
            maps[c].update({n: r1[c][n] for n in PHASE_OUT[1]})
        r2 = run([2], maps)
        for c in range(8):
            maps[c].update({n: r2[c][n] for n in PHASE_OUT[2]})
        r3 = run([3], maps)
    out = np.zeros((NB, SEQ, D), np.float32)
    for c in range(8):
        b, r = c // 2, c % 2
        o = np.asarray(r3[c]["outT"])
        out[b][core_tokens(r, NL0)[512:]] = o.transpose(2, 1, 0).reshape(T, D)
    return out
```
